# Optimizing a Trainium2 kernel written in Bass

```python
import math
import jax, jax.numpy as jnp
from jax import lax
import numpy as np

D_MODEL = 1024
BATCH = 32
SEQ = 2048
DEPTH = 4
DEC_BATCH = 8
DEC_SEQ = 8192
PAST_LEN = 128

MIX_WIDTH = D_MODEL
ATTN_WIDTH = MIX_WIDTH // 2
HYENA_WIDTH = MIX_WIDTH - ATTN_WIDTH
DA_HEADS = 4
DA_HEAD_DIM = ATTN_WIDTH // (2 * DA_HEADS)
DA_V_DIM = 2 * DA_HEAD_DIM
ROPE_DIM = DA_HEAD_DIM // 4
ROPE_THETA = 500000.0
Q_BLOCK = 128
IN_WIDTH = 3 * ATTN_WIDTH + 3 * HYENA_WIDTH
SHORT_CONV = 3
FILTER_EMB = 33
FILTER_ORDER = 64
FAST_DECAY_PCT = 0.3
SLOW_DECAY_PCT = 1.5
DECAY_TARGET = 1e-2
N_MEM = 256
X_HEADS = 4
X_HEAD_DIM = D_MODEL // X_HEADS
D_FF = -(-8 * D_MODEL // (3 * 256)) * 256
EPS = 1e-6

kernel_name = "hymba_diffattn_hyena_encoder"

F32 = jnp.float32


def rms_norm(x, g):
    xf = x.astype(F32)
    y = xf * lax.rsqrt(jnp.mean(xf * xf, axis=-1, keepdims=True) + EPS)
    return (y * g.astype(F32)).astype(x.dtype)


def rope_partial(x):
    L = x.shape[1]
    inv = ROPE_THETA ** (-jnp.arange(0, ROPE_DIM, 2, dtype=F32) / ROPE_DIM)
    ang = jnp.arange(L, dtype=F32)[:, None] * inv[None, :]
    cos = jnp.cos(ang)[:, None, None, :]
    sin = jnp.sin(ang)[:, None, None, :]
    xr = x[..., :ROPE_DIM].astype(F32)
    x1, x2 = xr[..., :ROPE_DIM // 2], xr[..., ROPE_DIM // 2:]
    rot = jnp.concatenate([x1 * cos - x2 * sin, x2 * cos + x1 * sin], axis=-1).astype(x.dtype)
    return jnp.concatenate([rot, x[..., ROPE_DIM:]], axis=-1)


def diff_attention(q, k, v, lam):
    B, L = q.shape[0], q.shape[1]
    nb = L // Q_BLOCK
    scale = DA_HEAD_DIM ** -0.5
    qb = q.reshape(B, nb, Q_BLOCK, DA_HEADS, 2, DA_HEAD_DIM).transpose(1, 0, 2, 3, 4, 5)

    def block(qblk):
        s = jnp.einsum('bqhcd,bkhcd->bchqk', qblk, k).astype(F32) * scale
        p = jax.nn.softmax(s, axis=-1)
        a = (p[:, 0] - lam * p[:, 1]).astype(v.dtype)
        return jnp.einsum('bhqk,bkhe->bqhe', a, v)

    o = lax.map(block, qb)
    return o.transpose(1, 0, 2, 3, 4).reshape(B, L, DA_HEADS, DA_V_DIM)


def short_conv(u, w, b):
    L = u.shape[1]
    pad = SHORT_CONV // 2
    up = jnp.pad(u, ((0, 0), (pad, pad), (0, 0)))
    y = w[0] * up[:, 0:L]
    for j in range(1, SHORT_CONV):
        y = y + w[j] * up[:, j:j + L]
    return y + b


def hyena_filters(L, w1, b1, fr1, w2, b2, fr2, w3):
    t = jnp.linspace(0.0, 1.0, L, dtype=F32)[:, None]
    bands = (FILTER_EMB - 1) // 2
    w = 2.0 * math.pi * jnp.arange(L, dtype=F32)[:, None] / L
    f = jnp.linspace(1e-4, bands - 1, bands, dtype=F32)[None, :]
    fw = f * w
    z = jnp.concatenate([t, jnp.cos(fw), -jnp.sin(fw)], axis=-1)
    h = jnp.sin(fr1 * (z.astype(w1.dtype) @ w1 + b1))
    h = jnp.sin(fr2 * (h @ w2 + b2))
    h = (h @ w3).astype(F32).reshape(L, 2, HYENA_WIDTH)
    max_decay = math.log(DECAY_TARGET) / FAST_DECAY_PCT
    min_decay = math.log(DECAY_TARGET) / SLOW_DECAY_PCT
    deltas = jnp.abs(jnp.linspace(min_decay, max_decay, HYENA_WIDTH, dtype=F32))
    h = h * jnp.exp(-t[:, :, None] * deltas)
    hf, hb = h[:, 0], h[:, 1]
    kern = jnp.concatenate([hf[:1] + hb[:1], hf[1:], jnp.zeros_like(hf[:1]), hb[:0:-1]], axis=0)
    return kern / jnp.sum(jnp.abs(kern), axis=0, keepdims=True)


def long_conv(u, kern):
    L = u.shape[1]
    uf = jnp.fft.rfft(u.astype(F32), n=2 * L, axis=1)
    kf = jnp.fft.rfft(kern, n=2 * L, axis=0)
    return jnp.fft.irfft(uf * kf[None], n=2 * L, axis=1)[:, :L]


def hybrid_mixer(h, l, P):
    B, L = h.shape[0], h.shape[1]
    proj = h @ P['w_in'][l]
    A = ATTN_WIDTH
    qa, ka, va, hy = proj[..., :A], proj[..., A:2 * A], proj[..., 2 * A:3 * A], proj[..., 3 * A:]
    q = rope_partial(qa.reshape(B, L, DA_HEADS, 2, DA_HEAD_DIM))
    k = rope_partial(ka.reshape(B, L, DA_HEADS, 2, DA_HEAD_DIM))
    v = va.reshape(B, L, DA_HEADS, DA_V_DIM)
    lam_init = 0.8 - 0.6 * math.exp(-0.3 * l)
    lam = (jnp.exp(jnp.sum(P['lambda_q1'][l].astype(F32) * P['lambda_k1'][l].astype(F32)))
           - jnp.exp(jnp.sum(P['lambda_q2'][l].astype(F32) * P['lambda_k2'][l].astype(F32))) + lam_init)
    o = diff_attention(q, k, v, lam)
    o = rms_norm(o, P['subln_g'][l]) * (1.0 - lam_init)
    attn_out = o.reshape(B, L, ATTN_WIDTH)
    hy = short_conv(hy, P['conv_w'][l], P['conv_b'][l])
    C = HYENA_WIDTH
    x0, x1, vh = hy[..., :C], hy[..., C:2 * C], hy[..., 2 * C:]
    kern = hyena_filters(L, P['filt_w1'][l], P['filt_b1'][l], P['filt_freq1'][l],
                         P['filt_w2'][l], P['filt_b2'][l], P['filt_freq2'][l], P['filt_w3'][l])
    u = x1 * vh
    y = (long_conv(u, kern).astype(u.dtype) + P['hyena_d'][l] * u) * x0
    return jnp.concatenate([attn_out, y], axis=-1) @ P['w_out'][l]


def memory_cross_attention(h, m, wq, wk, wv, wo):
    B, L = h.shape[0], h.shape[1]
    q = (h @ wq).reshape(B, L, X_HEADS, X_HEAD_DIM)
    k = (m @ wk).reshape(B, N_MEM, X_HEADS, X_HEAD_DIM)
    v = (m @ wv).reshape(B, N_MEM, X_HEADS, X_HEAD_DIM)
    s = jnp.einsum('bqhd,bkhd->bhqk', q, k).astype(F32) * (X_HEAD_DIM ** -0.5)
    p = jax.nn.softmax(s, axis=-1).astype(v.dtype)
    o = jnp.einsum('bhqk,bkhd->bqhd', p, v).reshape(B, L, D_MODEL)
    return o @ wo


def swiglu(h, wg, wu, wd):
    return (jax.nn.silu(h @ wg) * (h @ wu)) @ wd


def trunk(x, mem, P):
    for l in range(DEPTH):
        x = x + rms_norm(hybrid_mixer(rms_norm(x, P['ln_mix_pre'][l]), l, P), P['ln_mix_post'][l])
        m = rms_norm(mem, P['ln_mem'][l])
        x = x + rms_norm(memory_cross_attention(rms_norm(x, P['ln_x_pre'][l]), m, P['wq_x'][l], P['wk_x'][l],
                                                P['wv_x'][l], P['wo_x'][l]), P['ln_x_post'][l])
        x = x + rms_norm(swiglu(rms_norm(x, P['ln_ffn_pre'][l]), P['w_gate'][l], P['w_up'][l], P['w_down'][l]),
                         P['ln_ffn_post'][l])
    return x


def setup_inputs(seed: int = 0) -> dict:
    key = jax.random.key(seed)
    ks = iter(jax.random.split(key, 40))

    def nrm(shape, scale):
        return jax.random.normal(next(ks), shape, F32) * scale

    def gain(n):
        return 1.0 + nrm((DEPTH, n), 0.02)

    D = D_MODEL
    return {
        'x_prompt': nrm((BATCH, SEQ, D), 1.0),
        'x_sample': nrm((DEC_BATCH, DEC_SEQ, D), 1.0),
        'mem_prompt': nrm((BATCH, N_MEM, D), 1.0),
        'mem_sample': nrm((DEC_BATCH, N_MEM, D), 1.0),
        'ln_mix_pre': gain(D),
        'ln_mix_post': gain(D),
        'w_in': nrm((DEPTH, D, IN_WIDTH), D ** -0.5),
        'lambda_q1': nrm((DEPTH, DA_HEAD_DIM), 0.1),
        'lambda_k1': nrm((DEPTH, DA_HEAD_DIM), 0.1),
        'lambda_q2': nrm((DEPTH, DA_HEAD_DIM), 0.1),
        'lambda_k2': nrm((DEPTH, DA_HEAD_DIM), 0.1),
        'subln_g': gain(DA_V_DIM),
        'conv_w': nrm((DEPTH, SHORT_CONV, 3 * HYENA_WIDTH), SHORT_CONV ** -0.5),
        'conv_b': nrm((DEPTH, 3 * HYENA_WIDTH), 0.01),
        'filt_w1': nrm((DEPTH, FILTER_EMB, FILTER_ORDER), FILTER_EMB ** -0.5),
        'filt_b1': nrm((DEPTH, FILTER_ORDER), 0.01),
        'filt_freq1': gain(FILTER_ORDER),
        'filt_w2': nrm((DEPTH, FILTER_ORDER, FILTER_ORDER), FILTER_ORDER ** -0.5),
        'filt_b2': nrm((DEPTH, FILTER_ORDER), 0.01),
        'filt_freq2': gain(FILTER_ORDER),
        'filt_w3': nrm((DEPTH, FILTER_ORDER, 2 * HYENA_WIDTH), FILTER_ORDER ** -0.5),
        'hyena_d': nrm((DEPTH, HYENA_WIDTH), 1.0),
        'w_out': nrm((DEPTH, MIX_WIDTH, D), MIX_WIDTH ** -0.5),
        'ln_x_pre': gain(D),
        'ln_x_post': gain(D),
        'ln_mem': gain(D),
        'wq_x': nrm((DEPTH, D, D), D ** -0.5),
        'wk_x': nrm((DEPTH, D, D), D ** -0.5),
        'wv_x': nrm((DEPTH, D, D), D ** -0.5),
        'wo_x': nrm((DEPTH, D, D), D ** -0.5),
        'ln_ffn_pre': gain(D),
        'ln_ffn_post': gain(D),
        'w_gate': nrm((DEPTH, D, D_FF), D ** -0.5),
        'w_up': nrm((DEPTH, D, D_FF), D ** -0.5),
        'w_down': nrm((DEPTH, D_FF, D), D_FF ** -0.5),
    }


def reference(x_prompt, x_sample, mem_prompt, mem_sample, ln_mix_pre, ln_mix_post, w_in,
              lambda_q1, lambda_k1, lambda_q2, lambda_k2, subln_g, conv_w, conv_b,
              filt_w1, filt_b1, filt_freq1, filt_w2, filt_b2, filt_freq2, filt_w3, hyena_d, w_out,
              ln_x_pre, ln_x_post, ln_mem, wq_x, wk_x, wv_x, wo_x,
              ln_ffn_pre, ln_ffn_post, w_gate, w_up, w_down):
    P = dict(ln_mix_pre=ln_mix_pre, ln_mix_post=ln_mix_post, w_in=w_in,
             lambda_q1=lambda_q1, lambda_k1=lambda_k1, lambda_q2=lambda_q2, lambda_k2=lambda_k2,
             subln_g=subln_g, conv_w=conv_w, conv_b=conv_b,
             filt_w1=filt_w1, filt_b1=filt_b1, filt_freq1=filt_freq1,
             filt_w2=filt_w2, filt_b2=filt_b2, filt_freq2=filt_freq2, filt_w3=filt_w3,
             hyena_d=hyena_d, w_out=w_out,
             ln_x_pre=ln_x_pre, ln_x_post=ln_x_post, ln_mem=ln_mem,
             wq_x=wq_x, wk_x=wk_x, wv_x=wv_x, wo_x=wo_x,
             ln_ffn_pre=ln_ffn_pre, ln_ffn_post=ln_ffn_post,
             w_gate=w_gate, w_up=w_up, w_down=w_down)
    y_prompt = trunk(x_prompt, mem_prompt, P)
    y_sample = trunk(x_sample, mem_sample, P)
    return (y_prompt, y_sample)
```

```python
import math
from contextlib import ExitStack
import numpy as np
import ml_dtypes
import concourse.bass as bass
import concourse.mybir as mybir
from concourse.bass_utils import run_bass_kernel_spmd

F32, BF16 = mybir.dt.float32, mybir.dt.bfloat16
AF = mybir.ActivationFunctionType
ALU = mybir.AluOpType
AX = mybir.AxisListType

D = 1024
DEPTH = 4
DFF = 2816
NFF = 22
NMEM = 256
EPS = 1e-6
BF = ml_dtypes.bfloat16


class Res:
    __slots__ = ("w", "r")

    def __init__(self):
        self.w = None
        self.r = {}


class Sched:
    NDS = 12

    def __init__(self, nc, es):
        self.nc = nc
        self.eng = {"pe": nc.tensor, "act": nc.scalar, "dve": nc.vector, "pool": nc.gpsimd, "sp": nc.sync}
        self.sem = {}
        self.cnt = {}
        self.waited = {}
        for k in self.eng:
            self.sem[k] = es.enter_context(nc.semaphore("s_" + k))
            self.cnt[k] = 0
            self.waited[k] = {}
        self.dq = {}
        for q in ("sp", "pool", "act"):
            sems = []
            for i in range(self.NDS):
                key = ("d", q, i)
                self.sem[key] = es.enter_context(nc.semaphore("d_%s_%d" % (q, i)))
                self.cnt[key] = 0
                sems.append(key)
            self.dq[q] = [sems, 0]

    def _deps(self, rd, wr):
        deps = {}

        def add(ev):
            if ev is not None and deps.get(ev[0], 0) < ev[1]:
                deps[ev[0]] = ev[1]

        for r in rd:
            add(r.w)
        for r in wr:
            add(r.w)
            for k, v in r.r.items():
                add((k, v))
        return deps

    def _wait(self, e, deps):
        w = self.waited[e]
        for k, v in deps.items():
            if k == e and e == "pe":
                continue
            if w.get(k, 0) < v:
                self.eng[e].wait_ge(self.sem[k], v)
                w[k] = v

    def _mark(self, ev, rd, wr):
        for r in rd:
            if r.r.get(ev[0], 0) < ev[1]:
                r.r[ev[0]] = ev[1]
        for r in wr:
            r.w = ev
            r.r = {}

    def op(self, e, fn, rd=(), wr=()):
        self._wait(e, self._deps(rd, wr))
        ins = fn(self.eng[e])
        self.cnt[e] += 1
        ins.then_inc(self.sem[e], 1)
        self._mark((e, self.cnt[e]), rd, wr)

    def dma(self, q, out, in_, rd=(), wr=()):
        sems, n = self.dq[q]
        key = sems[n % self.NDS]
        self.dq[q][1] = n + 1
        deps = self._deps(rd, wr)
        if self.cnt[key] > 0 and deps.get(key, 0) < self.cnt[key]:
            deps[key] = self.cnt[key]
        self._wait(q, deps)
        self.eng[q].dma_start(out=out, in_=in_).then_inc(self.sem[key], 16)
        self.cnt[key] += 16
        self._mark((key, self.cnt[key]), rd, wr)

    def barrier(self, engines=("pe", "act", "dve", "pool", "sp")):
        allv = {k: v for k, v in self.cnt.items() if v > 0}
        for e in engines:
            self._wait(e, allv)


def _rope_tables():
    inv = (np.float32(500000.0) ** (-np.arange(0, 16, 2, dtype=np.float32) / np.float32(16))).astype(np.float32)
    pos = np.arange(8192, dtype=np.float32)
    ang = (pos[:, None] * inv[None, :]).astype(np.float32)
    cos, sin = np.cos(ang).astype(np.float32), np.sin(ang).astype(np.float32)
    C = np.ones((128, 8192), np.float32)
    S = np.zeros((128, 8192), np.float32)
    R = np.zeros((128, 128), np.float32)
    for p in range(128):
        d = p % 64
        if d < 8:
            C[p] = cos[:, d]
            S[p] = -sin[:, d]
            R[p + 8, p] = 1
        elif d < 16:
            C[p] = cos[:, d - 8]
            S[p] = sin[:, d - 8]
            R[p - 8, p] = 1
    return C, S, R


def _fft_tables(N1):
    N = 128 * N1
    H = N1 // 2
    t = {}
    n1 = np.arange(N1)[:, None]
    k1 = np.arange(N1)[None, :]
    a = -2 * np.pi * n1 * k1 / N1
    t["f1"] = np.concatenate([np.cos(a), np.sin(a)], 1).astype(BF)
    n2 = np.arange(128)[:, None]
    a = -2 * np.pi * n2 * k1 / N
    t["twr"] = np.concatenate([np.cos(a), np.cos(a)], 1).astype(np.float32)
    t["twi"] = np.concatenate([np.sin(a), np.sin(a)], 1).astype(np.float32)
    a = 2 * np.pi * np.arange(N1)[:, None] * np.arange(128)[None, :] / N
    t["itwr"] = np.concatenate([np.cos(a), np.cos(a)], 1).astype(np.float32)
    t["itwi"] = np.concatenate([np.sin(a), np.sin(a)], 1).astype(np.float32)
    a = 2 * np.pi * np.arange(N1)[:, None] * np.arange(H)[None, :] / N1
    t["g1"] = np.concatenate([np.cos(a) / N, -np.sin(a) / N], 1).astype(BF)
    return t


def _fft128_tables():
    a = -2 * np.pi * np.arange(128)[:, None] * np.arange(128)[None, :] / 128
    f2 = np.concatenate([np.cos(a), np.sin(a), -np.sin(a)], 1).astype(BF)
    b = -a
    gg = np.concatenate([np.cos(b), np.sin(b), -np.sin(b), np.cos(b)], 1).astype(BF)
    return f2, gg


def _filter_tables(L):
    t = np.linspace(0.0, 1.0, L, dtype=np.float32)
    w = (np.float32(2.0 * math.pi) * np.arange(L, dtype=np.float32) / np.float32(L)).astype(np.float32)
    f = np.linspace(1e-4, 15, 16, dtype=np.float32)
    fw = (f[None, :] * w[:, None]).astype(np.float32)
    z = np.concatenate([t[:, None], np.cos(fw), -np.sin(fw)], 1).astype(np.float32)
    idx = np.concatenate([np.arange(L), [0], np.arange(L - 1, 0, -1)])
    zt = np.ascontiguousarray(z[idx].T)
    tn = t[idx].copy()
    tn[L] = 1e4
    return zt, tn[None, :].astype(np.float32)


def _consts():
    c = {}
    C, S, R = _rope_tables()
    c["ropec"], c["ropes"] = C, S
    f2, gg = _fft128_tables()
    ident = np.eye(128, dtype=np.float32)
    c["cb"] = np.concatenate([ident.astype(BF), np.ones((128, 128), BF), R.astype(BF), f2, gg], 1)
    for N1 in (32, 128):
        for k, v in _fft_tables(N1).items():
            c["%s_%d" % (k, N1)] = v
    for L in (2048, 8192):
        zt, tn = _filter_tables(L)
        c["zt_%d" % L] = zt
        c["tn_%d" % L] = tn
    mn, mx = math.log(1e-2) / 1.5, math.log(1e-2) / 0.3
    c["negdelta"] = (-np.abs(np.linspace(mn, mx, 512, dtype=np.float32)))[None, :].astype(np.float32)
    return c


WSPEC = [("ln_mix_pre", (D,)), ("ln_mix_post", (D,)), ("w_in", (D, 3072)), ("lambda_q1", (64,)), ("lambda_k1", (64,)),
         ("lambda_q2", (64,)), ("lambda_k2", (64,)), ("subln_g", (128,)), ("conv_w", (3, 1536)), ("conv_b", (1536,)),
         ("filt_w1", (33, 64)), ("filt_b1", (64,)), ("filt_freq1", (64,)), ("filt_w2", (64, 64)), ("filt_b2", (64,)),
         ("filt_freq2", (64,)), ("filt_w3", (64, 1024)), ("hyena_d", (512,)), ("w_out", (D, D)), ("ln_x_pre", (D,)),
         ("ln_x_post", (D,)), ("ln_mem", (D,)), ("wq_x", (D, D)), ("wk_x", (D, D)), ("wv_x", (D, D)), ("wo_x", (D, D)),
         ("ln_ffn_pre", (D,)), ("ln_ffn_post", (D,)), ("w_gate", (D, DFF)), ("w_up", (D, DFF)), ("w_down", (DFF, D))]


def build(SEQS, depth=DEPTH, dbg=(), only=None):
    NT = sum(SEQS)
    NS = len(SEQS)
    LT = sorted(set(SEQS))
    consts = _consts()
    nc = bass.Bass("TRN2", target_bir_lowering=False)

    def din(name, shape, dt=F32):
        return nc.dram_tensor(name, list(shape), dt, kind="ExternalInput").ap()

    def dscr(name, shape, dt, out=False):
        kind = "ExternalOutput" if (out or name in dbg) else "Internal"
        return nc.dram_tensor(name, list(shape), dt, kind=kind).ap()

    X = din("x", [NT, D])
    MEM = din("mem", [NS * NMEM, D])
    W = {n: din(n, (depth,) + s) for n, s in WSPEC}
    CT = {}
    for k, v in consts.items():
        CT[k] = din("c_" + k, v.shape, BF16 if v.dtype == BF else F32)
    Y = dscr("y", [NT, D], F32, out=True)
    XR = dscr("xr", [NT, D], F32)
    QT = dscr("qt", [4, 128, NT], BF16)
    KT = dscr("kt", [4, 128, NT], BF16)
    V = dscr("v", [NT, 512], BF16)
    HY = dscr("hy", [12, 128, NT], F32)
    MIX = dscr("mix", [8, 128, NT], BF16)
    U = dscr("u", [512, NT], BF16)
    UX = dscr("ux", [4, 128, NT], F32)
    X0 = dscr("x0", [4, 128, NT], F32)
    YC = dscr("yc", [512, NT], F32)
    HT = dscr("ht", [NFF, 128, NT], BF16)
    KERN = {L: dscr("kern%d" % L, [512, 2 * L], BF16) for L in LT}
    KF = {L: dscr("kf%d" % L, [128, 512, 2, L // 64], BF16) for L in LT}

    es = ExitStack()
    S = Sched(nc, es)
    PS = [es.enter_context(nc.psum_tensor("ps%d" % i, [128, 512], F32)) for i in range(8)]
    PSR = [Res() for _ in range(8)]
    psn = [0]

    def psum():
        i = psn[0] % 8
        psn[0] += 1
        return PS[i], PSR[i]

    uid = [0]

    def sb(st, name, shape, dt):
        uid[0] += 1
        return st.enter_context(nc.sbuf_tensor("%s_%d" % (name, uid[0]), list(shape), dt))

    cb = sb(es, "cb", [128, 1280], BF16)
    cbR = Res()
    S.dma("sp", cb[:], CT["cb"][:, :], wr=[cbR])
    ident, ones, rmat = cb[:, 0:128], cb[:, 128:256], cb[:, 256:384]
    f2re, f2im, f2imn = cb[:, 384:512], cb[:, 512:640], cb[:, 640:768]
    gg1, gg2 = cb[:, 768:1024], cb[:, 1024:1280]
    rnorm = {L: sb(es, "rnorm%d" % L, [128, 4], F32) for L in LT}
    rnormR = {L: Res() for L in LT}
    nlam = sb(es, "nlam", [128, 1], F32)
    nlamR = Res()
    epsc = sb(es, "epsc", [128, 1], F32)
    S.op("pool", lambda e: e.memset(epsc[:], EPS), wr=[cbR])

    def bcast_row(ap1d, n):
        return ap1d.partition_broadcast(128)

    def col(ap1d):
        return ap1d.rearrange("(p o) -> p o", o=1)

    def load_w(st, name, ap2d, rows, cols, q="pool"):
        kc = rows // 128
        t = sb(st, name, [128, kc, cols], BF16)
        r = Res()
        for k in range(kc):
            S.dma(q, t[:, k, :], ap2d[k * 128:(k + 1) * 128, :], wr=[r])
        return t, r

    def load_gamma(st, name, ap1d):
        t = sb(st, name, [128, D], F32)
        r = Res()
        S.dma("sp", t[:], bcast_row(ap1d, D), wr=[r])
        return t, r

    def rms_rstd(st_small, src_ap, rd, n, ss, ssR, junk, junkR, idx):
        S.op("act", lambda e: e.activation(out=junk[:], in_=src_ap, func=AF.Square, accum_out=ss[:, idx:idx + 1]),
             rd=rd, wr=[junkR, ssR])

    def finish_rstd(ss, ssR, n, width):
        S.op("dve", lambda e: e.tensor_scalar(out=ss[:, 0:width], in0=ss[:, 0:width], scalar1=1.0 / n, scalar2=EPS,
                                              op0=ALU.mult, op1=ALU.add), rd=[ssR], wr=[ssR])
        S.op("act", lambda e: e.activation(out=ss[:, 0:width], in_=ss[:, 0:width], func=AF.Sqrt), rd=[ssR], wr=[ssR])
        S.op("dve", lambda e: e.reciprocal(out=ss[:, 0:width], in_=ss[:, 0:width]), rd=[ssR], wr=[ssR])

    def norm_transpose(xt, xtR, nj, gam, gamR, xn, xnR, xnT, xnTR, ss, ssR, junk, junkR):
        for j in range(nj):
            rms_rstd(None, xt[:, j, :], [xtR], D, ss, ssR, junk, junkR, j)
        finish_rstd(ss, ssR, D, nj)
        for j in range(nj):
            S.op("dve", lambda e: e.scalar_tensor_tensor(out=xn[:, j, :], in0=xt[:, j, :], scalar=ss[:, j:j + 1],
                                                         in1=gam[:], op0=ALU.mult, op1=ALU.mult),
                 rd=[xtR, ssR, gamR], wr=[xnR[j]])
        for j in range(nj):
            p, pR = psum()
            pb = p[:].bitcast(BF16)

            def tr(e):
                for c in range(8):
                    ins = e.transpose(pb[:, c * 128:(c + 1) * 128], xn[:, j, c * 128:(c + 1) * 128], ident)
                return ins
            S.op("pe", tr, rd=[xnR[j], cbR], wr=[pR])
            eng = "act" if j % 2 == 0 else "dve"
            src = pb.rearrange("p (c t) -> p c t", c=8)
            if eng == "act":
                S.op("act", lambda e: e.activation(out=xnT[:, :, j * 128:(j + 1) * 128], in_=src, func=AF.Copy),
                     rd=[pR], wr=[xnTR])
            else:
                S.op("dve", lambda e: e.tensor_copy(out=xnT[:, :, j * 128:(j + 1) * 128], in_=src), rd=[pR], wr=[xnTR])

    def mm_acc(e, out, pairs):
        n = len(pairs)
        for i, (l, r) in enumerate(pairs):
            ins = e.matmul(out, l, r, start=(i == 0), stop=(i == n - 1))
        return ins

    def postnorm_residual(ps2, ps2R, gam, gamR, xt_j, xtR, ss, ssR, junk, junkR, tmp, tmpR):
        for h in range(2):
            S.op("act", lambda e: e.activation(out=junk[:, 0:512], in_=ps2[h][:], func=AF.Square,
                                               accum_out=ss[:, h:h + 1]), rd=[ps2R[h]], wr=[junkR, ssR])
        S.op("dve", lambda e: e.tensor_tensor(out=ss[:, 2:3], in0=ss[:, 0:1], in1=ss[:, 1:2], op=ALU.add),
             rd=[ssR], wr=[ssR])
        S.op("dve", lambda e: e.tensor_scalar(out=ss[:, 2:3], in0=ss[:, 2:3], scalar1=1.0 / D, scalar2=EPS,
                                              op0=ALU.mult, op1=ALU.add), rd=[ssR], wr=[ssR])
        S.op("act", lambda e: e.activation(out=ss[:, 2:3], in_=ss[:, 2:3], func=AF.Sqrt), rd=[ssR], wr=[ssR])
        S.op("dve", lambda e: e.reciprocal(out=ss[:, 2:3], in_=ss[:, 2:3]), rd=[ssR], wr=[ssR])
        for h in range(2):
            S.op("dve", lambda e: e.scalar_tensor_tensor(out=tmp[:, h * 512:(h + 1) * 512], in0=ps2[h][:],
                                                         scalar=ss[:, 2:3], in1=gam[:, h * 512:(h + 1) * 512],
                                                         op0=ALU.mult, op1=ALU.mult),
                 rd=[ps2R[h], ssR, gamR], wr=[tmpR])
        S.op("pool", lambda e: e.tensor_tensor(out=xt_j, in0=xt_j, in1=tmp[:], op=ALU.add), rd=[tmpR, xtR], wr=[xtR])

    seq_off = [sum(SEQS[:i]) for i in range(NS)]
    tiles = []
    for si, L in enumerate(SEQS):
        for t in range(L // 512):
            tiles.append((seq_off[si] + t * 512, si, t * 512))

    def xview(ap, t0):
        return ap[t0:t0 + 512, :].rearrange("(j p) d -> p j d", p=128)

    def phase1(l, SRC):
        with ExitStack() as st:
            win, winR = load_w(st, "win", W["w_in"][l], D, 3072)
            gam, gamR = load_gamma(st, "g1", W["ln_mix_pre"][l])
            xt = [sb(st, "xt%d" % i, [128, 4, D], F32) for i in range(2)]
            xtR = [Res() for _ in range(2)]
            xn = sb(st, "xn", [128, 4, D], BF16)
            xnR = [Res() for _ in range(4)]
            xnT = sb(st, "xnT", [128, 8, 512], BF16)
            xnTR = Res()
            ss = sb(st, "ss", [128, 8], F32)
            ssR = Res()
            junk = sb(st, "junk", [128, D], F32)
            junkR = Res()
            rc = [sb(st, "rc%d" % i, [128, 2, 512], F32) for i in range(2)]
            rcR = [Res() for _ in range(2)]
            qsb = [sb(st, "qsb%d" % i, [128, 512], BF16) for i in range(2)]
            qsbR = [Res() for _ in range(2)]
            t1 = [sb(st, "t1%d" % i, [128, 512], F32) for i in range(2)]
            t1R = [Res() for _ in range(2)]
            t2 = [sb(st, "t2%d" % i, [128, 512], F32) for i in range(2)]
            t2R = [Res() for _ in range(2)]
            qr = [sb(st, "qr%d" % i, [128, 512], BF16) for i in range(3)]
            qrR = [Res() for _ in range(3)]
            hs = [sb(st, "hs%d" % i, [128, 512], F32) for i in range(3)]
            hsR = [Res() for _ in range(3)]
            cn = [0, 0, 0]
            S.dma("sp", xt[0][:], xview(SRC, tiles[0][0]), wr=[xtR[0]])
            for ti, (t0, si, p0) in enumerate(tiles):
                b = ti % 2
                if ti + 1 < len(tiles):
                    S.dma("sp", xt[1 - b][:], xview(SRC, tiles[ti + 1][0]), wr=[xtR[1 - b]])
                S.dma("sp", rc[b][:, 0, :], CT["ropec"][:, p0:p0 + 512], wr=[rcR[b]])
                S.dma("sp", rc[b][:, 1, :], CT["ropes"][:, p0:p0 + 512], wr=[rcR[b]])
                norm_transpose(xt[b], xtR[b], 4, gam, gamR, xn, xnR, xnT, xnTR, ss, ssR, junk, junkR)
                for ch in range(8):
                    p, pR = psum()
                    S.op("pe", lambda e: mm_acc(e, p[:], [(win[:, k, ch * 128:(ch + 1) * 128], xnT[:, k, :])
                                                          for k in range(8)]), rd=[winR, xnTR], wr=[pR])
                    i = cn[0] % 2
                    cn[0] += 1
                    S.op("act", lambda e: e.activation(out=qsb[i][:], in_=p[:], func=AF.Copy), rd=[pR], wr=[qsbR[i]])
                    p2, p2R = psum()
                    S.op("pe", lambda e: e.matmul(p2[:], rmat, qsb[i][:], start=True, stop=True),
                         rd=[qsbR[i], cbR], wr=[p2R])
                    S.op("dve", lambda e: e.tensor_tensor(out=t1[i][:], in0=qsb[i][:], in1=rc[b][:, 0, :], op=ALU.mult),
                         rd=[qsbR[i], rcR[b]], wr=[t1R[i]])
                    S.op("dve", lambda e: e.tensor_tensor(out=t2[i][:], in0=p2[:], in1=rc[b][:, 1, :], op=ALU.mult),
                         rd=[p2R, rcR[b]], wr=[t2R[i]])
                    o = cn[1] % 3
                    cn[1] += 1
                    S.op("pool", lambda e: e.tensor_tensor(out=qr[o][:], in0=t1[i][:], in1=t2[i][:], op=ALU.add),
                         rd=[t1R[i], t2R[i]], wr=[qrR[o]])
                    dst = (QT if ch < 4 else KT)[ch % 4, :, t0:t0 + 512]
                    S.dma("pool", dst, qr[o][:], rd=[qrR[o]])
                for j in range(4):
                    p, pR = psum()
                    S.op("pe", lambda e: mm_acc(e, p[:], [(xnT[:, k, j * 128:(j + 1) * 128], win[:, k, 1024:1536])
                                                          for k in range(8)]), rd=[winR, xnTR], wr=[pR])
                    o = cn[1] % 3
                    cn[1] += 1
                    S.op("act", lambda e: e.activation(out=qr[o][:], in_=p[:], func=AF.Copy), rd=[pR], wr=[qrR[o]])
                    S.dma("pool", V[t0 + j * 128:t0 + (j + 1) * 128, :], qr[o][:], rd=[qrR[o]])
                for ch in range(12):
                    p, pR = psum()
                    c0 = 1536 + ch * 128
                    S.op("pe", lambda e: mm_acc(e, p[:], [(win[:, k, c0:c0 + 128], xnT[:, k, :]) for k in range(8)]),
                         rd=[winR, xnTR], wr=[pR])
                    o = cn[2] % 3
                    cn[2] += 1
                    if ch % 2 == 0:
                        S.op("act", lambda e: e.activation(out=hs[o][:], in_=p[:], func=AF.Copy), rd=[pR], wr=[hsR[o]])
                    else:
                        S.op("dve", lambda e: e.tensor_copy(out=hs[o][:], in_=p[:]), rd=[pR], wr=[hsR[o]])
                    S.dma("pool", HY[ch, :, t0:t0 + 512], hs[o][:], rd=[hsR[o]])
            S.barrier()

    def lam_compute(l):
        lam_init = 0.8 - 0.6 * math.exp(-0.3 * l)
        with ExitStack() as st:
            lt = sb(st, "lamt", [128, 4, 64], F32)
            ltR = Res()
            for i, n in enumerate(("lambda_q1", "lambda_k1", "lambda_q2", "lambda_k2")):
                S.dma("sp", lt[:, i, :], bcast_row(W[n][l], 64), wr=[ltR])
            pr = sb(st, "lampr", [128, 2, 64], F32)
            prR = Res()
            sm = sb(st, "lamsm", [128, 2], F32)
            smR = Res()
            for i in range(2):
                S.op("dve", lambda e: e.tensor_tensor(out=pr[:, i, :], in0=lt[:, 2 * i, :], in1=lt[:, 2 * i + 1, :],
                                                      op=ALU.mult), rd=[ltR], wr=[prR])
            S.op("dve", lambda e: e.tensor_reduce(out=sm[:], in_=pr[:], axis=AX.X, op=ALU.add), rd=[prR], wr=[smR])
            S.op("act", lambda e: e.activation(out=sm[:], in_=sm[:], func=AF.Exp), rd=[smR], wr=[smR])
            S.op("dve", lambda e: e.scalar_tensor_tensor(out=nlam[:], in0=sm[:, 1:2], scalar=-lam_init, in1=sm[:, 0:1],
                                                         op0=ALU.add, op1=ALU.subtract), rd=[smR], wr=[nlamR])
            S.barrier()
        return lam_init

    def phase2(l):
        lam_init = lam_compute(l)
        with ExitStack() as st:
            gs = sb(st, "gsub", [128, 1], F32)
            gsR = Res()
            S.dma("sp", gs[:], col(W["subln_g"][l]), wr=[gsR])
            S.op("dve", lambda e: e.tensor_scalar(out=gs[:], in0=gs[:], scalar1=1.0 - lam_init, scalar2=None,
                                                  op0=ALU.mult), rd=[gsR], wr=[gsR])
            LM = max(SEQS)
            ksb = [[sb(st, "ksb%d_%d" % (i, c), [128, LM], BF16) for c in range(2)] for i in range(2)]
            ksbR = [Res() for _ in range(2)]
            for i in range(2):
                S.op("pool", lambda e: e.memset(ksb[i][0][64:128, :], 0.0), wr=[ksbR[i]])
                S.op("pool", lambda e: e.memset(ksb[i][1][0:64, :], 0.0), wr=[ksbR[i]])
            lz = [sb(st, "lz%d" % i, [128, 512], F32) for i in range(2)]
            lzR = [Res() for _ in range(2)]
            vsb = [sb(st, "vsb%d" % i, [128, LM // 128, 128], BF16) for i in range(2)]
            vsbR = [Res() for _ in range(2)]
            qsb = [sb(st, "q2sb%d" % i, [128, 512], BF16) for i in range(2)]
            qsbR = [Res() for _ in range(2)]
            NE = 8
            esb = [sb(st, "esb%d" % i, [128, 512], BF16) for i in range(NE)]
            esbR = [Res() for _ in range(NE)]
            rz = [sb(st, "rz%d" % i, [128, 512], F32) for i in range(2)]
            rzR = [Res() for _ in range(2)]
            tt = [sb(st, "tt%d" % i, [128, 512], F32) for i in range(2)]
            ttR = [Res() for _ in range(2)]
            osb = sb(st, "osb", [128, 512], F32)
            osbR = Res()
            sq = sb(st, "sq", [128, 512], BF16)
            sqR = Res()
            rs = sb(st, "rs", [128, 512], F32)
            rsR = Res()
            ob = [sb(st, "ob%d" % i, [128, 512], BF16) for i in range(2)]
            obR = [Res() for _ in range(2)]
            hn = 0
            qn = 0
            en = 0
            for si, L in enumerate(SEQS):
                s0 = seq_off[si]
                for h in range(4):
                    kb = hn % 2
                    hn += 1
                    S.dma("sp", ksb[kb][0][0:64, 0:L], KT[h, 0:64, s0:s0 + L], wr=[ksbR[kb]])
                    S.dma("sp", ksb[kb][1][64:128, 0:L], KT[h, 64:128, s0:s0 + L], wr=[ksbR[kb]])
                    S.dma("sp", vsb[kb][:, 0:L // 128, :],
                          V[s0:s0 + L, h * 128:(h + 1) * 128].rearrange("(kc p) e -> p kc e", p=128), wr=[vsbR[kb]])
                    for qt in range(L // 512):
                        t0 = s0 + qt * 512
                        qb = qn % 2
                        qn += 1
                        S.dma("sp", qsb[qb][:], QT[h, :, t0:t0 + 512], wr=[qsbR[qb]])
                        acc = [(PS[i], PSR[i]) for i in range(4)]
                        nk = L // 128
                        items = [(kc, c) for kc in range(nk) for c in range(2)]
                        LA = 3
                        einfo = {}
                        for idx in range(len(items) + LA):
                            if idx < len(items):
                                kc, c = items[idx]
                                sp_, spR = PS[4 + idx % 4], PSR[4 + idx % 4]
                                S.op("pe", lambda e: e.matmul(sp_[:], ksb[kb][c][:, kc * 128:(kc + 1) * 128],
                                                              qsb[qb][:], start=True, stop=True),
                                     rd=[ksbR[kb], qsbR[qb]], wr=[spR])
                                ei = en % NE
                                en += 1
                                S.op("act", lambda e: e.activation(out=esb[ei][:], in_=sp_[:], func=AF.Exp, scale=0.125),
                                     rd=[spR], wr=[esbR[ei]])
                                einfo[idx] = ei
                            if idx >= LA:
                                kc, c = items[idx - LA]
                                ei = einfo.pop(idx - LA)

                                def pv(e):
                                    e.matmul(acc[c][0][:], vsb[kb][:, kc, :], esb[ei][:], start=(kc == 0), stop=(kc == nk - 1))
                                    return e.matmul(acc[2 + c][0][:], ones, esb[ei][:], start=(kc == 0), stop=(kc == nk - 1))
                                S.op("pe", pv, rd=[vsbR[kb], esbR[ei], cbR], wr=[acc[c][1], acc[2 + c][1]])
                        for c in range(2):
                            S.op("dve", lambda e: e.tensor_copy(out=tt[c][:], in_=acc[c][0][:]), rd=[acc[c][1]], wr=[ttR[c]])
                            S.op("act", lambda e: e.activation(out=lz[c][:], in_=acc[2 + c][0][:], func=AF.Ln), rd=[acc[2 + c][1]], wr=[lzR[c]])
                        for c in range(2):
                            S.op("act", lambda e: e.activation(out=rz[c][:], in_=lz[c][:], func=AF.Exp, scale=-1.0), rd=[lzR[c]], wr=[rzR[c]])
                            S.op("dve", lambda e: e.tensor_tensor(out=tt[c][:], in0=tt[c][:], in1=rz[c][:], op=ALU.mult),
                                 rd=[ttR[c], rzR[c]], wr=[ttR[c]])
                        S.op("dve", lambda e: e.scalar_tensor_tensor(out=osb[:], in0=tt[1][:], scalar=nlam[:, 0:1], in1=tt[0][:],
                                                                     op0=ALU.mult, op1=ALU.add),
                             rd=[ttR[0], ttR[1], nlamR], wr=[osbR])
                        S.op("pool", lambda e: e.tensor_tensor(out=sq[:], in0=osb[:], in1=osb[:], op=ALU.mult), rd=[osbR], wr=[sqR])
                        psq, psqR = PS[4], PSR[4]
                        S.op("pe", lambda e: e.matmul(psq[:], ones, sq[:], start=True, stop=True), rd=[sqR, cbR], wr=[psqR])
                        S.op("act", lambda e: e.activation(out=rs[:], in_=psq[:], func=AF.Ln, scale=1.0 / 128, bias=epsc[:, 0:1]), rd=[psqR], wr=[rsR])
                        S.op("act", lambda e: e.activation(out=rs[:], in_=rs[:], func=AF.Exp, scale=-0.5), rd=[rsR], wr=[rsR])
                        oi = qn % 2
                        S.op("dve", lambda e: e.scalar_tensor_tensor(out=ob[oi][:], in0=osb[:], scalar=gs[:, 0:1], in1=rs[:],
                                                                     op0=ALU.mult, op1=ALU.mult),
                             rd=[osbR, rsR, gsR], wr=[obR[oi]])
                        S.dma("pool", MIX[h, :, t0:t0 + 512], ob[oi][:], rd=[obR[oi]])
            S.barrier()

    def sin_big(st, a, aR, n, tag):
        s4 = sb(st, "s4" + tag, [64, n], F32)
        s8 = sb(st, "s8" + tag, [64, n], F32)
        r4, r8 = Res(), Res()
        S.op("act", lambda e: e.activation(out=s4[:], in_=a, func=AF.Sin, scale=0.25), rd=[aR], wr=[r4])
        S.op("act", lambda e: e.activation(out=s8[:], in_=a, func=AF.Sin, scale=0.125), rd=[aR], wr=[r8])
        S.op("dve", lambda e: e.tensor_tensor(out=s8[:], in0=s8[:], in1=s8[:], op=ALU.mult), rd=[r8], wr=[r8])
        S.op("dve", lambda e: e.tensor_scalar(out=s8[:], in0=s8[:], scalar1=-2.0, scalar2=1.0, op0=ALU.mult, op1=ALU.add),
             rd=[r8], wr=[r8])
        S.op("dve", lambda e: e.tensor_tensor(out=s8[:], in0=s8[:], in1=s4[:], op=ALU.mult), rd=[r8, r4], wr=[r8])
        S.op("dve", lambda e: e.tensor_tensor(out=s4[:], in0=s4[:], in1=s4[:], op=ALU.mult), rd=[r4], wr=[r4])
        S.op("dve", lambda e: e.tensor_scalar(out=s4[:], in0=s4[:], scalar1=-2.0, scalar2=1.0, op0=ALU.mult, op1=ALU.add),
             rd=[r4], wr=[r4])
        S.op("dve", lambda e: e.scalar_tensor_tensor(out=a, in0=s8[:], scalar=4.0, in1=s4[:], op0=ALU.mult, op1=ALU.mult),
             rd=[r8, r4, aR], wr=[aR])

    def fft_fwd(st, L, src, c0, ncg, H, dst_kf=None, dst_sb=None, kfs=None, tabs=None):
        N1 = L // 64
        f1, twr, twi, ut, utR, pp, ppR, are, aim, aR = tabs
        S.dma("sp", ut[0:H, 0:ncg, :], src.rearrange("c (n1 n2) -> n1 c n2", n2=128), wr=[utR])
        cpb = 512 // (2 * N1)
        for g in range(ncg // cpb):
            p, pR = psum()
            pv = p[:].rearrange("p (c k) -> p c k", c=cpb)

            def s1(e):
                for c in range(cpb):
                    ins = e.matmul(pv[:, c, :], ut[0:H, g * cpb + c, :], f1[0:H, :], start=True, stop=True)
                return ins
            S.op("pe", s1, rd=[utR], wr=[pR])
            i = g % 2
            p1 = pp[i][:, 0, :].rearrange("p (c k) -> p c k", c=cpb)
            p2 = pp[i][:, 1, :].rearrange("p (c k) -> p c k", c=cpb)
            S.op("dve", lambda e: e.tensor_tensor(out=p1, in0=pv, in1=twr[:, None, :].to_broadcast([128, cpb, 2 * N1]),
                                                  op=ALU.mult), rd=[pR], wr=[ppR[i]])
            S.op("dve", lambda e: e.tensor_tensor(out=p2, in0=pv, in1=twi[:, None, :].to_broadcast([128, cpb, 2 * N1]),
                                                  op=ALU.mult), rd=[pR], wr=[ppR[i]])
            cs = slice(g * cpb, (g + 1) * cpb)
            S.op("pool", lambda e: e.tensor_tensor(out=are[:, cs, :], in0=p1[:, :, 0:N1], in1=p2[:, :, N1:2 * N1],
                                                   op=ALU.subtract), rd=[ppR[i]], wr=[aR])
            S.op("pool", lambda e: e.tensor_tensor(out=aim[:, cs, :], in0=p2[:, :, 0:N1], in1=p1[:, :, N1:2 * N1],
                                                   op=ALU.add), rd=[ppR[i]], wr=[aR])
        cpc = 512 // N1
        for g in range(ncg // cpc):
            cs = slice(g * cpc, (g + 1) * cpc)
            ar = are[:, cs, :].rearrange("p c k -> p (c k)")
            ai = aim[:, cs, :].rearrange("p c k -> p (c k)")
            pr_, prR = psum()
            pi_, piR = psum()
            S.op("pe", lambda e: mm_acc(e, pr_[:], [(f2re, ar), (f2imn, ai)]), rd=[aR, cbR], wr=[prR])
            S.op("pe", lambda e: mm_acc(e, pi_[:], [(f2im, ar), (f2re, ai)]), rd=[aR, cbR], wr=[piR])
            xr = pr_[:].rearrange("p (c k) -> p c k", c=cpc)
            xi = pi_[:].rearrange("p (c k) -> p c k", c=cpc)
            if dst_kf is not None:
                kst, kstR = dst_sb
                i = g % 2
                S.op("act", lambda e: e.activation(out=kst[i][:, 0:cpc, 0, :], in_=xr, func=AF.Copy), rd=[prR], wr=[kstR[i]])
                S.op("dve", lambda e: e.tensor_copy(out=kst[i][:, 0:cpc, 1, :], in_=xi), rd=[piR], wr=[kstR[i]])
                S.dma("pool", dst_kf[:, c0 + g * cpc:c0 + (g + 1) * cpc, :, :], kst[i][:, 0:cpc, :, :], rd=[kstR[i]])
            else:
                kf, kfR = kfs
                yre, yim, yR, mt, mtR = dst_sb
                kre = kf[:, cs, 0, :]
                kim = kf[:, cs, 1, :]
                i = g % 2
                m = [mt[i][:, q, :].rearrange("p (c k) -> p c k", c=cpc) for q in range(4)]
                S.op("dve", lambda e: e.tensor_tensor(out=m[0], in0=xr, in1=kre, op=ALU.mult), rd=[prR, kfR], wr=[mtR[i]])
                S.op("dve", lambda e: e.tensor_tensor(out=m[1], in0=xi, in1=kim, op=ALU.mult), rd=[piR, kfR], wr=[mtR[i]])
                S.op("dve", lambda e: e.tensor_tensor(out=m[2], in0=xr, in1=kim, op=ALU.mult), rd=[prR, kfR], wr=[mtR[i]])
                S.op("dve", lambda e: e.tensor_tensor(out=m[3], in0=xi, in1=kre, op=ALU.mult), rd=[piR, kfR], wr=[mtR[i]])
                S.op("pool", lambda e: e.tensor_tensor(out=yre[:, cs, :], in0=m[0], in1=m[1], op=ALU.subtract),
                     rd=[mtR[i]], wr=[yR])
                S.op("pool", lambda e: e.tensor_tensor(out=yim[:, cs, :], in0=m[2], in1=m[3], op=ALU.add),
                     rd=[mtR[i]], wr=[yR])

    def fft_tabs(st, L, ncg, H):
        N1 = L // 64
        f1 = sb(st, "f1", [N1, 2 * N1], BF16)
        twr = sb(st, "twr", [128, 2 * N1], F32)
        twi = sb(st, "twi", [128, 2 * N1], F32)
        tR = Res()
        S.dma("sp", f1[:], CT["f1_%d" % N1][:, :], wr=[tR])
        S.dma("sp", twr[:], CT["twr_%d" % N1][:, :], wr=[tR])
        S.dma("sp", twi[:], CT["twi_%d" % N1][:, :], wr=[tR])
        ut = sb(st, "ut", [N1, ncg, 128], BF16)
        pp = [sb(st, "pp%d" % i, [128, 2, 512], F32) for i in range(2)]
        are = sb(st, "are", [128, ncg, N1], BF16)
        aim = sb(st, "aim", [128, ncg, N1], BF16)
        S.barrier()
        return (f1, twr, twi, ut, Res(), pp, [Res(), Res()], are, aim, Res())

    def filters(l):
        for L in LT:
            N1 = L // 64
            with ExitStack() as st:
                w1 = sb(st, "fw1", [33, 64], F32)
                w2 = sb(st, "fw2", [64, 64], F32)
                w3 = sb(st, "fw3", [64, 1024], F32)
                pv = sb(st, "fpv", [64, 4], F32)
                nd = sb(st, "fnd", [1, 512], F32)
                wR = Res()
                S.dma("sp", w1[:], W["filt_w1"][l], wr=[wR])
                S.dma("sp", w2[:], W["filt_w2"][l], wr=[wR])
                S.dma("sp", w3[:], W["filt_w3"][l], wr=[wR])
                for i, n in enumerate(("filt_b1", "filt_freq1", "filt_b2", "filt_freq2")):
                    S.dma("sp", pv[:, i:i + 1], col(W[n][l]), wr=[wR])
                S.dma("sp", nd[:], CT["negdelta"][:, :], wr=[wR])
                nch = 2 * L // 512
                zt = [sb(st, "fzt%d" % i, [33, 512], F32) for i in range(2)]
                ztR = [Res(), Res()]
                tn = [sb(st, "ftn%d" % i, [1, 512], F32) for i in range(2)]
                a1 = sb(st, "fa1", [64, 512], F32)
                a1R = Res()
                a2 = sb(st, "fa2", [64, 512], F32)
                a2R = Res()
                wsb = [sb(st, "fws%d" % i, [128, 512], F32) for i in range(2)]
                wsbR = [Res(), Res()]
                kc_ = [sb(st, "fkc%d" % i, [128, 512], F32) for i in range(2)]
                kcR = [Res(), Res()]
                kb_ = [sb(st, "fkb%d" % i, [128, 512], BF16) for i in range(2)]
                kbR = [Res(), Res()]
                nrm = sb(st, "fnrm", [128, 4, nch], F32)
                nrmR = Res()
                n_ = 0
                for ci in range(nch):
                    b = ci % 2
                    S.dma("sp", zt[b][:], CT["zt_%d" % L][:, ci * 512:(ci + 1) * 512], wr=[ztR[b]])
                    S.dma("sp", tn[b][:], CT["tn_%d" % L][:, ci * 512:(ci + 1) * 512], wr=[ztR[b]])
                    p, pR = psum()
                    S.op("pe", lambda e: e.matmul(p[0:64, :], w1[:], zt[b][:], start=True, stop=True), rd=[wR, ztR[b]], wr=[pR])
                    S.op("dve", lambda e: e.tensor_scalar(out=a1[:], in0=p[0:64, :], scalar1=pv[:, 0:1], scalar2=pv[:, 1:2],
                                                          op0=ALU.add, op1=ALU.mult), rd=[pR, wR], wr=[a1R])
                    with ExitStack() as st2:
                        sin_big(st2, a1[:], a1R, 512, "a")
                        p, pR = psum()
                        S.op("pe", lambda e: e.matmul(p[0:64, :], w2[:], a1[:], start=True, stop=True), rd=[wR, a1R], wr=[pR])
                        S.op("dve", lambda e: e.tensor_scalar(out=a2[:], in0=p[0:64, :], scalar1=pv[:, 2:3], scalar2=pv[:, 3:4],
                                                              op0=ALU.add, op1=ALU.mult), rd=[pR, wR], wr=[a2R])
                        sin_big(st2, a2[:], a2R, 512, "b")
                        S.barrier(("act", "dve"))
                    half = 0 if ci * 512 < L else 1
                    for fc in range(4):
                        i = n_ % 2
                        n_ += 1
                        p3, p3R = psum()
                        wc = w3[:, half * 512 + fc * 128: half * 512 + (fc + 1) * 128]
                        S.op("pe", lambda e: e.matmul(p3[:], wc, a2[:], start=True, stop=True), rd=[wR, a2R], wr=[p3R])
                        pw, pwR = psum()
                        S.op("pe", lambda e: e.matmul(pw[:], nd[:, fc * 128:(fc + 1) * 128], tn[b][:], start=True, stop=True),
                             rd=[wR, ztR[b]], wr=[pwR])
                        S.op("act", lambda e: e.activation(out=wsb[i][:], in_=pw[:], func=AF.Exp), rd=[pwR], wr=[wsbR[i]])
                        S.op("dve", lambda e: e.tensor_tensor(out=kc_[i][:], in0=p3[:], in1=wsb[i][:], op=ALU.mult),
                             rd=[p3R, wsbR[i]], wr=[kcR[i]])
                        if ci == 0:
                            p4, p4R = psum()
                            wcb = w3[:, 512 + fc * 128: 512 + (fc + 1) * 128]
                            S.op("pe", lambda e: e.matmul(p4[:, 0:2], wcb, a2[:, 0:2], start=True, stop=True),
                                 rd=[wR, a2R], wr=[p4R])
                            S.op("dve", lambda e: e.tensor_tensor(out=kc_[i][:, 0:1], in0=kc_[i][:, 0:1], in1=p4[:, 0:1],
                                                                  op=ALU.add), rd=[p4R, kcR[i]], wr=[kcR[i]])
                        S.op("dve", lambda e: e.tensor_reduce(out=nrm[:, fc, ci:ci + 1], in_=kc_[i][:], axis=AX.X, op=ALU.add,
                                                              apply_absolute_value=True), rd=[kcR[i]], wr=[nrmR])
                        S.op("act", lambda e: e.activation(out=kb_[i][:], in_=kc_[i][:], func=AF.Copy), rd=[kcR[i]], wr=[kbR[i]])
                        S.dma("pool", KERN[L][fc * 128:(fc + 1) * 128, ci * 512:(ci + 1) * 512], kb_[i][:], rd=[kbR[i]])
                S.op("dve", lambda e: e.tensor_reduce(out=rnorm[L][:], in_=nrm[:], axis=AX.X, op=ALU.add), rd=[nrmR], wr=[rnormR[L]])
                S.op("dve", lambda e: e.reciprocal(out=rnorm[L][:], in_=rnorm[L][:]), rd=[rnormR[L]], wr=[rnormR[L]])
                S.barrier()
            with ExitStack() as st:
                NCG = 32
                tabs = fft_tabs(st, L, NCG, N1)
                kst = [sb(st, "kst%d" % i, [128, 512 // N1, 2, N1], BF16) for i in range(2)]
                kstR = [Res(), Res()]
                for c0 in range(0, 512, NCG):
                    fft_fwd(st, L, KERN[L][c0:c0 + NCG, :], c0, NCG, N1, dst_kf=KF[L], dst_sb=(kst, kstR), tabs=tabs)
                S.barrier()

    def phase3(l):
        sub = (lambda k: only is None or 'p3' in only or k in only)
        if sub('p3f'):
            filters(l)
        TB = 2048
        if sub('p3a'):
            with ExitStack() as st:
                cw = sb(st, "cw", [128, 12, 4], F32)
                hd = sb(st, "hd", [128, 4], F32)
                cwR = Res()
                for ch in range(12):
                    for j in range(3):
                        S.dma("sp", cw[:, ch, j:j + 1], col(W["conv_w"][l][j, ch * 128:(ch + 1) * 128]), wr=[cwR])
                    S.dma("sp", cw[:, ch, 3:4], col(W["conv_b"][l][ch * 128:(ch + 1) * 128]), wr=[cwR])
                for cc in range(4):
                    S.dma("sp", hd[:, cc:cc + 1], col(W["hyena_d"][l][cc * 128:(cc + 1) * 128]), wr=[cwR])
                hin = [[sb(st, "hin%d_%d" % (i, s), [128, TB + 2], F32) for s in range(3)] for i in range(2)]
                hinR = [[Res() for s in range(3)] for i in range(2)]
                cv = [sb(st, "cv%d" % s, [128, TB], F32) for s in range(3)]
                cvR = [Res() for s in range(3)]
                uu = sb(st, "uu", [128, TB], F32)
                uuR = Res()
                ub = [sb(st, "ub%d" % i, [128, TB], BF16) for i in range(2)]
                ubR = [Res(), Res()]
                uxo = [sb(st, "uxo%d" % i, [128, TB], F32) for i in range(2)]
                uxoR = [Res(), Res()]
                x0o = [sb(st, "x0o%d" % i, [128, TB], F32) for i in range(2)]
                x0oR = [Res(), Res()]
                n_ = 0
                for si, L in enumerate(SEQS):
                    s0 = seq_off[si]
                    for tb in range(L // TB):
                        a = tb * TB
                        lo = 1 if a == 0 else 0
                        hi = 1 if a + TB == L else 0
                        for cc in range(4):
                            i = n_ % 2
                            n_ += 1
                            for s in range(3):
                                ch = s * 4 + cc
                                if lo:
                                    S.op("pool", lambda e: e.memset(hin[i][s][:, 0:1], 0.0), wr=[hinR[i][s]])
                                if hi:
                                    S.op("pool", lambda e: e.memset(hin[i][s][:, TB + 1:TB + 2], 0.0), wr=[hinR[i][s]])
                                S.dma("sp", hin[i][s][:, lo:TB + 2 - hi], HY[ch, :, s0 + a - 1 + lo:s0 + a + TB + 1 - hi], wr=[hinR[i][s]])
                                S.op("dve", lambda e: e.tensor_scalar(out=cv[s][:], in0=hin[i][s][:, 1:TB + 1], scalar1=cw[:, ch, 1:2],
                                                                      scalar2=cw[:, ch, 3:4], op0=ALU.mult, op1=ALU.add),
                                     rd=[hinR[i][s], cwR], wr=[cvR[s]])
                                S.op("dve", lambda e: e.scalar_tensor_tensor(out=cv[s][:], in0=hin[i][s][:, 0:TB], scalar=cw[:, ch, 0:1],
                                                                             in1=cv[s][:], op0=ALU.mult, op1=ALU.add),
                                     rd=[hinR[i][s], cwR, cvR[s]], wr=[cvR[s]])
                                S.op("dve", lambda e: e.scalar_tensor_tensor(out=cv[s][:], in0=hin[i][s][:, 2:TB + 2], scalar=cw[:, ch, 2:3],
                                                                             in1=cv[s][:], op0=ALU.mult, op1=ALU.add),
                                     rd=[hinR[i][s], cwR, cvR[s]], wr=[cvR[s]])
                            S.op("pool", lambda e: e.tensor_tensor(out=uu[:], in0=cv[1][:], in1=cv[2][:], op=ALU.mult),
                                 rd=[cvR[1], cvR[2]], wr=[uuR])
                            S.op("act", lambda e: e.activation(out=ub[i][:], in_=uu[:], func=AF.Copy), rd=[uuR], wr=[ubR[i]])
                            S.op("dve", lambda e: e.scalar_tensor_tensor(out=uxo[i][:], in0=uu[:], scalar=hd[:, cc:cc + 1], in1=cv[0][:],
                                                                         op0=ALU.mult, op1=ALU.mult), rd=[uuR, cvR[0], cwR], wr=[uxoR[i]])
                            S.op("act", lambda e: e.activation(out=x0o[i][:], in_=cv[0][:], func=AF.Copy, scale=rnorm[L][:, cc:cc + 1]),
                                 rd=[cvR[0], rnormR[L]], wr=[x0oR[i]])
                            S.dma("pool", U[cc * 128:(cc + 1) * 128, s0 + a:s0 + a + TB], ub[i][:], rd=[ubR[i]])
                            S.dma("pool", UX[cc, :, s0 + a:s0 + a + TB], uxo[i][:], rd=[uxoR[i]])
                            S.dma("pool", X0[cc, :, s0 + a:s0 + a + TB], x0o[i][:], rd=[x0oR[i]])
                S.barrier()
        if sub('p3b'):
            for L in LT:
                N1 = L // 64
                H = N1 // 2
                NCG = 32
                with ExitStack() as st:
                    tabs = fft_tabs(st, L, NCG, H)
                    itr = sb(st, "itwr", [N1, 256], F32)
                    iti = sb(st, "itwi", [N1, 256], F32)
                    g1 = sb(st, "g1", [N1, 2 * H], BF16)
                    itR = Res()
                    S.dma("sp", itr[:], CT["itwr_%d" % N1][:, :], wr=[itR])
                    S.dma("sp", iti[:], CT["itwi_%d" % N1][:, :], wr=[itR])
                    S.dma("sp", g1[:], CT["g1_%d" % N1][:, :], wr=[itR])
                    kf = [sb(st, "kf%d" % i, [128, NCG, 2, N1], BF16) for i in range(2)]
                    kfR = [Res(), Res()]
                    yre = sb(st, "yre", [128, NCG, N1], BF16)
                    yim = sb(st, "yim", [128, NCG, N1], BF16)
                    yR = Res()
                    mt = [sb(st, "mt%d" % i, [128, 4, 512], F32) for i in range(2)]
                    mtR = [Res(), Res()]
                    bre = sb(st, "bre", [N1, NCG, 128], BF16)
                    bim = sb(st, "bim", [N1, NCG, 128], BF16)
                    bR = Res()
                    yo = [sb(st, "yo%d" % i, [H, NCG, 128], F32) for i in range(2)]
                    yoR = [Res(), Res()]
                    n_ = 0
                    for si, Ls in enumerate(SEQS):
                        if Ls != L:
                            continue
                        s0 = seq_off[si]
                        for c0 in range(0, 512, NCG):
                            i = n_ % 2
                            n_ += 1
                            S.dma("sp", kf[i][:], KF[L][:, c0:c0 + NCG, :, :], wr=[kfR[i]])
                            fft_fwd(st, L, U[c0:c0 + NCG, s0:s0 + L], c0, NCG, H, kfs=(kf[i], kfR[i]),
                                    dst_sb=(yre, yim, yR, mt, mtR), tabs=tabs)
                            pp, ppR = tabs[5], tabs[6]
                            for g in range(NCG // 2):
                                p, pR = psum()
                                pvw = p[0:N1, :].rearrange("p (c k) -> p c k", c=2)

                                def s1i(e):
                                    for c in range(2):
                                        cc_ = g * 2 + c
                                        e.matmul(pvw[:, c, :], yre[:, cc_, :], gg1, start=True, stop=False)
                                        ins = e.matmul(pvw[:, c, :], yim[:, cc_, :], gg2, start=False, stop=True)
                                    return ins
                                S.op("pe", s1i, rd=[yR, cbR], wr=[pR])
                                j = g % 2
                                p1 = pp[j][0:N1, 0, :].rearrange("p (c k) -> p c k", c=2)
                                p2 = pp[j][0:N1, 1, :].rearrange("p (c k) -> p c k", c=2)
                                S.op("dve", lambda e: e.tensor_tensor(out=p1, in0=pvw, in1=itr[:, None, :].to_broadcast([N1, 2, 256]),
                                                                      op=ALU.mult), rd=[pR, itR], wr=[ppR[j]])
                                S.op("dve", lambda e: e.tensor_tensor(out=p2, in0=pvw, in1=iti[:, None, :].to_broadcast([N1, 2, 256]),
                                                                      op=ALU.mult), rd=[pR, itR], wr=[ppR[j]])
                                cs = slice(g * 2, g * 2 + 2)
                                S.op("pool", lambda e: e.tensor_tensor(out=bre[:, cs, :], in0=p1[:, :, 0:128], in1=p2[:, :, 128:256],
                                                                       op=ALU.subtract), rd=[ppR[j]], wr=[bR])
                                S.op("pool", lambda e: e.tensor_tensor(out=bim[:, cs, :], in0=p2[:, :, 0:128], in1=p1[:, :, 128:256],
                                                                       op=ALU.add), rd=[ppR[j]], wr=[bR])
                            for g in range(NCG // 4):
                                cs = slice(g * 4, g * 4 + 4)
                                p, pR = psum()
                                br_ = bre[:, cs, :].rearrange("p c k -> p (c k)")
                                bi_ = bim[:, cs, :].rearrange("p c k -> p (c k)")
                                S.op("pe", lambda e: mm_acc(e, p[0:H, :], [(g1[:, 0:H], br_), (g1[:, H:2 * H], bi_)]),
                                     rd=[bR, itR], wr=[pR])
                                dst = yo[i][:, cs, :].rearrange("p c k -> p (c k)")
                                if g % 2 == 0:
                                    S.op("act", lambda e: e.activation(out=dst, in_=p[0:H, :], func=AF.Copy), rd=[pR], wr=[yoR[i]])
                                else:
                                    S.op("dve", lambda e: e.tensor_copy(out=dst, in_=p[0:H, :]), rd=[pR], wr=[yoR[i]])
                            S.dma("pool", YC[c0:c0 + NCG, s0:s0 + L].rearrange("c (n1 n2) -> n1 c n2", n2=128), yo[i][:], rd=[yoR[i]])
                    S.barrier()
        if sub('p3c'):
            with ExitStack() as st:
                yc = [sb(st, "ycs%d" % i, [128, TB], F32) for i in range(2)]
                ux = [sb(st, "uxs%d" % i, [128, TB], F32) for i in range(2)]
                x0 = [sb(st, "x0s%d" % i, [128, TB], F32) for i in range(2)]
                inR = [Res(), Res()]
                yb = [sb(st, "ybs%d" % i, [128, TB], BF16) for i in range(2)]
                ybR = [Res(), Res()]
                n_ = 0
                for t0 in range(0, NT, TB):
                    for cc in range(4):
                        i = n_ % 2
                        n_ += 1
                        S.dma("sp", yc[i][:], YC[cc * 128:(cc + 1) * 128, t0:t0 + TB], wr=[inR[i]])
                        S.dma("sp", ux[i][:], UX[cc, :, t0:t0 + TB], wr=[inR[i]])
                        S.dma("sp", x0[i][:], X0[cc, :, t0:t0 + TB], wr=[inR[i]])
                        S.op("dve", lambda e: e.tensor_tensor(out=yc[i][:], in0=yc[i][:], in1=x0[i][:], op=ALU.mult), rd=[inR[i]], wr=[inR[i]])
                        S.op("pool", lambda e: e.tensor_tensor(out=yb[i][:], in0=yc[i][:], in1=ux[i][:], op=ALU.add), rd=[inR[i]], wr=[ybR[i]])
                        S.dma("pool", MIX[4 + cc, :, t0:t0 + TB], yb[i][:], rd=[ybR[i]])
                S.barrier()

    def tok_major_proj(actT, actTR, nk, wt, wtR, j):
        res = []
        for h in range(2):
            p, pR = psum()
            S.op("pe", lambda e: mm_acc(e, p[:], [(actT[:, k, j * 128:(j + 1) * 128], wt[:, k, h * 512:(h + 1) * 512])
                                                  for k in range(nk)]), rd=[actTR, wtR], wr=[pR])
            res.append((p, pR))
        return [r[0] for r in res], [r[1] for r in res]

    def phase4a(l, SRC):
        with ExitStack() as st:
            wo, woR = load_w(st, "wout", W["w_out"][l], D, D)
            gam, gamR = load_gamma(st, "g4a", W["ln_mix_post"][l])
            xt = [sb(st, "xt%d" % i, [128, 4, D], F32) for i in range(2)]
            xtR = [Res() for _ in range(2)]
            mx = [sb(st, "mx%d" % i, [128, 8, 512], BF16) for i in range(2)]
            mxR = [Res() for _ in range(2)]
            ss = sb(st, "ss", [128, 8], F32)
            ssR = Res()
            junk = sb(st, "junk", [128, D], F32)
            junkR = Res()
            tmp = sb(st, "tmp", [128, D], F32)
            tmpR = Res()

            def ld(ti):
                b = ti % 2
                t0 = tiles[ti][0]
                S.dma("sp", xt[b][:], xview(SRC, t0), wr=[xtR[b]])
                S.dma("sp", mx[b][:], MIX[:, :, t0:t0 + 512].rearrange("c p t -> p c t"), wr=[mxR[b]])
            ld(0)
            for ti, (t0, si, p0) in enumerate(tiles):
                b = ti % 2
                if ti + 1 < len(tiles):
                    ld(ti + 1)
                for j in range(4):
                    ps2, ps2R = tok_major_proj(mx[b], mxR[b], 8, wo, woR, j)
                    postnorm_residual(ps2, ps2R, gam, gamR, xt[b][:, j, :], xtR[b], ss, ssR, junk, junkR, tmp, tmpR)
                S.dma("pool", xview(XR, t0), xt[b][:], rd=[xtR[b]])
            S.barrier()

    def phase4b(l):
        with ExitStack() as st:
            wq, wqR = load_w(st, "wq", W["wq_x"][l], D, D)
            wk, wkR = load_w(st, "wk", W["wk_x"][l], D, D)
            wv, wvR = load_w(st, "wv", W["wv_x"][l], D, D)
            wo, woR = load_w(st, "wo", W["wo_x"][l], D, D)
            gpre, gpreR = load_gamma(st, "gxpre", W["ln_x_pre"][l])
            gpost, gpostR = load_gamma(st, "gxpost", W["ln_x_post"][l])
            gmem, gmemR = load_gamma(st, "gmem", W["ln_mem"][l])
            xt = [sb(st, "xt%d" % i, [128, 4, D], F32) for i in range(2)]
            xtR = [Res() for _ in range(2)]
            xn = sb(st, "xn", [128, 4, D], BF16)
            xnR = [Res() for _ in range(4)]
            xnT = sb(st, "xnT", [128, 8, 512], BF16)
            xnTR = Res()
            qxT = sb(st, "qxT", [128, 8, 512], BF16)
            qxTR = [Res() for _ in range(8)]
            oT = sb(st, "oT", [128, 8, 512], BF16)
            oTR = Res()
            kxT = sb(st, "kxT", [128, 8, NMEM], BF16)
            kxTR = Res()
            vx = sb(st, "vx", [128, 2, D], BF16)
            vxR = Res()
            mt_ = sb(st, "memt", [128, 2, D], F32)
            mtR_ = Res()
            ss = sb(st, "ss", [128, 8], F32)
            ssR = Res()
            junk = sb(st, "junk", [128, D], F32)
            junkR = Res()
            tmp = sb(st, "tmp", [128, D], F32)
            tmpR = Res()
            esb = [sb(st, "esb%d" % i, [128, 512], BF16) for i in range(4)]
            esbR = [Res() for _ in range(4)]
            rz = sb(st, "rz", [128, 512], F32)
            rzR = Res()
            en = 0
            cur = -1
            S.dma("sp", xt[0][:], xview(XR, tiles[0][0]), wr=[xtR[0]])
            for ti, (t0, si, p0) in enumerate(tiles):
                b = ti % 2
                if ti + 1 < len(tiles):
                    S.dma("sp", xt[1 - b][:], xview(XR, tiles[ti + 1][0]), wr=[xtR[1 - b]])
                if si != cur:
                    cur = si
                    S.dma("sp", mt_[:], MEM[si * NMEM:(si + 1) * NMEM, :].rearrange("(j p) d -> p j d", p=128), wr=[mtR_])
                    norm_transpose(mt_, mtR_, 2, gmem, gmemR, xn, xnR, xnT, xnTR, ss, ssR, junk, junkR)
                    for fc in range(8):
                        p, pR = psum()
                        S.op("pe", lambda e: mm_acc(e, p[:, 0:NMEM], [(wk[:, k, fc * 128:(fc + 1) * 128], xnT[:, k, 0:NMEM])
                                                                      for k in range(8)]), rd=[wkR, xnTR], wr=[pR])
                        S.op("act", lambda e: e.activation(out=kxT[:, fc, :], in_=p[:, 0:NMEM], func=AF.Copy), rd=[pR], wr=[kxTR])
                    for mc in range(2):
                        for h in range(2):
                            p, pR = psum()
                            S.op("pe", lambda e: mm_acc(e, p[:], [(xnT[:, k, mc * 128:(mc + 1) * 128], wv[:, k, h * 512:(h + 1) * 512])
                                                                  for k in range(8)]), rd=[wvR, xnTR], wr=[pR])
                            S.op("dve", lambda e: e.tensor_copy(out=vx[:, mc, h * 512:(h + 1) * 512], in_=p[:]), rd=[pR], wr=[vxR])
                norm_transpose(xt[b], xtR[b], 4, gpre, gpreR, xn, xnR, xnT, xnTR, ss, ssR, junk, junkR)
                for fc in range(8):
                    p, pR = psum()
                    S.op("pe", lambda e: mm_acc(e, p[:], [(wq[:, k, fc * 128:(fc + 1) * 128], xnT[:, k, :]) for k in range(8)]),
                         rd=[wqR, xnTR], wr=[pR])
                    if fc % 2 == 0:
                        S.op("act", lambda e: e.activation(out=qxT[:, fc, :], in_=p[:], func=AF.Copy), rd=[pR], wr=[qxTR[fc]])
                    else:
                        S.op("dve", lambda e: e.tensor_copy(out=qxT[:, fc, :], in_=p[:]), rd=[pR], wr=[qxTR[fc]])
                for hx in range(4):
                    ee = []
                    for mc in range(2):
                        p, pR = psum()
                        S.op("pe", lambda e: mm_acc(e, p[:], [(kxT[:, 2 * hx + q, mc * 128:(mc + 1) * 128], qxT[:, 2 * hx + q, :])
                                                              for q in range(2)]), rd=[kxTR, qxTR[2 * hx], qxTR[2 * hx + 1]], wr=[pR])
                        ei = en % 4
                        en += 1
                        S.op("act", lambda e: e.activation(out=esb[ei][:], in_=p[:], func=AF.Exp, scale=1.0 / 16), rd=[pR], wr=[esbR[ei]])
                        ee.append(ei)
                    pz, pzR = psum()
                    S.op("pe", lambda e: mm_acc(e, pz[:], [(ones, esb[ee[0]][:]), (ones, esb[ee[1]][:])]),
                         rd=[cbR, esbR[ee[0]], esbR[ee[1]]], wr=[pzR])
                    S.op("dve", lambda e: e.reciprocal(out=rz[:], in_=pz[:]), rd=[pzR], wr=[rzR])
                    for q in range(2):
                        fc = 2 * hx + q
                        p, pR = psum()
                        S.op("pe", lambda e: mm_acc(e, p[:], [(vx[:, mc, fc * 128:(fc + 1) * 128], esb[ee[mc]][:]) for mc in range(2)]),
                             rd=[vxR, esbR[ee[0]], esbR[ee[1]]], wr=[pR])
                        S.op("dve", lambda e: e.tensor_tensor(out=oT[:, fc, :], in0=p[:], in1=rz[:], op=ALU.mult), rd=[pR, rzR], wr=[oTR])
                for j in range(4):
                    ps2, ps2R = tok_major_proj(oT, oTR, 8, wo, woR, j)
                    postnorm_residual(ps2, ps2R, gpost, gpostR, xt[b][:, j, :], xtR[b], ss, ssR, junk, junkR, tmp, tmpR)
                S.dma("pool", xview(XR, t0), xt[b][:], rd=[xtR[b]])
            S.barrier()

    def phase5(l):
        with ExitStack() as st:
            wg, wgR = load_w(st, "wg", W["w_gate"][l], D, DFF)
            wu, wuR = load_w(st, "wu", W["w_up"][l], D, DFF)
            gam, gamR = load_gamma(st, "g5", W["ln_ffn_pre"][l])
            xt = [sb(st, "xt%d" % i, [128, 4, D], F32) for i in range(2)]
            xtR = [Res() for _ in range(2)]
            xn = sb(st, "xn", [128, 4, D], BF16)
            xnR = [Res() for _ in range(4)]
            xnT = sb(st, "xnT", [128, 8, 512], BF16)
            xnTR = Res()
            ss = sb(st, "ss", [128, 8], F32)
            ssR = Res()
            junk = sb(st, "junk", [128, D], F32)
            junkR = Res()
            sg = [sb(st, "sg%d" % i, [128, 512], F32) for i in range(2)]
            sgR = [Res(), Res()]
            hb = [sb(st, "hb%d" % i, [128, 512], BF16) for i in range(3)]
            hbR = [Res() for _ in range(3)]
            n_ = 0
            S.dma("sp", xt[0][:], xview(XR, tiles[0][0]), wr=[xtR[0]])
            for ti, (t0, si, p0) in enumerate(tiles):
                b = ti % 2
                if ti + 1 < len(tiles):
                    S.dma("sp", xt[1 - b][:], xview(XR, tiles[ti + 1][0]), wr=[xtR[1 - b]])
                norm_transpose(xt[b], xtR[b], 4, gam, gamR, xn, xnR, xnT, xnTR, ss, ssR, junk, junkR)
                for f in range(NFF):
                    pg, pgR = psum()
                    pu, puR = psum()
                    S.op("pe", lambda e: mm_acc(e, pg[:], [(wg[:, k, f * 128:(f + 1) * 128], xnT[:, k, :]) for k in range(8)]),
                         rd=[wgR, xnTR], wr=[pgR])
                    S.op("pe", lambda e: mm_acc(e, pu[:], [(wu[:, k, f * 128:(f + 1) * 128], xnT[:, k, :]) for k in range(8)]),
                         rd=[wuR, xnTR], wr=[puR])
                    i = n_ % 2
                    o = n_ % 3
                    n_ += 1
                    S.op("act", lambda e: e.activation(out=sg[i][:], in_=pg[:], func=AF.Silu), rd=[pgR], wr=[sgR[i]])
                    S.op("dve", lambda e: e.tensor_tensor(out=hb[o][:], in0=pu[:], in1=sg[i][:], op=ALU.mult), rd=[puR, sgR[i]], wr=[hbR[o]])
                    S.dma("pool", HT[f, :, t0:t0 + 512], hb[o][:], rd=[hbR[o]])
            S.barrier()

    def phase6(l, DST):
        with ExitStack() as st:
            wd, wdR = load_w(st, "wd", W["w_down"][l], DFF, D)
            gam, gamR = load_gamma(st, "g6", W["ln_ffn_post"][l])
            xt = [sb(st, "xt%d" % i, [128, 4, D], F32) for i in range(2)]
            xtR = [Res() for _ in range(2)]
            hT = [sb(st, "hT%d" % i, [128, NFF, 512], BF16) for i in range(2)]
            hTR = [Res() for _ in range(2)]
            ss = sb(st, "ss", [128, 8], F32)
            ssR = Res()
            junk = sb(st, "junk", [128, D], F32)
            junkR = Res()
            tmp = sb(st, "tmp", [128, D], F32)
            tmpR = Res()

            def ld(ti):
                b = ti % 2
                t0 = tiles[ti][0]
                S.dma("sp", xt[b][:], xview(XR, t0), wr=[xtR[b]])
                S.dma("sp", hT[b][:], HT[:, :, t0:t0 + 512].rearrange("c p t -> p c t"), wr=[hTR[b]])
            ld(0)
            for ti, (t0, si, p0) in enumerate(tiles):
                b = ti % 2
                if ti + 1 < len(tiles):
                    ld(ti + 1)
                for j in range(4):
                    ps2, ps2R = tok_major_proj(hT[b], hTR[b], NFF, wd, wdR, j)
                    postnorm_residual(ps2, ps2R, gam, gamR, xt[b][:, j, :], xtR[b], ss, ssR, junk, junkR, tmp, tmpR)
                S.dma("pool", xview(DST, t0), xt[b][:], rd=[xtR[b]])
            S.barrier()

    S.barrier()
    for l in range(depth):
        src = X if l == 0 else XR
        for nm, fn in (("p1", lambda: phase1(l, src)), ("p2", lambda: phase2(l)), ("p3", lambda: phase3(l)),
                       ("p4a", lambda: phase4a(l, src)), ("p4b", lambda: phase4b(l)), ("p5", lambda: phase5(l)),
                       ("p6", lambda: phase6(l, Y if l == depth - 1 else XR))):
            if only is None or nm in only or (nm == 'p3' and any(k.startswith('p3') for k in only)):
                fn()
    S.barrier()
    es.close()
    return nc, consts


_CACHE = {}


def _core_inputs(core, x_prompt, x_sample, mem_prompt, mem_sample, weights, consts):
    xs = x_sample[core].reshape(-1, D)
    xp = x_prompt[4 * core:4 * core + 4].reshape(-1, D)
    m = {"x": np.ascontiguousarray(np.concatenate([xs, xp], 0)),
         "mem": np.ascontiguousarray(np.concatenate([mem_sample[core], mem_prompt[4 * core:4 * core + 4].reshape(-1, D)], 0))}
    for n, _ in WSPEC:
        m[n] = weights[n]
    for k, v in consts.items():
        m["c_" + k] = v
    return m


def kernel(**inputs):
    inputs = {k: np.asarray(v) for k, v in inputs.items()}
    SEQS = [8192, 2048, 2048, 2048, 2048]
    if "nc" not in _CACHE:
        _CACHE["nc"] = build(SEQS, DEPTH)
    nc, consts = _CACHE["nc"]
    weights = {n: np.ascontiguousarray(inputs[n], dtype=np.float32) for n, _ in WSPEC}
    in_maps = [_core_inputs(c, inputs["x_prompt"], inputs["x_sample"], inputs["mem_prompt"], inputs["mem_sample"],
                            weights, consts) for c in range(8)]
    res = run_bass_kernel_spmd(nc, in_maps, core_ids=list(range(8)))
    y_prompt = np.empty((32, 2048, D), np.float32)
    y_sample = np.empty((8, 8192, D), np.float32)
    for c in range(8):
        y = res.results[c]["y"]
        y_sample[c] = y[:8192]
        y_prompt[4 * c:4 * c + 4] = y[8192:].reshape(4, 2048, D)
    return (y_prompt, y_sample)
```

```python
import math
from contextlib import ExitStack
import numpy as np
import ml_dtypes
import concourse.bass as bass
import concourse.mybir as mybir
from concourse.bass_utils import run_bass_kernel_spmd

F32, BF16 = mybir.dt.float32, mybir.dt.bfloat16
AF = mybir.ActivationFunctionType
ALU = mybir.AluOpType
AX = mybir.AxisListType

D = 1024
DEPTH = 4
DFF = 2816
NFF = 22
NMEM = 256
EPS = 1e-6
BF = ml_dtypes.bfloat16


class Res:
    __slots__ = ("w", "r")

    def __init__(self):
        self.w = None
        self.r = {}


class Sched:
    NDS = 12

    def __init__(self, nc, es):
        self.nc = nc
        self.eng = {"pe": nc.tensor, "act": nc.scalar, "dve": nc.vector, "pool": nc.gpsimd, "sp": nc.sync}
        self.sem = {}
        self.cnt = {}
        self.waited = {}
        for k in self.eng:
            self.sem[k] = es.enter_context(nc.semaphore("s_" + k))
            self.cnt[k] = 0
            self.waited[k] = {}
        self.dq = {}
        for q in ("sp", "pool", "act"):
            sems = []
            for i in range(self.NDS):
                key = ("d", q, i)
                self.sem[key] = es.enter_context(nc.semaphore("d_%s_%d" % (q, i)))
                self.cnt[key] = 0
                sems.append(key)
            self.dq[q] = [sems, 0]

    def _deps(self, rd, wr):
        deps = {}

        def add(ev):
            if ev is not None and deps.get(ev[0], 0) < ev[1]:
                deps[ev[0]] = ev[1]

        for r in rd:
            add(r.w)
        for r in wr:
            add(r.w)
            for k, v in r.r.items():
                add((k, v))
        return deps

    def _wait(self, e, deps):
        w = self.waited[e]
        for k, v in deps.items():
            if k == e and e == "pe":
                continue
            if w.get(k, 0) < v:
                self.eng[e].wait_ge(self.sem[k], v)
                w[k] = v

    def _mark(self, ev, rd, wr):
        for r in rd:
            if r.r.get(ev[0], 0) < ev[1]:
                r.r[ev[0]] = ev[1]
        for r in wr:
            r.w = ev
            r.r = {}

    def op(self, e, fn, rd=(), wr=()):
        self._wait(e, self._deps(rd, wr))
        ins = fn(self.eng[e])
        self.cnt[e] += 1
        ins.then_inc(self.sem[e], 1)
        self._mark((e, self.cnt[e]), rd, wr)

    def dma(self, q, out, in_, rd=(), wr=()):
        sems, n = self.dq[q]
        key = sems[n % self.NDS]
        self.dq[q][1] = n + 1
        deps = self._deps(rd, wr)
        if self.cnt[key] > 0 and deps.get(key, 0) < self.cnt[key]:
            deps[key] = self.cnt[key]
        self._wait(q, deps)
        self.eng[q].dma_start(out=out, in_=in_).then_inc(self.sem[key], 16)
        self.cnt[key] += 16
        self._mark((key, self.cnt[key]), rd, wr)

    def barrier(self, engines=("pe", "act", "dve", "pool", "sp")):
        allv = {k: v for k, v in self.cnt.items() if v > 0}
        for e in engines:
            self._wait(e, allv)


def _rope_tables():
    inv = (np.float32(500000.0) ** (-np.arange(0, 16, 2, dtype=np.float32) / np.float32(16))).astype(np.float32)
    pos = np.arange(8192, dtype=np.float32)
    ang = (pos[:, None] * inv[None, :]).astype(np.float32)
    cos, sin = np.cos(ang).astype(np.float32), np.sin(ang).astype(np.float32)
    C = np.ones((128, 8192), np.float32)
    S = np.zeros((128, 8192), np.float32)
    R = np.zeros((128, 128), np.float32)
    for p in range(128):
        d = p % 64
        if d < 8:
            C[p] = cos[:, d]
            S[p] = -sin[:, d]
            R[p + 8, p] = 1
        elif d < 16:
            C[p] = cos[:, d - 8]
            S[p] = sin[:, d - 8]
            R[p - 8, p] = 1
    return C, S, R


def _fft_tables(N1):
    N = 128 * N1
    H = N1 // 2
    t = {}
    n1 = np.arange(N1)[:, None]
    k1 = np.arange(N1)[None, :]
    a = -2 * np.pi * n1 * k1 / N1
    t["f1"] = np.concatenate([np.cos(a), np.sin(a)], 1).astype(BF)
    n2 = np.arange(128)[:, None]
    a = -2 * np.pi * n2 * k1 / N
    t["twr"] = np.concatenate([np.cos(a), np.cos(a)], 1).astype(np.float32)
    t["twi"] = np.concatenate([np.sin(a), np.sin(a)], 1).astype(np.float32)
    a = 2 * np.pi * np.arange(N1)[:, None] * np.arange(128)[None, :] / N
    t["itwr"] = np.concatenate([np.cos(a), np.cos(a)], 1).astype(np.float32)
    t["itwi"] = np.concatenate([np.sin(a), np.sin(a)], 1).astype(np.float32)
    a = 2 * np.pi * np.arange(N1)[:, None] * np.arange(H)[None, :] / N1
    t["g1"] = np.concatenate([np.cos(a) / N, -np.sin(a) / N], 1).astype(BF)
    return t


def _fft128_tables():
    a = -2 * np.pi * np.arange(128)[:, None] * np.arange(128)[None, :] / 128
    f2 = np.concatenate([np.cos(a), np.sin(a), -np.sin(a)], 1).astype(BF)
    b = -a
    gg = np.concatenate([np.cos(b), np.sin(b), -np.sin(b), np.cos(b)], 1).astype(BF)
    return f2, gg


def _filter_tables(L):
    t = np.linspace(0.0, 1.0, L, dtype=np.float32)
    w = (np.float32(2.0 * math.pi) * np.arange(L, dtype=np.float32) / np.float32(L)).astype(np.float32)
    f = np.linspace(1e-4, 15, 16, dtype=np.float32)
    fw = (f[None, :] * w[:, None]).astype(np.float32)
    z = np.concatenate([t[:, None], np.cos(fw), -np.sin(fw)], 1).astype(np.float32)
    idx = np.concatenate([np.arange(L), [0], np.arange(L - 1, 0, -1)])
    zt = np.ascontiguousarray(z[idx].T)
    tn = t[idx].copy()
    tn[L] = 1e4
    return zt, tn[None, :].astype(np.float32)


def _consts():
    c = {}
    C, S, R = _rope_tables()
    c["ropec"], c["ropes"] = C, S
    f2, gg = _fft128_tables()
    ident = np.eye(128, dtype=np.float32)
    c["cb"] = np.concatenate([ident.astype(BF), np.ones((128, 128), BF), R.astype(BF), f2, gg], 1)
    for N1 in (32, 128):
        for k, v in _fft_tables(N1).items():
            c["%s_%d" % (k, N1)] = v
    g1 = c["g1_32"].astype(np.float32)
    g1p = np.zeros((128, 4, g1.shape[1]), np.float32)
    for c4 in range(4):
        g1p[c4 * 32:(c4 + 1) * 32, c4, :] = g1
    c["g1p_32"] = g1p.astype(BF)
    c["itwr4_32"] = np.tile(c["itwr_32"], (4, 1))
    c["itwi4_32"] = np.tile(c["itwi_32"], (4, 1))
    for L in (2048, 8192):
        zt, tn = _filter_tables(L)
        c["zt_%d" % L] = zt
        c["tn_%d" % L] = tn
    mn, mx = math.log(1e-2) / 1.5, math.log(1e-2) / 0.3
    c["negdelta"] = (-np.abs(np.linspace(mn, mx, 512, dtype=np.float32)))[None, :].astype(np.float32)
    return c


WSPEC = [("ln_mix_pre", (D,)), ("ln_mix_post", (D,)), ("w_in", (D, 3072)), ("lambda_q1", (64,)), ("lambda_k1", (64,)),
         ("lambda_q2", (64,)), ("lambda_k2", (64,)), ("subln_g", (128,)), ("conv_w", (3, 1536)), ("conv_b", (1536,)),
         ("filt_w1", (33, 64)), ("filt_b1", (64,)), ("filt_freq1", (64,)), ("filt_w2", (64, 64)), ("filt_b2", (64,)),
         ("filt_freq2", (64,)), ("filt_w3", (64, 1024)), ("hyena_d", (512,)), ("w_out", (D, D)), ("ln_x_pre", (D,)),
         ("ln_x_post", (D,)), ("ln_mem", (D,)), ("wq_x", (D, D)), ("wk_x", (D, D)), ("wv_x", (D, D)), ("wo_x", (D, D)),
         ("ln_ffn_pre", (D,)), ("ln_ffn_post", (D,)), ("w_gate", (D, DFF)), ("w_up", (D, DFF)), ("w_down", (DFF, D))]


def build(SEQS, depth=DEPTH, dbg=(), only=None):
    NT = sum(SEQS)
    NS = len(SEQS)
    LT = sorted(set(SEQS))
    consts = _consts()
    nc = bass.Bass("TRN2", target_bir_lowering=False)

    def din(name, shape, dt=F32):
        return nc.dram_tensor(name, list(shape), dt, kind="ExternalInput").ap()

    def dscr(name, shape, dt, out=False):
        kind = "ExternalOutput" if (out or name in dbg) else "Internal"
        return nc.dram_tensor(name, list(shape), dt, kind=kind).ap()

    X = din("x", [NT, D])
    MEM = din("mem", [NS * NMEM, D])
    W = {n: din(n, (depth,) + s) for n, s in WSPEC}
    CT = {}
    for k, v in consts.items():
        CT[k] = din("c_" + k, v.shape, BF16 if v.dtype == BF else F32)
    Y = dscr("y", [NT, D], F32, out=True)
    XR = dscr("xr", [NT, D], F32)
    QT = dscr("qt", [4, 128, NT], BF16)
    KT = dscr("kt", [4, 128, NT], BF16)
    V = dscr("v", [NT, 512], BF16)
    HY = dscr("hy", [12, 128, NT], F32)
    MIX = dscr("mix", [8, 128, NT], BF16)
    U = dscr("u", [512, NT], BF16)
    UX = dscr("ux", [4, 128, NT], F32)
    X0 = dscr("x0", [4, 128, NT], F32)
    YC = dscr("yc", [512, NT], F32)
    HT = dscr("ht", [NFF, 128, NT], BF16)
    KERN = {L: dscr("kern%d" % L, [512, 2 * L], BF16) for L in LT}
    KF = {L: dscr("kf%d" % L, [128, 512, 2, L // 64], BF16) for L in LT}

    es = ExitStack()
    S = Sched(nc, es)
    PSALL = es.enter_context(nc.psum_tensor("psall", [128, 4096], F32))
    PS = [PSALL[:, i * 512:(i + 1) * 512] for i in range(8)]
    PSR = [Res() for _ in range(8)]
    psn = [0]

    def psum():
        i = psn[0] % 8
        psn[0] += 1
        return PS[i], PSR[i]

    uid = [0]

    def sb(st, name, shape, dt):
        uid[0] += 1
        return st.enter_context(nc.sbuf_tensor("%s_%d" % (name, uid[0]), list(shape), dt))

    cb = sb(es, "cb", [128, 1280], BF16)
    cbR = Res()
    S.dma("sp", cb[:], CT["cb"][:, :], wr=[cbR])
    ident, ones, rmat = cb[:, 0:128], cb[:, 128:256], cb[:, 256:384]
    f2re, f2im, f2imn = cb[:, 384:512], cb[:, 512:640], cb[:, 640:768]
    gg1, gg2 = cb[:, 768:1024], cb[:, 1024:1280]
    rnorm = {L: sb(es, "rnorm%d" % L, [128, 4], F32) for L in LT}
    rnormR = {L: Res() for L in LT}
    nlam = sb(es, "nlam", [128, 1], F32)
    nlamR = Res()
    epsc = sb(es, "epsc", [128, 1], F32)
    S.op("pool", lambda e: e.memset(epsc[:], EPS), wr=[cbR])

    def bcast_row(ap1d, n):
        return ap1d.partition_broadcast(128)

    def col(ap1d):
        return ap1d.rearrange("(p o) -> p o", o=1)

    def load_w(st, name, ap2d, rows, cols, q="pool"):
        kc = rows // 128
        t = sb(st, name, [128, kc, cols], BF16)
        r = Res()
        for k in range(kc):
            S.dma(q, t[:, k, :], ap2d[k * 128:(k + 1) * 128, :], wr=[r])
        return t, r

    def load_gamma(st, name, ap1d):
        t = sb(st, name, [128, D], F32)
        r = Res()
        S.dma("sp", t[:], bcast_row(ap1d, D), wr=[r])
        return t, r

    def rms_rstd(st_small, src_ap, rd, n, ss, ssR, junk, junkR, idx):
        S.op("act", lambda e: e.activation(out=junk[:], in_=src_ap, func=AF.Square, accum_out=ss[:, idx:idx + 1]),
             rd=rd, wr=[junkR, ssR])

    def finish_rstd(ss, ssR, n, width):
        S.op("dve", lambda e: e.tensor_scalar(out=ss[:, 0:width], in0=ss[:, 0:width], scalar1=1.0 / n, scalar2=EPS,
                                              op0=ALU.mult, op1=ALU.add), rd=[ssR], wr=[ssR])
        S.op("act", lambda e: e.activation(out=ss[:, 0:width], in_=ss[:, 0:width], func=AF.Sqrt), rd=[ssR], wr=[ssR])
        S.op("dve", lambda e: e.reciprocal(out=ss[:, 0:width], in_=ss[:, 0:width]), rd=[ssR], wr=[ssR])

    def norm_transpose(xt, xtR, nj, gam, gamR, xn, xnR, xnT, xnTR, ss, ssR, junk, junkR):
        for j in range(nj):
            rms_rstd(None, xt[:, j, :], [xtR], D, ss, ssR, junk, junkR, j)
        finish_rstd(ss, ssR, D, nj)
        for j in range(nj):
            S.op("dve", lambda e: e.scalar_tensor_tensor(out=xn[:, j, :], in0=xt[:, j, :], scalar=ss[:, j:j + 1],
                                                         in1=gam[:], op0=ALU.mult, op1=ALU.mult),
                 rd=[xtR, ssR, gamR], wr=[xnR[j]])
        for j in range(nj):
            p, pR = psum()
            pb = p[:].bitcast(BF16)

            def tr(e):
                for c in range(8):
                    ins = e.transpose(pb[:, c * 128:(c + 1) * 128], xn[:, j, c * 128:(c + 1) * 128], ident)
                return ins
            S.op("pe", tr, rd=[xnR[j], cbR], wr=[pR])
            eng = "act" if j % 2 == 0 else "dve"
            src = pb.rearrange("p (c t) -> p c t", c=8)
            if eng == "act":
                S.op("act", lambda e: e.activation(out=xnT[:, :, j * 128:(j + 1) * 128], in_=src, func=AF.Copy),
                     rd=[pR], wr=[xnTR])
            else:
                S.op("dve", lambda e: e.tensor_copy(out=xnT[:, :, j * 128:(j + 1) * 128], in_=src), rd=[pR], wr=[xnTR])

    def mm_acc(e, out, pairs):
        n = len(pairs)
        for i, (l, r) in enumerate(pairs):
            ins = e.matmul(out, l, r, start=(i == 0), stop=(i == n - 1))
        return ins

    def postnorm_residual(ps2, ps2R, gam, gamR, xt_j, xtR, ss, ssR, junk, junkR, tmp, tmpR):
        for h in range(2):
            S.op("act", lambda e: e.activation(out=junk[:, 0:512], in_=ps2[h][:], func=AF.Square,
                                               accum_out=ss[:, h:h + 1]), rd=[ps2R[h]], wr=[junkR, ssR])
        S.op("dve", lambda e: e.tensor_tensor(out=ss[:, 2:3], in0=ss[:, 0:1], in1=ss[:, 1:2], op=ALU.add),
             rd=[ssR], wr=[ssR])
        S.op("dve", lambda e: e.tensor_scalar(out=ss[:, 2:3], in0=ss[:, 2:3], scalar1=1.0 / D, scalar2=EPS,
                                              op0=ALU.mult, op1=ALU.add), rd=[ssR], wr=[ssR])
        S.op("act", lambda e: e.activation(out=ss[:, 2:3], in_=ss[:, 2:3], func=AF.Sqrt), rd=[ssR], wr=[ssR])
        S.op("dve", lambda e: e.reciprocal(out=ss[:, 2:3], in_=ss[:, 2:3]), rd=[ssR], wr=[ssR])
        for h in range(2):
            S.op("dve", lambda e: e.scalar_tensor_tensor(out=tmp[:, h * 512:(h + 1) * 512], in0=ps2[h][:],
                                                         scalar=ss[:, 2:3], in1=gam[:, h * 512:(h + 1) * 512],
                                                         op0=ALU.mult, op1=ALU.mult),
                 rd=[ps2R[h], ssR, gamR], wr=[tmpR])
        S.op("pool", lambda e: e.tensor_tensor(out=xt_j, in0=xt_j, in1=tmp[:], op=ALU.add), rd=[tmpR, xtR], wr=[xtR])

    seq_off = [sum(SEQS[:i]) for i in range(NS)]
    tiles = []
    for si, L in enumerate(SEQS):
        for t in range(L // 512):
            tiles.append((seq_off[si] + t * 512, si, t * 512))

    def xview(ap, t0):
        return ap[t0:t0 + 512, :].rearrange("(j p) d -> p j d", p=128)

    def phase1(l, SRC):
        with ExitStack() as st:
            win, winR = load_w(st, "win", W["w_in"][l], D, 3072)
            gam, gamR = load_gamma(st, "g1", W["ln_mix_pre"][l])
            xt = [sb(st, "xt%d" % i, [128, 4, D], F32) for i in range(2)]
            xtR = [Res() for _ in range(2)]
            xn2 = [sb(st, "xn%d" % i, [128, 4, D], BF16) for i in range(2)]
            xnR2 = [[Res() for _ in range(4)] for i in range(2)]
            xnT2 = [sb(st, "xnT%d" % i, [128, 8, 512], BF16) for i in range(2)]
            xnTR2 = [Res(), Res()]
            ss = sb(st, "ss", [128, 8], F32)
            ssR = Res()
            junk = sb(st, "junk", [128, D], F32)
            junkR = Res()
            rc = [sb(st, "rc%d" % i, [128, 2, 512], F32) for i in range(2)]
            rcR = [Res() for _ in range(2)]
            qsb = [sb(st, "qsb%d" % i, [128, 512], BF16) for i in range(2)]
            qsbR = [Res() for _ in range(2)]
            t1 = [sb(st, "t1%d" % i, [128, 512], F32) for i in range(2)]
            t1R = [Res() for _ in range(2)]
            t2 = [sb(st, "t2%d" % i, [128, 512], F32) for i in range(2)]
            t2R = [Res() for _ in range(2)]
            qr = [sb(st, "qr%d" % i, [128, 512], BF16) for i in range(3)]
            qrR = [Res() for _ in range(3)]
            hs = [sb(st, "hs%d" % i, [128, 512], F32) for i in range(3)]
            hsR = [Res() for _ in range(3)]
            cn = [0, 0, 0]
            S.dma("sp", xt[0][:], xview(SRC, tiles[0][0]), wr=[xtR[0]])
            norm_transpose(xt[0], xtR[0], 4, gam, gamR, xn2[0], xnR2[0], xnT2[0], xnTR2[0], ss, ssR, junk, junkR)
            for ti, (t0, si, p0) in enumerate(tiles):
                b = ti % 2
                xnT, xnTR = xnT2[b], xnTR2[b]
                if ti + 1 < len(tiles):
                    S.dma("sp", xt[1 - b][:], xview(SRC, tiles[ti + 1][0]), wr=[xtR[1 - b]])
                S.dma("sp", rc[b][:, 0, :], CT["ropec"][:, p0:p0 + 512], wr=[rcR[b]])
                S.dma("sp", rc[b][:, 1, :], CT["ropes"][:, p0:p0 + 512], wr=[rcR[b]])
                for ch in range(8):
                    p, pR = psum()
                    S.op("pe", lambda e: mm_acc(e, p[:], [(win[:, k, ch * 128:(ch + 1) * 128], xnT[:, k, :])
                                                          for k in range(8)]), rd=[winR, xnTR], wr=[pR])
                    i = cn[0] % 2
                    cn[0] += 1
                    S.op("act", lambda e: e.activation(out=qsb[i][:], in_=p[:], func=AF.Copy), rd=[pR], wr=[qsbR[i]])
                    p2, p2R = psum()
                    S.op("pe", lambda e: e.matmul(p2[:], rmat, qsb[i][:], start=True, stop=True),
                         rd=[qsbR[i], cbR], wr=[p2R])
                    S.op("dve", lambda e: e.tensor_tensor(out=t1[i][:], in0=qsb[i][:], in1=rc[b][:, 0, :], op=ALU.mult),
                         rd=[qsbR[i], rcR[b]], wr=[t1R[i]])
                    S.op("dve", lambda e: e.tensor_tensor(out=t2[i][:], in0=p2[:], in1=rc[b][:, 1, :], op=ALU.mult),
                         rd=[p2R, rcR[b]], wr=[t2R[i]])
                    o = cn[1] % 3
                    cn[1] += 1
                    S.op("pool", lambda e: e.tensor_tensor(out=qr[o][:], in0=t1[i][:], in1=t2[i][:], op=ALU.add),
                         rd=[t1R[i], t2R[i]], wr=[qrR[o]])
                    dst = (QT if ch < 4 else KT)[ch % 4, :, t0:t0 + 512]
                    S.dma("pool", dst, qr[o][:], rd=[qrR[o]])
                if ti + 1 < len(tiles):
                    norm_transpose(xt[1 - b], xtR[1 - b], 4, gam, gamR, xn2[1 - b], xnR2[1 - b], xnT2[1 - b], xnTR2[1 - b],
                                   ss, ssR, junk, junkR)
                for j in range(4):
                    p, pR = psum()
                    S.op("pe", lambda e: mm_acc(e, p[:], [(xnT[:, k, j * 128:(j + 1) * 128], win[:, k, 1024:1536])
                                                          for k in range(8)]), rd=[winR, xnTR], wr=[pR])
                    o = cn[1] % 3
                    cn[1] += 1
                    S.op("act", lambda e: e.activation(out=qr[o][:], in_=p[:], func=AF.Copy), rd=[pR], wr=[qrR[o]])
                    S.dma("pool", V[t0 + j * 128:t0 + (j + 1) * 128, :], qr[o][:], rd=[qrR[o]])
                for ch in range(12):
                    p, pR = psum()
                    c0 = 1536 + ch * 128
                    S.op("pe", lambda e: mm_acc(e, p[:], [(win[:, k, c0:c0 + 128], xnT[:, k, :]) for k in range(8)]),
                         rd=[winR, xnTR], wr=[pR])
                    o = cn[2] % 3
                    cn[2] += 1
                    if ch % 2 == 0:
                        S.op("act", lambda e: e.activation(out=hs[o][:], in_=p[:], func=AF.Copy), rd=[pR], wr=[hsR[o]])
                    else:
                        S.op("dve", lambda e: e.tensor_copy(out=hs[o][:], in_=p[:]), rd=[pR], wr=[hsR[o]])
                    S.dma("pool", HY[ch, :, t0:t0 + 512], hs[o][:], rd=[hsR[o]])
            S.barrier()

    def lam_compute(l):
        lam_init = 0.8 - 0.6 * math.exp(-0.3 * l)
        with ExitStack() as st:
            lt = sb(st, "lamt", [128, 4, 64], F32)
            ltR = Res()
            for i, n in enumerate(("lambda_q1", "lambda_k1", "lambda_q2", "lambda_k2")):
                S.dma("sp", lt[:, i, :], bcast_row(W[n][l], 64), wr=[ltR])
            pr = sb(st, "lampr", [128, 2, 64], F32)
            prR = Res()
            sm = sb(st, "lamsm", [128, 2], F32)
            smR = Res()
            for i in range(2):
                S.op("dve", lambda e: e.tensor_tensor(out=pr[:, i, :], in0=lt[:, 2 * i, :], in1=lt[:, 2 * i + 1, :],
                                                      op=ALU.mult), rd=[ltR], wr=[prR])
            S.op("dve", lambda e: e.tensor_reduce(out=sm[:], in_=pr[:], axis=AX.X, op=ALU.add), rd=[prR], wr=[smR])
            S.op("act", lambda e: e.activation(out=sm[:], in_=sm[:], func=AF.Exp), rd=[smR], wr=[smR])
            S.op("dve", lambda e: e.scalar_tensor_tensor(out=nlam[:], in0=sm[:, 1:2], scalar=-lam_init, in1=sm[:, 0:1],
                                                         op0=ALU.add, op1=ALU.subtract), rd=[smR], wr=[nlamR])
            S.barrier()
        return lam_init

    def phase2(l):
        lam_init = lam_compute(l)
        with ExitStack() as st:
            gs = sb(st, "gsub", [128, 1], F32)
            gsR = Res()
            S.dma("sp", gs[:], col(W["subln_g"][l]), wr=[gsR])
            S.op("dve", lambda e: e.tensor_scalar(out=gs[:], in0=gs[:], scalar1=1.0 - lam_init, scalar2=None,
                                                  op0=ALU.mult), rd=[gsR], wr=[gsR])
            LM = max(SEQS)
            ksb = [[sb(st, "ksb%d_%d" % (i, c), [128, LM], BF16) for c in range(2)] for i in range(2)]
            ksbR = [Res() for _ in range(2)]
            for i in range(2):
                S.op("pool", lambda e: e.memset(ksb[i][0][64:128, :], 0.0), wr=[ksbR[i]])
                S.op("pool", lambda e: e.memset(ksb[i][1][0:64, :], 0.0), wr=[ksbR[i]])
            lz = [sb(st, "lz%d" % i, [128, 512], F32) for i in range(2)]
            lzR = [Res() for _ in range(2)]
            vsb = [sb(st, "vsb%d" % i, [128, LM // 128, 128], BF16) for i in range(2)]
            vsbR = [Res() for _ in range(2)]
            qsb = [sb(st, "q2sb%d" % i, [128, 512], BF16) for i in range(2)]
            qsbR = [Res() for _ in range(2)]
            NE = 8
            esb = [sb(st, "esb%d" % i, [128, 512], BF16) for i in range(NE)]
            esbR = [Res() for _ in range(NE)]
            rz = [sb(st, "rz%d" % i, [128, 512], F32) for i in range(2)]
            rzR = [Res() for _ in range(2)]
            tt = [sb(st, "tt%d" % i, [128, 512], F32) for i in range(2)]
            ttR = [Res() for _ in range(2)]
            osb = sb(st, "osb", [128, 512], F32)
            osbR = Res()
            sq = sb(st, "sq", [128, 512], BF16)
            sqR = Res()
            rs = sb(st, "rs", [128, 512], F32)
            rsR = Res()
            ob = [sb(st, "ob%d" % i, [128, 512], BF16) for i in range(2)]
            obR = [Res() for _ in range(2)]
            hn = 0
            qn = 0
            en = 0
            for si, L in enumerate(SEQS):
                s0 = seq_off[si]
                for h in range(4):
                    kb = hn % 2
                    hn += 1
                    S.dma("sp", ksb[kb][0][0:64, 0:L], KT[h, 0:64, s0:s0 + L], wr=[ksbR[kb]])
                    S.dma("sp", ksb[kb][1][64:128, 0:L], KT[h, 64:128, s0:s0 + L], wr=[ksbR[kb]])
                    S.dma("sp", vsb[kb][:, 0:L // 128, :],
                          V[s0:s0 + L, h * 128:(h + 1) * 128].rearrange("(kc p) e -> p kc e", p=128), wr=[vsbR[kb]])
                    for qt in range(L // 512):
                        t0 = s0 + qt * 512
                        qb = qn % 2
                        qn += 1
                        S.dma("sp", qsb[qb][:], QT[h, :, t0:t0 + 512], wr=[qsbR[qb]])
                        acc = [(PS[i], PSR[i]) for i in range(4)]
                        nk = L // 128
                        items = [(kc, c) for kc in range(nk) for c in range(2)]
                        LA = 3
                        einfo = {}
                        for idx in range(len(items) + LA):
                            if idx < len(items):
                                kc, c = items[idx]
                                sp_, spR = PS[4 + idx % 4], PSR[4 + idx % 4]
                                S.op("pe", lambda e: e.matmul(sp_[:], ksb[kb][c][:, kc * 128:(kc + 1) * 128],
                                                              qsb[qb][:], start=True, stop=True),
                                     rd=[ksbR[kb], qsbR[qb]], wr=[spR])
                                ei = en % NE
                                en += 1
                                S.op("act", lambda e: e.activation(out=esb[ei][:], in_=sp_[:], func=AF.Exp, scale=0.125),
                                     rd=[spR], wr=[esbR[ei]])
                                einfo[idx] = ei
                            if idx >= LA:
                                kc, c = items[idx - LA]
                                ei = einfo.pop(idx - LA)

                                def pv(e):
                                    e.matmul(acc[c][0][:], vsb[kb][:, kc, :], esb[ei][:], start=(kc == 0), stop=(kc == nk - 1))
                                    return e.matmul(acc[2 + c][0][:], ones, esb[ei][:], start=(kc == 0), stop=(kc == nk - 1))
                                S.op("pe", pv, rd=[vsbR[kb], esbR[ei], cbR], wr=[acc[c][1], acc[2 + c][1]])
                        for c in range(2):
                            S.op("dve", lambda e: e.tensor_copy(out=tt[c][:], in_=acc[c][0][:]), rd=[acc[c][1]], wr=[ttR[c]])
                            S.op("act", lambda e: e.activation(out=lz[c][:], in_=acc[2 + c][0][:], func=AF.Ln), rd=[acc[2 + c][1]], wr=[lzR[c]])
                        for c in range(2):
                            S.op("act", lambda e: e.activation(out=rz[c][:], in_=lz[c][:], func=AF.Exp, scale=-1.0), rd=[lzR[c]], wr=[rzR[c]])
                            S.op("dve", lambda e: e.tensor_tensor(out=tt[c][:], in0=tt[c][:], in1=rz[c][:], op=ALU.mult),
                                 rd=[ttR[c], rzR[c]], wr=[ttR[c]])
                        S.op("dve", lambda e: e.scalar_tensor_tensor(out=osb[:], in0=tt[1][:], scalar=nlam[:, 0:1], in1=tt[0][:],
                                                                     op0=ALU.mult, op1=ALU.add),
                             rd=[ttR[0], ttR[1], nlamR], wr=[osbR])
                        S.op("pool", lambda e: e.tensor_tensor(out=sq[:], in0=osb[:], in1=osb[:], op=ALU.mult), rd=[osbR], wr=[sqR])
                        psq, psqR = PS[4], PSR[4]
                        S.op("pe", lambda e: e.matmul(psq[:], ones, sq[:], start=True, stop=True), rd=[sqR, cbR], wr=[psqR])
                        S.op("act", lambda e: e.activation(out=rs[:], in_=psq[:], func=AF.Ln, scale=1.0 / 128, bias=epsc[:, 0:1]), rd=[psqR], wr=[rsR])
                        S.op("act", lambda e: e.activation(out=rs[:], in_=rs[:], func=AF.Exp, scale=-0.5), rd=[rsR], wr=[rsR])
                        oi = qn % 2
                        S.op("dve", lambda e: e.scalar_tensor_tensor(out=ob[oi][:], in0=osb[:], scalar=gs[:, 0:1], in1=rs[:],
                                                                     op0=ALU.mult, op1=ALU.mult),
                             rd=[osbR, rsR, gsR], wr=[obR[oi]])
                        S.dma("pool", MIX[h, :, t0:t0 + 512], ob[oi][:], rd=[obR[oi]])
            S.barrier()

    def sin_big(a, aR, s4, s8, r4, r8):
        S.op("act", lambda e: e.activation(out=s4[:], in_=a, func=AF.Sin, scale=0.25), rd=[aR], wr=[r4])
        S.op("act", lambda e: e.activation(out=s8[:], in_=a, func=AF.Sin, scale=0.125), rd=[aR], wr=[r8])
        S.op("dve", lambda e: e.tensor_tensor(out=s8[:], in0=s8[:], in1=s8[:], op=ALU.mult), rd=[r8], wr=[r8])
        S.op("dve", lambda e: e.tensor_scalar(out=s8[:], in0=s8[:], scalar1=-2.0, scalar2=1.0, op0=ALU.mult, op1=ALU.add),
             rd=[r8], wr=[r8])
        S.op("dve", lambda e: e.tensor_tensor(out=s8[:], in0=s8[:], in1=s4[:], op=ALU.mult), rd=[r8, r4], wr=[r8])
        S.op("dve", lambda e: e.tensor_tensor(out=s4[:], in0=s4[:], in1=s4[:], op=ALU.mult), rd=[r4], wr=[r4])
        S.op("dve", lambda e: e.tensor_scalar(out=s4[:], in0=s4[:], scalar1=-2.0, scalar2=1.0, op0=ALU.mult, op1=ALU.add),
             rd=[r4], wr=[r4])
        S.op("dve", lambda e: e.scalar_tensor_tensor(out=a, in0=s8[:], scalar=4.0, in1=s4[:], op0=ALU.mult, op1=ALU.mult),
             rd=[r8, r4, aR], wr=[aR])

    def fft_fwd(st, L, src, c0, ncg, H, dst_kf=None, dst_sb=None, kfs=None, tabs=None):
        N1 = L // 64
        f1, twr, twi, ut, utR, pp, ppR, are, aim, aR = tabs
        S.dma("sp", ut[0:H, 0:ncg, :], src.rearrange("c (n1 n2) -> n1 c n2", n2=128), wr=[utR])
        cpb = 512 // (2 * N1)
        for g in range(ncg // cpb):
            p, pR = psum()
            pv = p[:].rearrange("p (c k) -> p c k", c=cpb)

            def s1(e):
                for c in range(cpb):
                    ins = e.matmul(pv[:, c, :], ut[0:H, g * cpb + c, :], f1[0:H, :], start=True, stop=True)
                return ins
            S.op("pe", s1, rd=[utR], wr=[pR])
            i = g % 2
            p1 = pp[i][:, 0, :].rearrange("p (c k) -> p c k", c=cpb)
            p2 = pp[i][:, 1, :].rearrange("p (c k) -> p c k", c=cpb)
            S.op("dve", lambda e: e.tensor_tensor(out=p1, in0=pv, in1=twr[:, None, :].to_broadcast([128, cpb, 2 * N1]),
                                                  op=ALU.mult), rd=[pR], wr=[ppR[i]])
            S.op("dve", lambda e: e.tensor_tensor(out=p2, in0=pv, in1=twi[:, None, :].to_broadcast([128, cpb, 2 * N1]),
                                                  op=ALU.mult), rd=[pR], wr=[ppR[i]])
            cs = slice(g * cpb, (g + 1) * cpb)
            S.op("pool", lambda e: e.tensor_tensor(out=are[:, cs, :], in0=p1[:, :, 0:N1], in1=p2[:, :, N1:2 * N1],
                                                   op=ALU.subtract), rd=[ppR[i]], wr=[aR])
            S.op("pool", lambda e: e.tensor_tensor(out=aim[:, cs, :], in0=p2[:, :, 0:N1], in1=p1[:, :, N1:2 * N1],
                                                   op=ALU.add), rd=[ppR[i]], wr=[aR])
        cpc = 512 // N1
        for g in range(ncg // cpc):
            cs = slice(g * cpc, (g + 1) * cpc)
            ar = are[:, cs, :].rearrange("p c k -> p (c k)")
            ai = aim[:, cs, :].rearrange("p c k -> p (c k)")
            pr_, prR = psum()
            pi_, piR = psum()
            S.op("pe", lambda e: mm_acc(e, pr_[:], [(f2re, ar), (f2imn, ai)]), rd=[aR, cbR], wr=[prR])
            S.op("pe", lambda e: mm_acc(e, pi_[:], [(f2im, ar), (f2re, ai)]), rd=[aR, cbR], wr=[piR])
            xr = pr_[:].rearrange("p (c k) -> p c k", c=cpc)
            xi = pi_[:].rearrange("p (c k) -> p c k", c=cpc)
            if dst_kf is not None:
                kst, kstR = dst_sb
                i = g % 2
                S.op("act", lambda e: e.activation(out=kst[i][:, 0:cpc, 0, :], in_=xr, func=AF.Copy), rd=[prR], wr=[kstR[i]])
                S.op("dve", lambda e: e.tensor_copy(out=kst[i][:, 0:cpc, 1, :], in_=xi), rd=[piR], wr=[kstR[i]])
                S.dma("pool", dst_kf[:, c0 + g * cpc:c0 + (g + 1) * cpc, :, :], kst[i][:, 0:cpc, :, :], rd=[kstR[i]])
            else:
                kf, kfR = kfs
                yre, yim, yR, mt, mtR = dst_sb
                kre = kf[:, cs, 0, :]
                kim = kf[:, cs, 1, :]
                i = g % 2
                m = [mt[i][:, q, :].rearrange("p (c k) -> p c k", c=cpc) for q in range(4)]
                S.op("dve", lambda e: e.tensor_tensor(out=m[0], in0=xr, in1=kre, op=ALU.mult), rd=[prR, kfR], wr=[mtR[i]])
                S.op("dve", lambda e: e.tensor_tensor(out=m[1], in0=xi, in1=kim, op=ALU.mult), rd=[piR, kfR], wr=[mtR[i]])
                S.op("dve", lambda e: e.tensor_tensor(out=m[2], in0=xr, in1=kim, op=ALU.mult), rd=[prR, kfR], wr=[mtR[i]])
                S.op("dve", lambda e: e.tensor_tensor(out=m[3], in0=xi, in1=kre, op=ALU.mult), rd=[piR, kfR], wr=[mtR[i]])
                S.op("pool", lambda e: e.tensor_tensor(out=yre[:, cs, :], in0=m[0], in1=m[1], op=ALU.subtract),
                     rd=[mtR[i]], wr=[yR])
                S.op("pool", lambda e: e.tensor_tensor(out=yim[:, cs, :], in0=m[2], in1=m[3], op=ALU.add),
                     rd=[mtR[i]], wr=[yR])

    def fft_tabs(st, L, ncg, H):
        N1 = L // 64
        f1 = sb(st, "f1", [N1, 2 * N1], BF16)
        twr = sb(st, "twr", [128, 2 * N1], F32)
        twi = sb(st, "twi", [128, 2 * N1], F32)
        tR = Res()
        S.dma("sp", f1[:], CT["f1_%d" % N1][:, :], wr=[tR])
        S.dma("sp", twr[:], CT["twr_%d" % N1][:, :], wr=[tR])
        S.dma("sp", twi[:], CT["twi_%d" % N1][:, :], wr=[tR])
        ut = sb(st, "ut", [N1, ncg, 128], BF16)
        pp = [sb(st, "pp%d" % i, [128, 2, 512], F32) for i in range(2)]
        are = sb(st, "are", [128, ncg, N1], BF16)
        aim = sb(st, "aim", [128, ncg, N1], BF16)
        S.barrier()
        return (f1, twr, twi, ut, Res(), pp, [Res(), Res()], are, aim, Res())

    def filters(l):
        CW = 2048
        GA, GB = PSALL[:, 0:2048], PSALL[:, 2048:4096]
        GAR, GBR = PSR[0:4], PSR[4:8]
        for L in LT:
            N1 = L // 64
            with ExitStack() as st:
                w1 = sb(st, "fw1", [33, 64], F32)
                w2 = sb(st, "fw2", [64, 64], F32)
                w3 = sb(st, "fw3", [64, 1024], F32)
                pv = sb(st, "fpv", [64, 4], F32)
                nd = sb(st, "fnd", [1, 512], F32)
                wR = Res()
                S.dma("sp", w1[:], W["filt_w1"][l], wr=[wR])
                S.dma("sp", w2[:], W["filt_w2"][l], wr=[wR])
                S.dma("sp", w3[:], W["filt_w3"][l], wr=[wR])
                for i, n in enumerate(("filt_b1", "filt_freq1", "filt_b2", "filt_freq2")):
                    S.dma("sp", pv[:, i:i + 1], col(W[n][l]), wr=[wR])
                S.dma("sp", nd[:], CT["negdelta"][:, :], wr=[wR])
                nch = 2 * L // CW
                zt = [sb(st, "fzt%d" % i, [33, CW], F32) for i in range(2)]
                ztR = [Res(), Res()]
                tn = [sb(st, "ftn%d" % i, [1, CW], F32) for i in range(2)]
                a1 = sb(st, "fa1", [64, CW], F32)
                a1R = Res()
                a2 = sb(st, "fa2", [64, CW], F32)
                a2R = Res()
                s4 = sb(st, "fs4", [64, CW], F32)
                s8 = sb(st, "fs8", [64, CW], F32)
                r4, r8 = Res(), Res()
                wsb = [sb(st, "fws%d" % i, [128, CW], F32) for i in range(2)]
                wsbR = [Res(), Res()]
                kc_ = [sb(st, "fkc%d" % i, [128, CW], F32) for i in range(2)]
                kcR = [Res(), Res()]
                kb_ = [sb(st, "fkb%d" % i, [128, CW], BF16) for i in range(2)]
                kbR = [Res(), Res()]
                nrm = sb(st, "fnrm", [128, 4, nch], F32)
                nrmR = Res()
                n_ = 0

                def mm4(e, dst, rows, lhsT, rhs):
                    for q in range(4):
                        ins = e.matmul(dst[0:rows, q * 512:(q + 1) * 512], lhsT, rhs[:, q * 512:(q + 1) * 512], start=True, stop=True)
                    return ins
                for ci in range(nch):
                    b = ci % 2
                    S.dma("sp", zt[b][:], CT["zt_%d" % L][:, ci * CW:(ci + 1) * CW], wr=[ztR[b]])
                    S.dma("sp", tn[b][:], CT["tn_%d" % L][:, ci * CW:(ci + 1) * CW], wr=[ztR[b]])
                    S.op("pe", lambda e: mm4(e, GA, 64, w1[:], zt[b]), rd=[wR, ztR[b]], wr=GAR)
                    S.op("dve", lambda e: e.tensor_scalar(out=a1[:], in0=GA[0:64, :], scalar1=pv[:, 0:1], scalar2=pv[:, 1:2],
                                                          op0=ALU.add, op1=ALU.mult), rd=GAR + [wR], wr=[a1R])
                    sin_big(a1[:], a1R, s4, s8, r4, r8)
                    S.op("pe", lambda e: mm4(e, GB, 64, w2[:], a1), rd=[wR, a1R], wr=GBR)
                    S.op("dve", lambda e: e.tensor_scalar(out=a2[:], in0=GB[0:64, :], scalar1=pv[:, 2:3], scalar2=pv[:, 3:4],
                                                          op0=ALU.add, op1=ALU.mult), rd=GBR + [wR], wr=[a2R])
                    sin_big(a2[:], a2R, s4, s8, r4, r8)
                    half = 0 if ci * CW < L else 1
                    for fc in range(4):
                        i = n_ % 2
                        n_ += 1
                        wc = w3[:, half * 512 + fc * 128: half * 512 + (fc + 1) * 128]
                        S.op("pe", lambda e: mm4(e, GA, 128, wc, a2), rd=[wR, a2R], wr=GAR)
                        S.op("pe", lambda e: mm4(e, GB, 128, nd[:, fc * 128:(fc + 1) * 128], tn[b]), rd=[wR, ztR[b]], wr=GBR)
                        S.op("act", lambda e: e.activation(out=wsb[i][:], in_=GB, func=AF.Exp), rd=GBR, wr=[wsbR[i]])
                        S.op("dve", lambda e: e.tensor_tensor(out=kc_[i][:], in0=GA, in1=wsb[i][:], op=ALU.mult),
                             rd=GAR + [wsbR[i]], wr=[kcR[i]])
                        if ci == 0:
                            wcb = w3[:, 512 + fc * 128: 512 + (fc + 1) * 128]
                            S.op("pe", lambda e: e.matmul(GB[:, 0:2], wcb, a2[:, 0:2], start=True, stop=True),
                                 rd=[wR, a2R], wr=[GBR[0]])
                            S.op("dve", lambda e: e.tensor_tensor(out=kc_[i][:, 0:1], in0=kc_[i][:, 0:1], in1=GB[:, 0:1],
                                                                  op=ALU.add), rd=[GBR[0], kcR[i]], wr=[kcR[i]])
                        S.op("dve", lambda e: e.tensor_reduce(out=nrm[:, fc, ci:ci + 1], in_=kc_[i][:], axis=AX.X, op=ALU.add,
                                                              apply_absolute_value=True), rd=[kcR[i]], wr=[nrmR])
                        S.op("act", lambda e: e.activation(out=kb_[i][:], in_=kc_[i][:], func=AF.Copy), rd=[kcR[i]], wr=[kbR[i]])
                        S.dma("pool", KERN[L][fc * 128:(fc + 1) * 128, ci * CW:(ci + 1) * CW], kb_[i][:], rd=[kbR[i]])
                S.op("dve", lambda e: e.tensor_reduce(out=rnorm[L][:], in_=nrm[:], axis=AX.X, op=ALU.add), rd=[nrmR], wr=[rnormR[L]])
                S.op("dve", lambda e: e.reciprocal(out=rnorm[L][:], in_=rnorm[L][:]), rd=[rnormR[L]], wr=[rnormR[L]])
                S.barrier()
            with ExitStack() as st:
                NCG = 32
                tabs = fft_tabs(st, L, NCG, N1)
                kst = [sb(st, "kst%d" % i, [128, 512 // N1, 2, N1], BF16) for i in range(2)]
                kstR = [Res(), Res()]
                for c0 in range(0, 512, NCG):
                    fft_fwd(st, L, KERN[L][c0:c0 + NCG, :], c0, NCG, N1, dst_kf=KF[L], dst_sb=(kst, kstR), tabs=tabs)
                S.barrier()

    def phase3(l):
        sub = (lambda k: only is None or 'p3' in only or k in only)
        if sub('p3f'):
            filters(l)
        TB = 2048
        if sub('p3a'):
            with ExitStack() as st:
                cw = sb(st, "cw", [128, 12, 4], F32)
                hd = sb(st, "hd", [128, 4], F32)
                cwR = Res()
                for ch in range(12):
                    for j in range(3):
                        S.dma("sp", cw[:, ch, j:j + 1], col(W["conv_w"][l][j, ch * 128:(ch + 1) * 128]), wr=[cwR])
                    S.dma("sp", cw[:, ch, 3:4], col(W["conv_b"][l][ch * 128:(ch + 1) * 128]), wr=[cwR])
                for cc in range(4):
                    S.dma("sp", hd[:, cc:cc + 1], col(W["hyena_d"][l][cc * 128:(cc + 1) * 128]), wr=[cwR])
                hin = [[sb(st, "hin%d_%d" % (i, s), [128, TB + 2], F32) for s in range(3)] for i in range(2)]
                hinR = [[Res() for s in range(3)] for i in range(2)]
                cv = [sb(st, "cv%d" % s, [128, TB], F32) for s in range(3)]
                cvR = [Res() for s in range(3)]
                uu = sb(st, "uu", [128, TB], F32)
                uuR = Res()
                ub = [sb(st, "ub%d" % i, [128, TB], BF16) for i in range(2)]
                ubR = [Res(), Res()]
                uxo = [sb(st, "uxo%d" % i, [128, TB], F32) for i in range(2)]
                uxoR = [Res(), Res()]
                x0o = [sb(st, "x0o%d" % i, [128, TB], F32) for i in range(2)]
                x0oR = [Res(), Res()]
                n_ = 0
                for si, L in enumerate(SEQS):
                    s0 = seq_off[si]
                    for tb in range(L // TB):
                        a = tb * TB
                        lo = 1 if a == 0 else 0
                        hi = 1 if a + TB == L else 0
                        for cc in range(4):
                            i = n_ % 2
                            n_ += 1
                            for s in range(3):
                                ch = s * 4 + cc
                                if lo:
                                    S.op("pool", lambda e: e.memset(hin[i][s][:, 0:1], 0.0), wr=[hinR[i][s]])
                                if hi:
                                    S.op("pool", lambda e: e.memset(hin[i][s][:, TB + 1:TB + 2], 0.0), wr=[hinR[i][s]])
                                S.dma("sp", hin[i][s][:, lo:TB + 2 - hi], HY[ch, :, s0 + a - 1 + lo:s0 + a + TB + 1 - hi], wr=[hinR[i][s]])
                                S.op("dve", lambda e: e.tensor_scalar(out=cv[s][:], in0=hin[i][s][:, 1:TB + 1], scalar1=cw[:, ch, 1:2],
                                                                      scalar2=cw[:, ch, 3:4], op0=ALU.mult, op1=ALU.add),
                                     rd=[hinR[i][s], cwR], wr=[cvR[s]])
                                S.op("dve", lambda e: e.scalar_tensor_tensor(out=cv[s][:], in0=hin[i][s][:, 0:TB], scalar=cw[:, ch, 0:1],
                                                                             in1=cv[s][:], op0=ALU.mult, op1=ALU.add),
                                     rd=[hinR[i][s], cwR, cvR[s]], wr=[cvR[s]])
                                S.op("dve", lambda e: e.scalar_tensor_tensor(out=cv[s][:], in0=hin[i][s][:, 2:TB + 2], scalar=cw[:, ch, 2:3],
                                                                             in1=cv[s][:], op0=ALU.mult, op1=ALU.add),
                                     rd=[hinR[i][s], cwR, cvR[s]], wr=[cvR[s]])
                            S.op("pool", lambda e: e.tensor_tensor(out=uu[:], in0=cv[1][:], in1=cv[2][:], op=ALU.mult),
                                 rd=[cvR[1], cvR[2]], wr=[uuR])
                            S.op("act", lambda e: e.activation(out=ub[i][:], in_=uu[:], func=AF.Copy), rd=[uuR], wr=[ubR[i]])
                            S.op("dve", lambda e: e.scalar_tensor_tensor(out=uxo[i][:], in0=uu[:], scalar=hd[:, cc:cc + 1], in1=cv[0][:],
                                                                         op0=ALU.mult, op1=ALU.mult), rd=[uuR, cvR[0], cwR], wr=[uxoR[i]])
                            S.op("act", lambda e: e.activation(out=x0o[i][:], in_=cv[0][:], func=AF.Copy, scale=rnorm[L][:, cc:cc + 1]),
                                 rd=[cvR[0], rnormR[L]], wr=[x0oR[i]])
                            S.dma("pool", U[cc * 128:(cc + 1) * 128, s0 + a:s0 + a + TB], ub[i][:], rd=[ubR[i]])
                            S.dma("pool", UX[cc, :, s0 + a:s0 + a + TB], uxo[i][:], rd=[uxoR[i]])
                            S.dma("pool", X0[cc, :, s0 + a:s0 + a + TB], x0o[i][:], rd=[x0oR[i]])
                S.barrier()
        if sub('p3b'):
            for L in LT:
                N1 = L // 64
                H = N1 // 2
                NCG = 32
                with ExitStack() as st:
                    tabs = fft_tabs(st, L, NCG, H)
                    itr = sb(st, "itwr", [N1, 256], F32)
                    iti = sb(st, "itwi", [N1, 256], F32)
                    g1 = sb(st, "g1", [N1, 2 * H], BF16)
                    itR = Res()
                    S.dma("sp", itr[:], CT["itwr_%d" % N1][:, :], wr=[itR])
                    S.dma("sp", iti[:], CT["itwi_%d" % N1][:, :], wr=[itR])
                    S.dma("sp", g1[:], CT["g1_%d" % N1][:, :], wr=[itR])
                    if N1 == 32:
                        itr4 = sb(st, "itr4", [128, 256], F32)
                        iti4 = sb(st, "iti4", [128, 256], F32)
                        g1p = sb(st, "g1p", [128, 4, 2 * H], BF16)
                        bre4 = sb(st, "bre4", [128, NCG // 4, 128], BF16)
                        bim4 = sb(st, "bim4", [128, NCG // 4, 128], BF16)
                        S.dma("sp", itr4[:], CT["itwr4_32"][:, :], wr=[itR])
                        S.dma("sp", iti4[:], CT["itwi4_32"][:, :], wr=[itR])
                        S.dma("sp", g1p[:], CT["g1p_32"][:, :, :], wr=[itR])
                    kf = [sb(st, "kf%d" % i, [128, NCG, 2, N1], BF16) for i in range(2)]
                    kfR = [Res(), Res()]
                    yre = sb(st, "yre", [128, NCG, N1], BF16)
                    yim = sb(st, "yim", [128, NCG, N1], BF16)
                    yR = Res()
                    mt = [sb(st, "mt%d" % i, [128, 4, 512], F32) for i in range(2)]
                    mtR = [Res(), Res()]
                    bre = sb(st, "bre", [N1, NCG, 128], BF16)
                    bim = sb(st, "bim", [N1, NCG, 128], BF16)
                    bR = Res()
                    yo = [sb(st, "yo%d" % i, [H, NCG, 128], F32) for i in range(2)]
                    yoR = [Res(), Res()]
                    n_ = 0
                    for si, Ls in enumerate(SEQS):
                        if Ls != L:
                            continue
                        s0 = seq_off[si]
                        for c0 in range(0, 512, NCG):
                            i = n_ % 2
                            n_ += 1
                            S.dma("sp", kf[i][:], KF[L][:, c0:c0 + NCG, :, :], wr=[kfR[i]])
                            fft_fwd(st, L, U[c0:c0 + NCG, s0:s0 + L], c0, NCG, H, kfs=(kf[i], kfR[i]),
                                    dst_sb=(yre, yim, yR, mt, mtR), tabs=tabs)
                            pp, ppR = tabs[5], tabs[6]
                            if N1 == 32:
                                for g in range(NCG // 8):
                                    p, pR = psum()
                                    pvw = p[:].rearrange("p (c k) -> p c k", c=2)

                                    def s1b(e):
                                        for sc in range(2):
                                            c_ = (g * 2 + sc) * 4
                                            e.matmul(pvw[:, sc, :], yre[:, c_:c_ + 4, :].rearrange("p c k -> p (c k)"), gg1, start=True, stop=False)
                                            ins = e.matmul(pvw[:, sc, :], yim[:, c_:c_ + 4, :].rearrange("p c k -> p (c k)"), gg2, start=False, stop=True)
                                        return ins
                                    S.op("pe", s1b, rd=[yR, cbR], wr=[pR])
                                    j = g % 2
                                    p1 = pp[j][:, 0, :].rearrange("p (c k) -> p c k", c=2)
                                    p2 = pp[j][:, 1, :].rearrange("p (c k) -> p c k", c=2)
                                    S.op("dve", lambda e: e.tensor_tensor(out=p1, in0=pvw, in1=itr4[:, None, :].to_broadcast([128, 2, 256]),
                                                                          op=ALU.mult), rd=[pR, itR], wr=[ppR[j]])
                                    S.op("dve", lambda e: e.tensor_tensor(out=p2, in0=pvw, in1=iti4[:, None, :].to_broadcast([128, 2, 256]),
                                                                          op=ALU.mult), rd=[pR, itR], wr=[ppR[j]])
                                    cs = slice(g * 2, g * 2 + 2)
                                    S.op("pool", lambda e: e.tensor_tensor(out=bre4[:, cs, :], in0=p1[:, :, 0:128], in1=p2[:, :, 128:256],
                                                                           op=ALU.subtract), rd=[ppR[j]], wr=[bR])
                                    S.op("pool", lambda e: e.tensor_tensor(out=bim4[:, cs, :], in0=p2[:, :, 0:128], in1=p1[:, :, 128:256],
                                                                           op=ALU.add), rd=[ppR[j]], wr=[bR])
                                for g in range(NCG // 4):
                                    cs = slice(g * 4, g * 4 + 4)
                                    p, pR = psum()

                                    def s2b(e):
                                        for c4 in range(4):
                                            e.matmul(p[0:H, c4 * 128:(c4 + 1) * 128], g1p[:, c4, 0:H], bre4[:, g, :], start=True, stop=False)
                                            ins = e.matmul(p[0:H, c4 * 128:(c4 + 1) * 128], g1p[:, c4, H:2 * H], bim4[:, g, :], start=False, stop=True)
                                        return ins
                                    S.op("pe", s2b, rd=[bR, itR], wr=[pR])
                                    dst = yo[i][:, cs, :].rearrange("p c k -> p (c k)")
                                    if g % 2 == 0:
                                        S.op("act", lambda e: e.activation(out=dst, in_=p[0:H, :], func=AF.Copy), rd=[pR], wr=[yoR[i]])
                                    else:
                                        S.op("dve", lambda e: e.tensor_copy(out=dst, in_=p[0:H, :]), rd=[pR], wr=[yoR[i]])
                            else:
                                for g in range(NCG // 2):
                                    p, pR = psum()
                                    pvw = p[0:N1, :].rearrange("p (c k) -> p c k", c=2)

                                    def s1i(e):
                                        for c in range(2):
                                            cc_ = g * 2 + c
                                            e.matmul(pvw[:, c, :], yre[:, cc_, :], gg1, start=True, stop=False)
                                            ins = e.matmul(pvw[:, c, :], yim[:, cc_, :], gg2, start=False, stop=True)
                                        return ins
                                    S.op("pe", s1i, rd=[yR, cbR], wr=[pR])
                                    j = g % 2
                                    p1 = pp[j][0:N1, 0, :].rearrange("p (c k) -> p c k", c=2)
                                    p2 = pp[j][0:N1, 1, :].rearrange("p (c k) -> p c k", c=2)
                                    S.op("dve", lambda e: e.tensor_tensor(out=p1, in0=pvw, in1=itr[:, None, :].to_broadcast([N1, 2, 256]),
                                                                          op=ALU.mult), rd=[pR, itR], wr=[ppR[j]])
                                    S.op("dve", lambda e: e.tensor_tensor(out=p2, in0=pvw, in1=iti[:, None, :].to_broadcast([N1, 2, 256]),
                                                                          op=ALU.mult), rd=[pR, itR], wr=[ppR[j]])
                                    cs = slice(g * 2, g * 2 + 2)
                                    S.op("pool", lambda e: e.tensor_tensor(out=bre[:, cs, :], in0=p1[:, :, 0:128], in1=p2[:, :, 128:256],
                                                                           op=ALU.subtract), rd=[ppR[j]], wr=[bR])
                                    S.op("pool", lambda e: e.tensor_tensor(out=bim[:, cs, :], in0=p2[:, :, 0:128], in1=p1[:, :, 128:256],
                                                                           op=ALU.add), rd=[ppR[j]], wr=[bR])
                                for g in range(NCG // 4):
                                    cs = slice(g * 4, g * 4 + 4)
                                    p, pR = psum()
                                    br_ = bre[:, cs, :].rearrange("p c k -> p (c k)")
                                    bi_ = bim[:, cs, :].rearrange("p c k -> p (c k)")
                                    S.op("pe", lambda e: mm_acc(e, p[0:H, :], [(g1[:, 0:H], br_), (g1[:, H:2 * H], bi_)]),
                                         rd=[bR, itR], wr=[pR])
                                    dst = yo[i][:, cs, :].rearrange("p c k -> p (c k)")
                                    if g % 2 == 0:
                                        S.op("act", lambda e: e.activation(out=dst, in_=p[0:H, :], func=AF.Copy), rd=[pR], wr=[yoR[i]])
                                    else:
                                        S.op("dve", lambda e: e.tensor_copy(out=dst, in_=p[0:H, :]), rd=[pR], wr=[yoR[i]])
                            S.dma("pool", YC[c0:c0 + NCG, s0:s0 + L].rearrange("c (n1 n2) -> n1 c n2", n2=128), yo[i][:], rd=[yoR[i]])
                    S.barrier()
        if sub('p3c'):
            with ExitStack() as st:
                yc = [sb(st, "ycs%d" % i, [128, TB], F32) for i in range(2)]
                ux = [sb(st, "uxs%d" % i, [128, TB], F32) for i in range(2)]
                x0 = [sb(st, "x0s%d" % i, [128, TB], F32) for i in range(2)]
                inR = [Res(), Res()]
                yb = [sb(st, "ybs%d" % i, [128, TB], BF16) for i in range(2)]
                ybR = [Res(), Res()]
                n_ = 0
                for t0 in range(0, NT, TB):
                    for cc in range(4):
                        i = n_ % 2
                        n_ += 1
                        S.dma("sp", yc[i][:], YC[cc * 128:(cc + 1) * 128, t0:t0 + TB], wr=[inR[i]])
                        S.dma("sp", ux[i][:], UX[cc, :, t0:t0 + TB], wr=[inR[i]])
                        S.dma("sp", x0[i][:], X0[cc, :, t0:t0 + TB], wr=[inR[i]])
                        S.op("dve", lambda e: e.tensor_tensor(out=yc[i][:], in0=yc[i][:], in1=x0[i][:], op=ALU.mult), rd=[inR[i]], wr=[inR[i]])
                        S.op("pool", lambda e: e.tensor_tensor(out=yb[i][:], in0=yc[i][:], in1=ux[i][:], op=ALU.add), rd=[inR[i]], wr=[ybR[i]])
                        S.dma("pool", MIX[4 + cc, :, t0:t0 + TB], yb[i][:], rd=[ybR[i]])
                S.barrier()

    def tok_major_proj(actT, actTR, nk, wt, wtR, j):
        res = []
        for h in range(2):
            p, pR = psum()
            S.op("pe", lambda e: mm_acc(e, p[:], [(actT[:, k, j * 128:(j + 1) * 128], wt[:, k, h * 512:(h + 1) * 512])
                                                  for k in range(nk)]), rd=[actTR, wtR], wr=[pR])
            res.append((p, pR))
        return [r[0] for r in res], [r[1] for r in res]

    def phase4a(l, SRC):
        with ExitStack() as st:
            wo, woR = load_w(st, "wout", W["w_out"][l], D, D)
            gam, gamR = load_gamma(st, "g4a", W["ln_mix_post"][l])
            xt = [sb(st, "xt%d" % i, [128, 4, D], F32) for i in range(2)]
            xtR = [Res() for _ in range(2)]
            mx = [sb(st, "mx%d" % i, [128, 8, 512], BF16) for i in range(2)]
            mxR = [Res() for _ in range(2)]
            ss = sb(st, "ss", [128, 8], F32)
            ssR = Res()
            junk = sb(st, "junk", [128, D], F32)
            junkR = Res()
            tmp = sb(st, "tmp", [128, D], F32)
            tmpR = Res()

            def ld(ti):
                b = ti % 2
                t0 = tiles[ti][0]
                S.dma("sp", xt[b][:], xview(SRC, t0), wr=[xtR[b]])
                S.dma("sp", mx[b][:], MIX[:, :, t0:t0 + 512].rearrange("c p t -> p c t"), wr=[mxR[b]])
            ld(0)
            for ti, (t0, si, p0) in enumerate(tiles):
                b = ti % 2
                if ti + 1 < len(tiles):
                    ld(ti + 1)
                for j in range(4):
                    ps2, ps2R = tok_major_proj(mx[b], mxR[b], 8, wo, woR, j)
                    postnorm_residual(ps2, ps2R, gam, gamR, xt[b][:, j, :], xtR[b], ss, ssR, junk, junkR, tmp, tmpR)
                S.dma("pool", xview(XR, t0), xt[b][:], rd=[xtR[b]])
            S.barrier()

    def phase4b(l):
        with ExitStack() as st:
            wq, wqR = load_w(st, "wq", W["wq_x"][l], D, D)
            wk, wkR = load_w(st, "wk", W["wk_x"][l], D, D)
            wv, wvR = load_w(st, "wv", W["wv_x"][l], D, D)
            wo, woR = load_w(st, "wo", W["wo_x"][l], D, D)
            gpre, gpreR = load_gamma(st, "gxpre", W["ln_x_pre"][l])
            gpost, gpostR = load_gamma(st, "gxpost", W["ln_x_post"][l])
            gmem, gmemR = load_gamma(st, "gmem", W["ln_mem"][l])
            xt = [sb(st, "xt%d" % i, [128, 4, D], F32) for i in range(2)]
            xtR = [Res() for _ in range(2)]
            xn = sb(st, "xn", [128, 4, D], BF16)
            xnR = [Res() for _ in range(4)]
            xnT = sb(st, "xnT", [128, 8, 512], BF16)
            xnTR = Res()
            qxT = sb(st, "qxT", [128, 8, 512], BF16)
            qxTR = [Res() for _ in range(8)]
            oT = sb(st, "oT", [128, 8, 512], BF16)
            oTR = Res()
            kxT = sb(st, "kxT", [128, 8, NMEM], BF16)
            kxTR = Res()
            vx = sb(st, "vx", [128, 2, D], BF16)
            vxR = Res()
            mt_ = sb(st, "memt", [128, 2, D], F32)
            mtR_ = Res()
            ss = sb(st, "ss", [128, 8], F32)
            ssR = Res()
            junk = sb(st, "junk", [128, D], F32)
            junkR = Res()
            tmp = sb(st, "tmp", [128, D], F32)
            tmpR = Res()
            esb = [sb(st, "esb%d" % i, [128, 512], BF16) for i in range(4)]
            esbR = [Res() for _ in range(4)]
            rz = sb(st, "rz", [128, 512], F32)
            rzR = Res()
            en = 0
            cur = -1
            S.dma("sp", xt[0][:], xview(XR, tiles[0][0]), wr=[xtR[0]])
            for ti, (t0, si, p0) in enumerate(tiles):
                b = ti % 2
                if ti + 1 < len(tiles):
                    S.dma("sp", xt[1 - b][:], xview(XR, tiles[ti + 1][0]), wr=[xtR[1 - b]])
                if si != cur:
                    cur = si
                    S.dma("sp", mt_[:], MEM[si * NMEM:(si + 1) * NMEM, :].rearrange("(j p) d -> p j d", p=128), wr=[mtR_])
                    norm_transpose(mt_, mtR_, 2, gmem, gmemR, xn, xnR, xnT, xnTR, ss, ssR, junk, junkR)
                    for fc in range(8):
                        p, pR = psum()
                        S.op("pe", lambda e: mm_acc(e, p[:, 0:NMEM], [(wk[:, k, fc * 128:(fc + 1) * 128], xnT[:, k, 0:NMEM])
                                                                      for k in range(8)]), rd=[wkR, xnTR], wr=[pR])
                        S.op("act", lambda e: e.activation(out=kxT[:, fc, :], in_=p[:, 0:NMEM], func=AF.Copy), rd=[pR], wr=[kxTR])
                    for mc in range(2):
                        for h in range(2):
                            p, pR = psum()
                            S.op("pe", lambda e: mm_acc(e, p[:], [(xnT[:, k, mc * 128:(mc + 1) * 128], wv[:, k, h * 512:(h + 1) * 512])
                                                                  for k in range(8)]), rd=[wvR, xnTR], wr=[pR])
                            S.op("dve", lambda e: e.tensor_copy(out=vx[:, mc, h * 512:(h + 1) * 512], in_=p[:]), rd=[pR], wr=[vxR])
                norm_transpose(xt[b], xtR[b], 4, gpre, gpreR, xn, xnR, xnT, xnTR, ss, ssR, junk, junkR)
                for fc in range(8):
                    p, pR = psum()
                    S.op("pe", lambda e: mm_acc(e, p[:], [(wq[:, k, fc * 128:(fc + 1) * 128], xnT[:, k, :]) for k in range(8)]),
                         rd=[wqR, xnTR], wr=[pR])
                    if fc % 2 == 0:
                        S.op("act", lambda e: e.activation(out=qxT[:, fc, :], in_=p[:], func=AF.Copy), rd=[pR], wr=[qxTR[fc]])
                    else:
                        S.op("dve", lambda e: e.tensor_copy(out=qxT[:, fc, :], in_=p[:]), rd=[pR], wr=[qxTR[fc]])
                for hx in range(4):
                    ee = []
                    for mc in range(2):
                        p, pR = psum()
                        S.op("pe", lambda e: mm_acc(e, p[:], [(kxT[:, 2 * hx + q, mc * 128:(mc + 1) * 128], qxT[:, 2 * hx + q, :])
                                                              for q in range(2)]), rd=[kxTR, qxTR[2 * hx], qxTR[2 * hx + 1]], wr=[pR])
                        ei = en % 4
                        en += 1
                        S.op("act", lambda e: e.activation(out=esb[ei][:], in_=p[:], func=AF.Exp, scale=1.0 / 16), rd=[pR], wr=[esbR[ei]])
                        ee.append(ei)
                    pz, pzR = psum()
                    S.op("pe", lambda e: mm_acc(e, pz[:], [(ones, esb[ee[0]][:]), (ones, esb[ee[1]][:])]),
                         rd=[cbR, esbR[ee[0]], esbR[ee[1]]], wr=[pzR])
                    S.op("act", lambda e: e.activation(out=rz[:], in_=pz[:], func=AF.Ln), rd=[pzR], wr=[rzR])
                    S.op("act", lambda e: e.activation(out=rz[:], in_=rz[:], func=AF.Exp, scale=-1.0), rd=[rzR], wr=[rzR])
                    for q in range(2):
                        fc = 2 * hx + q
                        p, pR = psum()
                        S.op("pe", lambda e: mm_acc(e, p[:], [(vx[:, mc, fc * 128:(fc + 1) * 128], esb[ee[mc]][:]) for mc in range(2)]),
                             rd=[vxR, esbR[ee[0]], esbR[ee[1]]], wr=[pR])
                        S.op("dve", lambda e: e.tensor_tensor(out=oT[:, fc, :], in0=p[:], in1=rz[:], op=ALU.mult), rd=[pR, rzR], wr=[oTR])
                for j in range(4):
                    ps2, ps2R = tok_major_proj(oT, oTR, 8, wo, woR, j)
                    postnorm_residual(ps2, ps2R, gpost, gpostR, xt[b][:, j, :], xtR[b], ss, ssR, junk, junkR, tmp, tmpR)
                S.dma("pool", xview(XR, t0), xt[b][:], rd=[xtR[b]])
            S.barrier()

    def phase5(l):
        with ExitStack() as st:
            wg, wgR = load_w(st, "wg", W["w_gate"][l], D, DFF)
            wu, wuR = load_w(st, "wu", W["w_up"][l], D, DFF)
            gam, gamR = load_gamma(st, "g5", W["ln_ffn_pre"][l])
            xt = [sb(st, "xt%d" % i, [128, 4, D], F32) for i in range(2)]
            xtR = [Res() for _ in range(2)]
            xn2 = [sb(st, "xn%d" % i, [128, 4, D], BF16) for i in range(2)]
            xnR2 = [[Res() for _ in range(4)] for i in range(2)]
            xnT2 = [sb(st, "xnT%d" % i, [128, 8, 512], BF16) for i in range(2)]
            xnTR2 = [Res(), Res()]
            ss = sb(st, "ss", [128, 8], F32)
            ssR = Res()
            junk = sb(st, "junk", [128, D], F32)
            junkR = Res()
            sg = [sb(st, "sg%d" % i, [128, 512], F32) for i in range(2)]
            sgR = [Res(), Res()]
            hb = [sb(st, "hb%d" % i, [128, 512], BF16) for i in range(3)]
            hbR = [Res() for _ in range(3)]
            n_ = 0
            S.dma("sp", xt[0][:], xview(XR, tiles[0][0]), wr=[xtR[0]])
            norm_transpose(xt[0], xtR[0], 4, gam, gamR, xn2[0], xnR2[0], xnT2[0], xnTR2[0], ss, ssR, junk, junkR)
            for ti, (t0, si, p0) in enumerate(tiles):
                b = ti % 2
                xnT, xnTR = xnT2[b], xnTR2[b]
                if ti + 1 < len(tiles):
                    S.dma("sp", xt[1 - b][:], xview(XR, tiles[ti + 1][0]), wr=[xtR[1 - b]])
                for f in range(NFF):
                    if f == NFF // 2 and ti + 1 < len(tiles):
                        norm_transpose(xt[1 - b], xtR[1 - b], 4, gam, gamR, xn2[1 - b], xnR2[1 - b], xnT2[1 - b], xnTR2[1 - b],
                                       ss, ssR, junk, junkR)
                    pg, pgR = psum()
                    pu, puR = psum()
                    S.op("pe", lambda e: mm_acc(e, pg[:], [(wg[:, k, f * 128:(f + 1) * 128], xnT[:, k, :]) for k in range(8)]),
                         rd=[wgR, xnTR], wr=[pgR])
                    S.op("pe", lambda e: mm_acc(e, pu[:], [(wu[:, k, f * 128:(f + 1) * 128], xnT[:, k, :]) for k in range(8)]),
                         rd=[wuR, xnTR], wr=[puR])
                    i = n_ % 2
                    o = n_ % 3
                    n_ += 1
                    S.op("act", lambda e: e.activation(out=sg[i][:], in_=pg[:], func=AF.Silu), rd=[pgR], wr=[sgR[i]])
                    S.op("dve", lambda e: e.tensor_tensor(out=hb[o][:], in0=pu[:], in1=sg[i][:], op=ALU.mult), rd=[puR, sgR[i]], wr=[hbR[o]])
                    S.dma("pool", HT[f, :, t0:t0 + 512], hb[o][:], rd=[hbR[o]])
            S.barrier()

    def phase6(l, DST):
        with ExitStack() as st:
            wd, wdR = load_w(st, "wd", W["w_down"][l], DFF, D)
            gam, gamR = load_gamma(st, "g6", W["ln_ffn_post"][l])
            xt = [sb(st, "xt%d" % i, [128, 4, D], F32) for i in range(2)]
            xtR = [Res() for _ in range(2)]
            hT = [sb(st, "hT%d" % i, [128, NFF, 512], BF16) for i in range(2)]
            hTR = [Res() for _ in range(2)]
            ss = sb(st, "ss", [128, 8], F32)
            ssR = Res()
            junk = sb(st, "junk", [128, D], F32)
            junkR = Res()
            tmp = sb(st, "tmp", [128, D], F32)
            tmpR = Res()

            def ld(ti):
                b = ti % 2
                t0 = tiles[ti][0]
                S.dma("sp", xt[b][:], xview(XR, t0), wr=[xtR[b]])
                S.dma("sp", hT[b][:], HT[:, :, t0:t0 + 512].rearrange("c p t -> p c t"), wr=[hTR[b]])
            ld(0)
            for ti, (t0, si, p0) in enumerate(tiles):
                b = ti % 2
                if ti + 1 < len(tiles):
                    ld(ti + 1)
                for j in range(4):
                    ps2, ps2R = tok_major_proj(hT[b], hTR[b], NFF, wd, wdR, j)
                    postnorm_residual(ps2, ps2R, gam, gamR, xt[b][:, j, :], xtR[b], ss, ssR, junk, junkR, tmp, tmpR)
                S.dma("pool", xview(DST, t0), xt[b][:], rd=[xtR[b]])
            S.barrier()

    S.barrier()
    for l in range(depth):
        src = X if l == 0 else XR
        for nm, fn in (("p1", lambda: phase1(l, src)), ("p2", lambda: phase2(l)), ("p3", lambda: phase3(l)),
                       ("p4a", lambda: phase4a(l, src)), ("p4b", lambda: phase4b(l)), ("p5", lambda: phase5(l)),
                       ("p6", lambda: phase6(l, Y if l == depth - 1 else XR))):
            if only is None or nm in only or (nm == 'p3' and any(k.startswith('p3') for k in only)):
                fn()
    S.barrier()
    es.close()
    return nc, consts


_CACHE = {}


def _core_inputs(core, x_prompt, x_sample, mem_prompt, mem_sample, weights, consts):
    xs = x_sample[core].reshape(-1, D)
    xp = x_prompt[4 * core:4 * core + 4].reshape(-1, D)
    m = {"x": np.ascontiguousarray(np.concatenate([xs, xp], 0)),
         "mem": np.ascontiguousarray(np.concatenate([mem_sample[core], mem_prompt[4 * core:4 * core + 4].reshape(-1, D)], 0))}
    for n, _ in WSPEC:
        m[n] = weights[n]
    for k, v in consts.items():
        m["c_" + k] = v
    return m


def kernel(**inputs):
    inputs = {k: np.asarray(v) for k, v in inputs.items()}
    SEQS = [8192, 2048, 2048, 2048, 2048]
    if "nc" not in _CACHE:
        _CACHE["nc"] = build(SEQS, DEPTH)
    nc, consts = _CACHE["nc"]
    weights = {n: np.ascontiguousarray(inputs[n], dtype=np.float32) for n, _ in WSPEC}
    in_maps = [_core_inputs(c, inputs["x_prompt"], inputs["x_sample"], inputs["mem_prompt"], inputs["mem_sample"],
                            weights, consts) for c in range(8)]
    res = run_bass_kernel_spmd(nc, in_maps, core_ids=list(range(8)))
    y_prompt = np.empty((32, 2048, D), np.float32)
    y_sample = np.empty((8, 8192, D), np.float32)
    for c in range(8):
        y = res.results[c]["y"]
        y_sample[c] = y[:8192]
        y_prompt[4 * c:4 * c + 4] = y[8192:].reshape(4, 2048, D)
    return (y_prompt, y_sample)
```

```python
import math
from contextlib import ExitStack
import numpy as np
import ml_dtypes
import concourse.bass as bass
import concourse.mybir as mybir
from concourse.bass_utils import run_bass_kernel_spmd

F32, BF16 = mybir.dt.float32, mybir.dt.bfloat16
AF = mybir.ActivationFunctionType
ALU = mybir.AluOpType
AX = mybir.AxisListType

D = 1024
DEPTH = 4
DFF = 2816
NFF = 22
NMEM = 256
EPS = 1e-6
BF = ml_dtypes.bfloat16


class Res:
    __slots__ = ("w", "r")

    def __init__(self):
        self.w = None
        self.r = {}


class Sched:
    NDS = 12

    def __init__(self, nc, es):
        self.nc = nc
        self.eng = {"pe": nc.tensor, "act": nc.scalar, "dve": nc.vector, "pool": nc.gpsimd, "sp": nc.sync}
        self.sem = {}
        self.cnt = {}
        self.waited = {}
        for k in self.eng:
            self.sem[k] = es.enter_context(nc.semaphore("s_" + k))
            self.cnt[k] = 0
            self.waited[k] = {}
        self.dq = {}
        for q in ("sp", "pool", "act"):
            sems = []
            for i in range(self.NDS):
                key = ("d", q, i)
                self.sem[key] = es.enter_context(nc.semaphore("d_%s_%d" % (q, i)))
                self.cnt[key] = 0
                sems.append(key)
            self.dq[q] = [sems, 0]

    def _deps(self, rd, wr):
        deps = {}

        def add(ev):
            if ev is not None and deps.get(ev[0], 0) < ev[1]:
                deps[ev[0]] = ev[1]

        for r in rd:
            add(r.w)
        for r in wr:
            add(r.w)
            for k, v in r.r.items():
                add((k, v))
        return deps

    def _wait(self, e, deps):
        w = self.waited[e]
        for k, v in deps.items():
            if k == e and e == "pe":
                continue
            if w.get(k, 0) < v:
                self.eng[e].wait_ge(self.sem[k], v)
                w[k] = v

    def _mark(self, ev, rd, wr):
        for r in rd:
            if r.r.get(ev[0], 0) < ev[1]:
                r.r[ev[0]] = ev[1]
        for r in wr:
            r.w = ev
            r.r = {}

    def op(self, e, fn, rd=(), wr=()):
        self._wait(e, self._deps(rd, wr))
        ins = fn(self.eng[e])
        self.cnt[e] += 1
        ins.then_inc(self.sem[e], 1)
        self._mark((e, self.cnt[e]), rd, wr)

    def dma(self, q, out, in_, rd=(), wr=()):
        sems, n = self.dq[q]
        key = sems[n % self.NDS]
        self.dq[q][1] = n + 1
        deps = self._deps(rd, wr)
        if self.cnt[key] > 0 and deps.get(key, 0) < self.cnt[key]:
            deps[key] = self.cnt[key]
        self._wait(q, deps)
        self.eng[q].dma_start(out=out, in_=in_).then_inc(self.sem[key], 16)
        self.cnt[key] += 16
        self._mark((key, self.cnt[key]), rd, wr)

    def barrier(self, engines=("pe", "act", "dve", "pool", "sp")):
        allv = {k: v for k, v in self.cnt.items() if v > 0}
        for e in engines:
            self._wait(e, allv)


def _rope_tables():
    inv = (np.float32(500000.0) ** (-np.arange(0, 16, 2, dtype=np.float32) / np.float32(16))).astype(np.float32)
    pos = np.arange(8192, dtype=np.float32)
    ang = (pos[:, None] * inv[None, :]).astype(np.float32)
    cos, sin = np.cos(ang).astype(np.float32), np.sin(ang).astype(np.float32)
    C = np.ones((128, 8192), np.float32)
    S = np.zeros((128, 8192), np.float32)
    R = np.zeros((128, 128), np.float32)
    for p in range(128):
        d = p % 64
        if d < 8:
            C[p] = cos[:, d]
            S[p] = -sin[:, d]
            R[p + 8, p] = 1
        elif d < 16:
            C[p] = cos[:, d - 8]
            S[p] = sin[:, d - 8]
            R[p - 8, p] = 1
    return C, S, R


def _fft_tables(N1):
    N = 128 * N1
    H = N1 // 2
    t = {}
    n1 = np.arange(N1)[:, None]
    k1 = np.arange(N1)[None, :]
    a = -2 * np.pi * n1 * k1 / N1
    t["f1"] = np.concatenate([np.cos(a), np.sin(a)], 1).astype(BF)
    n2 = np.arange(128)[:, None]
    a = -2 * np.pi * n2 * k1 / N
    t["twr"] = np.concatenate([np.cos(a), np.cos(a)], 1).astype(np.float32)
    t["twi"] = np.concatenate([np.sin(a), np.sin(a)], 1).astype(np.float32)
    a = 2 * np.pi * np.arange(N1)[:, None] * np.arange(128)[None, :] / N
    t["itwr"] = np.concatenate([np.cos(a), np.cos(a)], 1).astype(np.float32)
    t["itwi"] = np.concatenate([np.sin(a), np.sin(a)], 1).astype(np.float32)
    a = 2 * np.pi * np.arange(N1)[:, None] * np.arange(H)[None, :] / N1
    t["g1"] = np.concatenate([np.cos(a) / N, -np.sin(a) / N], 1).astype(BF)
    return t


def _fft128_tables():
    a = -2 * np.pi * np.arange(128)[:, None] * np.arange(128)[None, :] / 128
    f2 = np.concatenate([np.cos(a), np.sin(a), -np.sin(a)], 1).astype(BF)
    b = -a
    gg = np.concatenate([np.cos(b), np.sin(b), -np.sin(b), np.cos(b)], 1).astype(BF)
    return f2, gg


def _filter_tables(L):
    t = np.linspace(0.0, 1.0, L, dtype=np.float32)
    w = (np.float32(2.0 * math.pi) * np.arange(L, dtype=np.float32) / np.float32(L)).astype(np.float32)
    f = np.linspace(1e-4, 15, 16, dtype=np.float32)
    fw = (f[None, :] * w[:, None]).astype(np.float32)
    z = np.concatenate([t[:, None], np.cos(fw), -np.sin(fw)], 1).astype(np.float32)
    idx = np.concatenate([np.arange(L), [0], np.arange(L - 1, 0, -1)])
    zt = np.ascontiguousarray(z[idx].T)
    tn = t[idx].copy()
    tn[L] = 1e4
    return zt, tn[None, :].astype(np.float32)


def _consts():
    c = {}
    C, S, R = _rope_tables()
    c["ropec"], c["ropes"] = C, S
    f2, gg = _fft128_tables()
    ident = np.eye(128, dtype=np.float32)
    c["cb"] = np.concatenate([ident.astype(BF), np.ones((128, 128), BF), R.astype(BF), f2, gg], 1)
    for N1 in (32, 128):
        for k, v in _fft_tables(N1).items():
            c["%s_%d" % (k, N1)] = v
    g1 = c["g1_32"].astype(np.float32)
    g1p = np.zeros((128, 4, g1.shape[1]), np.float32)
    for c4 in range(4):
        g1p[c4 * 32:(c4 + 1) * 32, c4, :] = g1
    c["g1p_32"] = g1p.astype(BF)
    c["itwr4_32"] = np.tile(c["itwr_32"], (4, 1))
    c["itwi4_32"] = np.tile(c["itwi_32"], (4, 1))
    for L in (2048, 8192):
        zt, tn = _filter_tables(L)
        c["zt_%d" % L] = zt
        c["tn_%d" % L] = tn
    mn, mx = math.log(1e-2) / 1.5, math.log(1e-2) / 0.3
    c["negdelta"] = (-np.abs(np.linspace(mn, mx, 512, dtype=np.float32)))[None, :].astype(np.float32)
    return c


WSPEC = [("ln_mix_pre", (D,)), ("ln_mix_post", (D,)), ("w_in", (D, 3072)), ("lambda_q1", (64,)), ("lambda_k1", (64,)),
         ("lambda_q2", (64,)), ("lambda_k2", (64,)), ("subln_g", (128,)), ("conv_w", (3, 1536)), ("conv_b", (1536,)),
         ("filt_w1", (33, 64)), ("filt_b1", (64,)), ("filt_freq1", (64,)), ("filt_w2", (64, 64)), ("filt_b2", (64,)),
         ("filt_freq2", (64,)), ("filt_w3", (64, 1024)), ("hyena_d", (512,)), ("w_out", (D, D)), ("ln_x_pre", (D,)),
         ("ln_x_post", (D,)), ("ln_mem", (D,)), ("wq_x", (D, D)), ("wk_x", (D, D)), ("wv_x", (D, D)), ("wo_x", (D, D)),
         ("ln_ffn_pre", (D,)), ("ln_ffn_post", (D,)), ("w_gate", (D, DFF)), ("w_up", (D, DFF)), ("w_down", (DFF, D))]


def build(SEQS, depth=DEPTH, dbg=(), only=None):
    NT = sum(SEQS)
    NS = len(SEQS)
    LT = sorted(set(SEQS))
    consts = _consts()
    nc = bass.Bass("TRN2", target_bir_lowering=False)

    def din(name, shape, dt=F32):
        return nc.dram_tensor(name, list(shape), dt, kind="ExternalInput").ap()

    def dscr(name, shape, dt, out=False):
        kind = "ExternalOutput" if (out or name in dbg) else "Internal"
        return nc.dram_tensor(name, list(shape), dt, kind=kind).ap()

    X = din("x", [NT, D])
    MEM = din("mem", [NS * NMEM, D])
    W = {n: din(n, (depth,) + s) for n, s in WSPEC}
    CT = {}
    for k, v in consts.items():
        CT[k] = din("c_" + k, v.shape, BF16 if v.dtype == BF else F32)
    Y = dscr("y", [NT, D], F32, out=True)
    XR = dscr("xr", [NT, D], F32)
    QT = dscr("qt", [4, 128, NT], BF16)
    KT = dscr("kt", [4, 128, NT], BF16)
    V = dscr("v", [NT, 512], BF16)
    HY = dscr("hy", [12, 128, NT], F32)
    MIX = dscr("mix", [8, 128, NT], BF16)
    U = dscr("u", [512, NT], BF16)
    UX = dscr("ux", [4, 128, NT], F32)
    X0 = dscr("x0", [4, 128, NT], F32)
    YC = dscr("yc", [512, NT], F32)
    HT = dscr("ht", [NFF, 128, NT], BF16)
    KERN = {L: dscr("kern%d" % L, [512, 2 * L], BF16) for L in LT}
    KF = {L: dscr("kf%d" % L, [128, 512, 2, L // 64], BF16) for L in LT}

    es = ExitStack()
    S = Sched(nc, es)
    PS = [es.enter_context(nc.psum_tensor("ps%d" % i, [128, 512], F32)) for i in range(8)]
    PSR = [Res() for _ in range(8)]
    psn = [0]

    def psum():
        i = psn[0] % 8
        psn[0] += 1
        return PS[i], PSR[i]

    uid = [0]

    def sb(st, name, shape, dt):
        uid[0] += 1
        return st.enter_context(nc.sbuf_tensor("%s_%d" % (name, uid[0]), list(shape), dt))

    cb = sb(es, "cb", [128, 1280], BF16)
    cbR = Res()
    S.dma("sp", cb[:], CT["cb"][:, :], wr=[cbR])
    ident, ones, rmat = cb[:, 0:128], cb[:, 128:256], cb[:, 256:384]
    f2re, f2im, f2imn = cb[:, 384:512], cb[:, 512:640], cb[:, 640:768]
    gg1, gg2 = cb[:, 768:1024], cb[:, 1024:1280]
    rnorm = {L: sb(es, "rnorm%d" % L, [128, 4], F32) for L in LT}
    rnormR = {L: Res() for L in LT}
    nlam = sb(es, "nlam", [128, 1], F32)
    nlamR = Res()
    epsc = sb(es, "epsc", [128, 1], F32)
    S.op("pool", lambda e: e.memset(epsc[:], EPS), wr=[cbR])

    def bcast_row(ap1d, n):
        return ap1d.partition_broadcast(128)

    def col(ap1d):
        return ap1d.rearrange("(p o) -> p o", o=1)

    def load_w(st, name, ap2d, rows, cols, q="pool"):
        kc = rows // 128
        t = sb(st, name, [128, kc, cols], BF16)
        r = Res()
        for k in range(kc):
            S.dma(q, t[:, k, :], ap2d[k * 128:(k + 1) * 128, :], wr=[r])
        return t, r

    def load_gamma(st, name, ap1d):
        t = sb(st, name, [128, D], F32)
        r = Res()
        S.dma("sp", t[:], bcast_row(ap1d, D), wr=[r])
        return t, r

    def rms_rstd(st_small, src_ap, rd, n, ss, ssR, junk, junkR, idx):
        S.op("act", lambda e: e.activation(out=junk[:], in_=src_ap, func=AF.Square, accum_out=ss[:, idx:idx + 1]),
             rd=rd, wr=[junkR, ssR])

    def finish_rstd(ss, ssR, n, width):
        S.op("dve", lambda e: e.tensor_scalar(out=ss[:, 0:width], in0=ss[:, 0:width], scalar1=1.0 / n, scalar2=EPS,
                                              op0=ALU.mult, op1=ALU.add), rd=[ssR], wr=[ssR])
        S.op("act", lambda e: e.activation(out=ss[:, 0:width], in_=ss[:, 0:width], func=AF.Sqrt), rd=[ssR], wr=[ssR])
        S.op("dve", lambda e: e.reciprocal(out=ss[:, 0:width], in_=ss[:, 0:width]), rd=[ssR], wr=[ssR])

    def norm_transpose(xt, xtR, nj, gam, gamR, xn, xnR, xnT, xnTR, ss, ssR, junk, junkR):
        for j in range(nj):
            rms_rstd(None, xt[:, j, :], [xtR], D, ss, ssR, junk, junkR, j)
        finish_rstd(ss, ssR, D, nj)
        for j in range(nj):
            S.op("dve", lambda e: e.scalar_tensor_tensor(out=xn[:, j, :], in0=xt[:, j, :], scalar=ss[:, j:j + 1],
                                                         in1=gam[:], op0=ALU.mult, op1=ALU.mult),
                 rd=[xtR, ssR, gamR], wr=[xnR[j]])
        for j in range(nj):
            p, pR = psum()
            pb = p[:].bitcast(BF16)

            def tr(e):
                for c in range(8):
                    ins = e.transpose(pb[:, c * 128:(c + 1) * 128], xn[:, j, c * 128:(c + 1) * 128], ident)
                return ins
            S.op("pe", tr, rd=[xnR[j], cbR], wr=[pR])
            eng = "act" if j % 2 == 0 else "dve"
            src = pb.rearrange("p (c t) -> p c t", c=8)
            if eng == "act":
                S.op("act", lambda e: e.activation(out=xnT[:, :, j * 128:(j + 1) * 128], in_=src, func=AF.Copy),
                     rd=[pR], wr=[xnTR])
            else:
                S.op("dve", lambda e: e.tensor_copy(out=xnT[:, :, j * 128:(j + 1) * 128], in_=src), rd=[pR], wr=[xnTR])

    def mm_acc(e, out, pairs):
        n = len(pairs)
        for i, (l, r) in enumerate(pairs):
            ins = e.matmul(out, l, r, start=(i == 0), stop=(i == n - 1))
        return ins

    def postnorm_residual(ps2, ps2R, gam, gamR, xt_j, xtR, ss, ssR, junk, junkR, tmp, tmpR):
        for h in range(2):
            S.op("act", lambda e: e.activation(out=junk[:, 0:512], in_=ps2[h][:], func=AF.Square,
                                               accum_out=ss[:, h:h + 1]), rd=[ps2R[h]], wr=[junkR, ssR])
        S.op("dve", lambda e: e.tensor_tensor(out=ss[:, 2:3], in0=ss[:, 0:1], in1=ss[:, 1:2], op=ALU.add),
             rd=[ssR], wr=[ssR])
        S.op("dve", lambda e: e.tensor_scalar(out=ss[:, 2:3], in0=ss[:, 2:3], scalar1=1.0 / D, scalar2=EPS,
                                              op0=ALU.mult, op1=ALU.add), rd=[ssR], wr=[ssR])
        S.op("act", lambda e: e.activation(out=ss[:, 2:3], in_=ss[:, 2:3], func=AF.Sqrt), rd=[ssR], wr=[ssR])
        S.op("dve", lambda e: e.reciprocal(out=ss[:, 2:3], in_=ss[:, 2:3]), rd=[ssR], wr=[ssR])
        for h in range(2):
            S.op("dve", lambda e: e.scalar_tensor_tensor(out=tmp[:, h * 512:(h + 1) * 512], in0=ps2[h][:],
                                                         scalar=ss[:, 2:3], in1=gam[:, h * 512:(h + 1) * 512],
                                                         op0=ALU.mult, op1=ALU.mult),
                 rd=[ps2R[h], ssR, gamR], wr=[tmpR])
        S.op("pool", lambda e: e.tensor_tensor(out=xt_j, in0=xt_j, in1=tmp[:], op=ALU.add), rd=[tmpR, xtR], wr=[xtR])

    seq_off = [sum(SEQS[:i]) for i in range(NS)]
    tiles = []
    for si, L in enumerate(SEQS):
        for t in range(L // 512):
            tiles.append((seq_off[si] + t * 512, si, t * 512))

    def xview(ap, t0):
        return ap[t0:t0 + 512, :].rearrange("(j p) d -> p j d", p=128)

    def phase1(l, SRC):
        with ExitStack() as st:
            win, winR = load_w(st, "win", W["w_in"][l], D, 3072)
            gam, gamR = load_gamma(st, "g1", W["ln_mix_pre"][l])
            xt = [sb(st, "xt%d" % i, [128, 4, D], F32) for i in range(2)]
            xtR = [Res() for _ in range(2)]
            xn2 = [sb(st, "xn%d" % i, [128, 4, D], BF16) for i in range(2)]
            xnR2 = [[Res() for _ in range(4)] for i in range(2)]
            xnT2 = [sb(st, "xnT%d" % i, [128, 8, 512], BF16) for i in range(2)]
            xnTR2 = [Res(), Res()]
            ss = sb(st, "ss", [128, 8], F32)
            ssR = Res()
            junk = sb(st, "junk", [128, D], F32)
            junkR = Res()
            rc = [sb(st, "rc%d" % i, [128, 2, 512], F32) for i in range(2)]
            rcR = [Res() for _ in range(2)]
            qsb = [sb(st, "qsb%d" % i, [128, 512], BF16) for i in range(2)]
            qsbR = [Res() for _ in range(2)]
            t1 = [sb(st, "t1%d" % i, [128, 512], F32) for i in range(2)]
            t1R = [Res() for _ in range(2)]
            t2 = [sb(st, "t2%d" % i, [128, 512], F32) for i in range(2)]
            t2R = [Res() for _ in range(2)]
            qr = [sb(st, "qr%d" % i, [128, 512], BF16) for i in range(3)]
            qrR = [Res() for _ in range(3)]
            hs = [sb(st, "hs%d" % i, [128, 512], F32) for i in range(3)]
            hsR = [Res() for _ in range(3)]
            cn = [0, 0, 0]
            S.dma("sp", xt[0][:], xview(SRC, tiles[0][0]), wr=[xtR[0]])
            norm_transpose(xt[0], xtR[0], 4, gam, gamR, xn2[0], xnR2[0], xnT2[0], xnTR2[0], ss, ssR, junk, junkR)
            for ti, (t0, si, p0) in enumerate(tiles):
                b = ti % 2
                xnT, xnTR = xnT2[b], xnTR2[b]
                if ti + 1 < len(tiles):
                    S.dma("sp", xt[1 - b][:], xview(SRC, tiles[ti + 1][0]), wr=[xtR[1 - b]])
                S.dma("sp", rc[b][:, 0, :], CT["ropec"][:, p0:p0 + 512], wr=[rcR[b]])
                S.dma("sp", rc[b][:, 1, :], CT["ropes"][:, p0:p0 + 512], wr=[rcR[b]])
                for ch in range(8):
                    p, pR = psum()
                    S.op("pe", lambda e: mm_acc(e, p[:], [(win[:, k, ch * 128:(ch + 1) * 128], xnT[:, k, :])
                                                          for k in range(8)]), rd=[winR, xnTR], wr=[pR])
                    i = cn[0] % 2
                    cn[0] += 1
                    S.op("act", lambda e: e.activation(out=qsb[i][:], in_=p[:], func=AF.Copy), rd=[pR], wr=[qsbR[i]])
                    p2, p2R = psum()
                    S.op("pe", lambda e: e.matmul(p2[:], rmat, qsb[i][:], start=True, stop=True),
                         rd=[qsbR[i], cbR], wr=[p2R])
                    S.op("dve", lambda e: e.tensor_tensor(out=t1[i][:], in0=qsb[i][:], in1=rc[b][:, 0, :], op=ALU.mult),
                         rd=[qsbR[i], rcR[b]], wr=[t1R[i]])
                    S.op("dve", lambda e: e.tensor_tensor(out=t2[i][:], in0=p2[:], in1=rc[b][:, 1, :], op=ALU.mult),
                         rd=[p2R, rcR[b]], wr=[t2R[i]])
                    o = cn[1] % 3
                    cn[1] += 1
                    S.op("pool", lambda e: e.tensor_tensor(out=qr[o][:], in0=t1[i][:], in1=t2[i][:], op=ALU.add),
                         rd=[t1R[i], t2R[i]], wr=[qrR[o]])
                    dst = (QT if ch < 4 else KT)[ch % 4, :, t0:t0 + 512]
                    S.dma("pool", dst, qr[o][:], rd=[qrR[o]])
                if ti + 1 < len(tiles):
                    norm_transpose(xt[1 - b], xtR[1 - b], 4, gam, gamR, xn2[1 - b], xnR2[1 - b], xnT2[1 - b], xnTR2[1 - b],
                                   ss, ssR, junk, junkR)
                for j in range(4):
                    p, pR = psum()
                    S.op("pe", lambda e: mm_acc(e, p[:], [(xnT[:, k, j * 128:(j + 1) * 128], win[:, k, 1024:1536])
                                                          for k in range(8)]), rd=[winR, xnTR], wr=[pR])
                    o = cn[1] % 3
                    cn[1] += 1
                    S.op("act", lambda e: e.activation(out=qr[o][:], in_=p[:], func=AF.Copy), rd=[pR], wr=[qrR[o]])
                    S.dma("pool", V[t0 + j * 128:t0 + (j + 1) * 128, :], qr[o][:], rd=[qrR[o]])
                for ch in range(12):
                    p, pR = psum()
                    c0 = 1536 + ch * 128
                    S.op("pe", lambda e: mm_acc(e, p[:], [(win[:, k, c0:c0 + 128], xnT[:, k, :]) for k in range(8)]),
                         rd=[winR, xnTR], wr=[pR])
                    o = cn[2] % 3
                    cn[2] += 1
                    if ch % 2 == 0:
                        S.op("act", lambda e: e.activation(out=hs[o][:], in_=p[:], func=AF.Copy), rd=[pR], wr=[hsR[o]])
                    else:
                        S.op("dve", lambda e: e.tensor_copy(out=hs[o][:], in_=p[:]), rd=[pR], wr=[hsR[o]])
                    S.dma("pool", HY[ch, :, t0:t0 + 512], hs[o][:], rd=[hsR[o]])
            S.barrier()

    def lam_compute(l):
        lam_init = 0.8 - 0.6 * math.exp(-0.3 * l)
        with ExitStack() as st:
            lt = sb(st, "lamt", [128, 4, 64], F32)
            ltR = Res()
            for i, n in enumerate(("lambda_q1", "lambda_k1", "lambda_q2", "lambda_k2")):
                S.dma("sp", lt[:, i, :], bcast_row(W[n][l], 64), wr=[ltR])
            pr = sb(st, "lampr", [128, 2, 64], F32)
            prR = Res()
            sm = sb(st, "lamsm", [128, 2], F32)
            smR = Res()
            for i in range(2):
                S.op("dve", lambda e: e.tensor_tensor(out=pr[:, i, :], in0=lt[:, 2 * i, :], in1=lt[:, 2 * i + 1, :],
                                                      op=ALU.mult), rd=[ltR], wr=[prR])
            S.op("dve", lambda e: e.tensor_reduce(out=sm[:], in_=pr[:], axis=AX.X, op=ALU.add), rd=[prR], wr=[smR])
            S.op("act", lambda e: e.activation(out=sm[:], in_=sm[:], func=AF.Exp), rd=[smR], wr=[smR])
            S.op("dve", lambda e: e.scalar_tensor_tensor(out=nlam[:], in0=sm[:, 1:2], scalar=-lam_init, in1=sm[:, 0:1],
                                                         op0=ALU.add, op1=ALU.subtract), rd=[smR], wr=[nlamR])
            S.barrier()
        return lam_init

    def phase2(l):
        lam_init = lam_compute(l)
        with ExitStack() as st:
            gs = sb(st, "gsub", [128, 1], F32)
            gsR = Res()
            S.dma("sp", gs[:], col(W["subln_g"][l]), wr=[gsR])
            S.op("dve", lambda e: e.tensor_scalar(out=gs[:], in0=gs[:], scalar1=1.0 - lam_init, scalar2=None,
                                                  op0=ALU.mult), rd=[gsR], wr=[gsR])
            LM = max(SEQS)
            ksb = [[sb(st, "ksb%d_%d" % (i, c), [128, LM], BF16) for c in range(2)] for i in range(2)]
            ksbR = [Res() for _ in range(2)]
            for i in range(2):
                S.op("pool", lambda e: e.memset(ksb[i][0][64:128, :], 0.0), wr=[ksbR[i]])
                S.op("pool", lambda e: e.memset(ksb[i][1][0:64, :], 0.0), wr=[ksbR[i]])
            lz = [sb(st, "lz%d" % i, [128, 512], F32) for i in range(2)]
            lzR = [Res() for _ in range(2)]
            vsb = [sb(st, "vsb%d" % i, [128, LM // 128, 128], BF16) for i in range(2)]
            vsbR = [Res() for _ in range(2)]
            qsb = [sb(st, "q2sb%d" % i, [128, 512], BF16) for i in range(2)]
            qsbR = [Res() for _ in range(2)]
            NE = 8
            esb = [sb(st, "esb%d" % i, [128, 512], BF16) for i in range(NE)]
            esbR = [Res() for _ in range(NE)]
            rz = [sb(st, "rz%d" % i, [128, 512], F32) for i in range(2)]
            rzR = [Res() for _ in range(2)]
            tt = [sb(st, "tt%d" % i, [128, 512], F32) for i in range(2)]
            ttR = [Res() for _ in range(2)]
            osb = sb(st, "osb", [128, 512], F32)
            osbR = Res()
            sq = sb(st, "sq", [128, 512], BF16)
            sqR = Res()
            rs = sb(st, "rs", [128, 512], F32)
            rsR = Res()
            ob = [sb(st, "ob%d" % i, [128, 512], BF16) for i in range(2)]
            obR = [Res() for _ in range(2)]
            hn = 0
            qn = 0
            en = 0
            for si, L in enumerate(SEQS):
                s0 = seq_off[si]
                for h in range(4):
                    kb = hn % 2
                    hn += 1
                    S.dma("sp", ksb[kb][0][0:64, 0:L], KT[h, 0:64, s0:s0 + L], wr=[ksbR[kb]])
                    S.dma("sp", ksb[kb][1][64:128, 0:L], KT[h, 64:128, s0:s0 + L], wr=[ksbR[kb]])
                    S.dma("sp", vsb[kb][:, 0:L // 128, :],
                          V[s0:s0 + L, h * 128:(h + 1) * 128].rearrange("(kc p) e -> p kc e", p=128), wr=[vsbR[kb]])
                    for qt in range(L // 512):
                        t0 = s0 + qt * 512
                        qb = qn % 2
                        qn += 1
                        S.dma("sp", qsb[qb][:], QT[h, :, t0:t0 + 512], wr=[qsbR[qb]])
                        acc = [(PS[i], PSR[i]) for i in range(4)]
                        nk = L // 128
                        items = [(kc, c) for kc in range(nk) for c in range(2)]
                        LA = 3
                        einfo = {}
                        for idx in range(len(items) + LA):
                            if idx < len(items):
                                kc, c = items[idx]
                                sp_, spR = PS[4 + idx % 4], PSR[4 + idx % 4]
                                S.op("pe", lambda e: e.matmul(sp_[:], ksb[kb][c][:, kc * 128:(kc + 1) * 128],
                                                              qsb[qb][:], start=True, stop=True),
                                     rd=[ksbR[kb], qsbR[qb]], wr=[spR])
                                ei = en % NE
                                en += 1
                                S.op("act", lambda e: e.activation(out=esb[ei][:], in_=sp_[:], func=AF.Exp, scale=0.125),
                                     rd=[spR], wr=[esbR[ei]])
                                einfo[idx] = ei
                            if idx >= LA:
                                kc, c = items[idx - LA]
                                ei = einfo.pop(idx - LA)

                                def pv(e):
                                    e.matmul(acc[c][0][:], vsb[kb][:, kc, :], esb[ei][:], start=(kc == 0), stop=(kc == nk - 1))
                                    return e.matmul(acc[2 + c][0][:], ones, esb[ei][:], start=(kc == 0), stop=(kc == nk - 1))
                                S.op("pe", pv, rd=[vsbR[kb], esbR[ei], cbR], wr=[acc[c][1], acc[2 + c][1]])
                        for c in range(2):
                            S.op("dve", lambda e: e.tensor_copy(out=tt[c][:], in_=acc[c][0][:]), rd=[acc[c][1]], wr=[ttR[c]])
                            S.op("act", lambda e: e.activation(out=lz[c][:], in_=acc[2 + c][0][:], func=AF.Ln), rd=[acc[2 + c][1]], wr=[lzR[c]])
                        for c in range(2):
                            S.op("act", lambda e: e.activation(out=rz[c][:], in_=lz[c][:], func=AF.Exp, scale=-1.0), rd=[lzR[c]], wr=[rzR[c]])
                            S.op("dve", lambda e: e.tensor_tensor(out=tt[c][:], in0=tt[c][:], in1=rz[c][:], op=ALU.mult),
                                 rd=[ttR[c], rzR[c]], wr=[ttR[c]])
                        S.op("dve", lambda e: e.scalar_tensor_tensor(out=osb[:], in0=tt[1][:], scalar=nlam[:, 0:1], in1=tt[0][:],
                                                                     op0=ALU.mult, op1=ALU.add),
                             rd=[ttR[0], ttR[1], nlamR], wr=[osbR])
                        S.op("pool", lambda e: e.tensor_tensor(out=sq[:], in0=osb[:], in1=osb[:], op=ALU.mult), rd=[osbR], wr=[sqR])
                        psq, psqR = PS[4], PSR[4]
                        S.op("pe", lambda e: e.matmul(psq[:], ones, sq[:], start=True, stop=True), rd=[sqR, cbR], wr=[psqR])
                        S.op("act", lambda e: e.activation(out=rs[:], in_=psq[:], func=AF.Ln, scale=1.0 / 128, bias=epsc[:, 0:1]), rd=[psqR], wr=[rsR])
                        S.op("act", lambda e: e.activation(out=rs[:], in_=rs[:], func=AF.Exp, scale=-0.5), rd=[rsR], wr=[rsR])
                        oi = qn % 2
                        S.op("dve", lambda e: e.scalar_tensor_tensor(out=ob[oi][:], in0=osb[:], scalar=gs[:, 0:1], in1=rs[:],
                                                                     op0=ALU.mult, op1=ALU.mult),
                             rd=[osbR, rsR, gsR], wr=[obR[oi]])
                        S.dma("pool", MIX[h, :, t0:t0 + 512], ob[oi][:], rd=[obR[oi]])
            S.barrier()

    def sin_big(st, a, aR, n, tag):
        s4 = sb(st, "s4" + tag, [64, n], F32)
        s8 = sb(st, "s8" + tag, [64, n], F32)
        r4, r8 = Res(), Res()
        S.op("act", lambda e: e.activation(out=s4[:], in_=a, func=AF.Sin, scale=0.25), rd=[aR], wr=[r4])
        S.op("act", lambda e: e.activation(out=s8[:], in_=a, func=AF.Sin, scale=0.125), rd=[aR], wr=[r8])
        S.op("dve", lambda e: e.tensor_tensor(out=s8[:], in0=s8[:], in1=s8[:], op=ALU.mult), rd=[r8], wr=[r8])
        S.op("dve", lambda e: e.tensor_scalar(out=s8[:], in0=s8[:], scalar1=-2.0, scalar2=1.0, op0=ALU.mult, op1=ALU.add),
             rd=[r8], wr=[r8])
        S.op("dve", lambda e: e.tensor_tensor(out=s8[:], in0=s8[:], in1=s4[:], op=ALU.mult), rd=[r8, r4], wr=[r8])
        S.op("dve", lambda e: e.tensor_tensor(out=s4[:], in0=s4[:], in1=s4[:], op=ALU.mult), rd=[r4], wr=[r4])
        S.op("dve", lambda e: e.tensor_scalar(out=s4[:], in0=s4[:], scalar1=-2.0, scalar2=1.0, op0=ALU.mult, op1=ALU.add),
             rd=[r4], wr=[r4])
        S.op("dve", lambda e: e.scalar_tensor_tensor(out=a, in0=s8[:], scalar=4.0, in1=s4[:], op0=ALU.mult, op1=ALU.mult),
             rd=[r8, r4, aR], wr=[aR])

    def fft_fwd(st, L, src, c0, ncg, H, dst_kf=None, dst_sb=None, kfs=None, tabs=None):
        N1 = L // 64
        f1, twr, twi, ut, utR, pp, ppR, are, aim, aR = tabs
        S.dma("sp", ut[0:H, 0:ncg, :], src.rearrange("c (n1 n2) -> n1 c n2", n2=128), wr=[utR])
        cpb = 512 // (2 * N1)
        for g in range(ncg // cpb):
            p, pR = psum()
            pv = p[:].rearrange("p (c k) -> p c k", c=cpb)

            def s1(e):
                for c in range(cpb):
                    ins = e.matmul(pv[:, c, :], ut[0:H, g * cpb + c, :], f1[0:H, :], start=True, stop=True)
                return ins
            S.op("pe", s1, rd=[utR], wr=[pR])
            i = g % 2
            p1 = pp[i][:, 0, :].rearrange("p (c k) -> p c k", c=cpb)
            p2 = pp[i][:, 1, :].rearrange("p (c k) -> p c k", c=cpb)
            S.op("dve", lambda e: e.tensor_tensor(out=p1, in0=pv, in1=twr[:, None, :].to_broadcast([128, cpb, 2 * N1]),
                                                  op=ALU.mult), rd=[pR], wr=[ppR[i]])
            S.op("dve", lambda e: e.tensor_tensor(out=p2, in0=pv, in1=twi[:, None, :].to_broadcast([128, cpb, 2 * N1]),
                                                  op=ALU.mult), rd=[pR], wr=[ppR[i]])
            cs = slice(g * cpb, (g + 1) * cpb)
            S.op("pool", lambda e: e.tensor_tensor(out=are[:, cs, :], in0=p1[:, :, 0:N1], in1=p2[:, :, N1:2 * N1],
                                                   op=ALU.subtract), rd=[ppR[i]], wr=[aR])
            S.op("pool", lambda e: e.tensor_tensor(out=aim[:, cs, :], in0=p2[:, :, 0:N1], in1=p1[:, :, N1:2 * N1],
                                                   op=ALU.add), rd=[ppR[i]], wr=[aR])
        cpc = 512 // N1
        for g in range(ncg // cpc):
            cs = slice(g * cpc, (g + 1) * cpc)
            ar = are[:, cs, :].rearrange("p c k -> p (c k)")
            ai = aim[:, cs, :].rearrange("p c k -> p (c k)")
            pr_, prR = psum()
            pi_, piR = psum()
            S.op("pe", lambda e: mm_acc(e, pr_[:], [(f2re, ar), (f2imn, ai)]), rd=[aR, cbR], wr=[prR])
            S.op("pe", lambda e: mm_acc(e, pi_[:], [(f2im, ar), (f2re, ai)]), rd=[aR, cbR], wr=[piR])
            xr = pr_[:].rearrange("p (c k) -> p c k", c=cpc)
            xi = pi_[:].rearrange("p (c k) -> p c k", c=cpc)
            if dst_kf is not None:
                kst, kstR = dst_sb
                i = g % 2
                S.op("act", lambda e: e.activation(out=kst[i][:, 0:cpc, 0, :], in_=xr, func=AF.Copy), rd=[prR], wr=[kstR[i]])
                S.op("dve", lambda e: e.tensor_copy(out=kst[i][:, 0:cpc, 1, :], in_=xi), rd=[piR], wr=[kstR[i]])
                S.dma("pool", dst_kf[:, c0 + g * cpc:c0 + (g + 1) * cpc, :, :], kst[i][:, 0:cpc, :, :], rd=[kstR[i]])
            else:
                kf, kfR = kfs
                yre, yim, yR, mt, mtR = dst_sb
                kre = kf[:, cs, 0, :]
                kim = kf[:, cs, 1, :]
                i = g % 2
                m = [mt[i][:, q, :].rearrange("p (c k) -> p c k", c=cpc) for q in range(4)]
                S.op("dve", lambda e: e.tensor_tensor(out=m[0], in0=xr, in1=kre, op=ALU.mult), rd=[prR, kfR], wr=[mtR[i]])
                S.op("dve", lambda e: e.tensor_tensor(out=m[1], in0=xi, in1=kim, op=ALU.mult), rd=[piR, kfR], wr=[mtR[i]])
                S.op("dve", lambda e: e.tensor_tensor(out=m[2], in0=xr, in1=kim, op=ALU.mult), rd=[prR, kfR], wr=[mtR[i]])
                S.op("dve", lambda e: e.tensor_tensor(out=m[3], in0=xi, in1=kre, op=ALU.mult), rd=[piR, kfR], wr=[mtR[i]])
                S.op("pool", lambda e: e.tensor_tensor(out=yre[:, cs, :], in0=m[0], in1=m[1], op=ALU.subtract),
                     rd=[mtR[i]], wr=[yR])
                S.op("pool", lambda e: e.tensor_tensor(out=yim[:, cs, :], in0=m[2], in1=m[3], op=ALU.add),
                     rd=[mtR[i]], wr=[yR])

    def fft_tabs(st, L, ncg, H):
        N1 = L // 64
        f1 = sb(st, "f1", [N1, 2 * N1], BF16)
        twr = sb(st, "twr", [128, 2 * N1], F32)
        twi = sb(st, "twi", [128, 2 * N1], F32)
        tR = Res()
        S.dma("sp", f1[:], CT["f1_%d" % N1][:, :], wr=[tR])
        S.dma("sp", twr[:], CT["twr_%d" % N1][:, :], wr=[tR])
        S.dma("sp", twi[:], CT["twi_%d" % N1][:, :], wr=[tR])
        ut = sb(st, "ut", [N1, ncg, 128], BF16)
        pp = [sb(st, "pp%d" % i, [128, 2, 512], F32) for i in range(2)]
        are = sb(st, "are", [128, ncg, N1], BF16)
        aim = sb(st, "aim", [128, ncg, N1], BF16)
        S.barrier()
        return (f1, twr, twi, ut, Res(), pp, [Res(), Res()], are, aim, Res())

    def filters(l):
        for L in LT:
            N1 = L // 64
            with ExitStack() as st:
                w1 = sb(st, "fw1", [33, 64], F32)
                w2 = sb(st, "fw2", [64, 64], F32)
                w3 = sb(st, "fw3", [64, 1024], F32)
                pv = sb(st, "fpv", [64, 4], F32)
                nd = sb(st, "fnd", [1, 512], F32)
                wR = Res()
                S.dma("sp", w1[:], W["filt_w1"][l], wr=[wR])
                S.dma("sp", w2[:], W["filt_w2"][l], wr=[wR])
                S.dma("sp", w3[:], W["filt_w3"][l], wr=[wR])
                for i, n in enumerate(("filt_b1", "filt_freq1", "filt_b2", "filt_freq2")):
                    S.dma("sp", pv[:, i:i + 1], col(W[n][l]), wr=[wR])
                S.dma("sp", nd[:], CT["negdelta"][:, :], wr=[wR])
                nch = 2 * L // 512
                zt = [sb(st, "fzt%d" % i, [33, 512], F32) for i in range(2)]
                ztR = [Res(), Res()]
                tn = [sb(st, "ftn%d" % i, [1, 512], F32) for i in range(2)]
                a1 = sb(st, "fa1", [64, 512], F32)
                a1R = Res()
                a2 = sb(st, "fa2", [64, 512], F32)
                a2R = Res()
                wsb = [sb(st, "fws%d" % i, [128, 512], F32) for i in range(2)]
                wsbR = [Res(), Res()]
                kc_ = [sb(st, "fkc%d" % i, [128, 512], F32) for i in range(2)]
                kcR = [Res(), Res()]
                kb_ = [sb(st, "fkb%d" % i, [128, 512], BF16) for i in range(2)]
                kbR = [Res(), Res()]
                nrm = sb(st, "fnrm", [128, 4, nch], F32)
                nrmR = Res()
                n_ = 0
                for ci in range(nch):
                    b = ci % 2
                    S.dma("sp", zt[b][:], CT["zt_%d" % L][:, ci * 512:(ci + 1) * 512], wr=[ztR[b]])
                    S.dma("sp", tn[b][:], CT["tn_%d" % L][:, ci * 512:(ci + 1) * 512], wr=[ztR[b]])
                    p, pR = psum()
                    S.op("pe", lambda e: e.matmul(p[0:64, :], w1[:], zt[b][:], start=True, stop=True), rd=[wR, ztR[b]], wr=[pR])
                    S.op("dve", lambda e: e.tensor_scalar(out=a1[:], in0=p[0:64, :], scalar1=pv[:, 0:1], scalar2=pv[:, 1:2],
                                                          op0=ALU.add, op1=ALU.mult), rd=[pR, wR], wr=[a1R])
                    with ExitStack() as st2:
                        sin_big(st2, a1[:], a1R, 512, "a")
                        p, pR = psum()
                        S.op("pe", lambda e: e.matmul(p[0:64, :], w2[:], a1[:], start=True, stop=True), rd=[wR, a1R], wr=[pR])
                        S.op("dve", lambda e: e.tensor_scalar(out=a2[:], in0=p[0:64, :], scalar1=pv[:, 2:3], scalar2=pv[:, 3:4],
                                                              op0=ALU.add, op1=ALU.mult), rd=[pR, wR], wr=[a2R])
                        sin_big(st2, a2[:], a2R, 512, "b")
                        S.barrier(("act", "dve"))
                    half = 0 if ci * 512 < L else 1
                    for fc in range(4):
                        i = n_ % 2
                        n_ += 1
                        p3, p3R = psum()
                        wc = w3[:, half * 512 + fc * 128: half * 512 + (fc + 1) * 128]
                        S.op("pe", lambda e: e.matmul(p3[:], wc, a2[:], start=True, stop=True), rd=[wR, a2R], wr=[p3R])
                        pw, pwR = psum()
                        S.op("pe", lambda e: e.matmul(pw[:], nd[:, fc * 128:(fc + 1) * 128], tn[b][:], start=True, stop=True),
                             rd=[wR, ztR[b]], wr=[pwR])
                        S.op("act", lambda e: e.activation(out=wsb[i][:], in_=pw[:], func=AF.Exp), rd=[pwR], wr=[wsbR[i]])
                        S.op("dve", lambda e: e.tensor_tensor(out=kc_[i][:], in0=p3[:], in1=wsb[i][:], op=ALU.mult),
                             rd=[p3R, wsbR[i]], wr=[kcR[i]])
                        if ci == 0:
                            p4, p4R = psum()
                            wcb = w3[:, 512 + fc * 128: 512 + (fc + 1) * 128]
                            S.op("pe", lambda e: e.matmul(p4[:, 0:2], wcb, a2[:, 0:2], start=True, stop=True),
                                 rd=[wR, a2R], wr=[p4R])
                            S.op("dve", lambda e: e.tensor_tensor(out=kc_[i][:, 0:1], in0=kc_[i][:, 0:1], in1=p4[:, 0:1],
                                                                  op=ALU.add), rd=[p4R, kcR[i]], wr=[kcR[i]])
                        S.op("dve", lambda e: e.tensor_reduce(out=nrm[:, fc, ci:ci + 1], in_=kc_[i][:], axis=AX.X, op=ALU.add,
                                                              apply_absolute_value=True), rd=[kcR[i]], wr=[nrmR])
                        S.op("act", lambda e: e.activation(out=kb_[i][:], in_=kc_[i][:], func=AF.Copy), rd=[kcR[i]], wr=[kbR[i]])
                        S.dma("pool", KERN[L][fc * 128:(fc + 1) * 128, ci * 512:(ci + 1) * 512], kb_[i][:], rd=[kbR[i]])
                S.op("dve", lambda e: e.tensor_reduce(out=rnorm[L][:], in_=nrm[:], axis=AX.X, op=ALU.add), rd=[nrmR], wr=[rnormR[L]])
                S.op("dve", lambda e: e.reciprocal(out=rnorm[L][:], in_=rnorm[L][:]), rd=[rnormR[L]], wr=[rnormR[L]])
                S.barrier()
            with ExitStack() as st:
                NCG = 32
                tabs = fft_tabs(st, L, NCG, N1)
                kst = [sb(st, "kst%d" % i, [128, 512 // N1, 2, N1], BF16) for i in range(2)]
                kstR = [Res(), Res()]
                for c0 in range(0, 512, NCG):
                    fft_fwd(st, L, KERN[L][c0:c0 + NCG, :], c0, NCG, N1, dst_kf=KF[L], dst_sb=(kst, kstR), tabs=tabs)
                S.barrier()

    def phase3(l):
        sub = (lambda k: only is None or 'p3' in only or k in only)
        if sub('p3f'):
            filters(l)
        TB = 2048
        if sub('p3a'):
            with ExitStack() as st:
                cw = sb(st, "cw", [128, 12, 4], F32)
                hd = sb(st, "hd", [128, 4], F32)
                cwR = Res()
                for ch in range(12):
                    for j in range(3):
                        S.dma("sp", cw[:, ch, j:j + 1], col(W["conv_w"][l][j, ch * 128:(ch + 1) * 128]), wr=[cwR])
                    S.dma("sp", cw[:, ch, 3:4], col(W["conv_b"][l][ch * 128:(ch + 1) * 128]), wr=[cwR])
                for cc in range(4):
                    S.dma("sp", hd[:, cc:cc + 1], col(W["hyena_d"][l][cc * 128:(cc + 1) * 128]), wr=[cwR])
                hin = [[sb(st, "hin%d_%d" % (i, s), [128, TB + 2], F32) for s in range(3)] for i in range(2)]
                hinR = [[Res() for s in range(3)] for i in range(2)]
                cv = [sb(st, "cv%d" % s, [128, TB], F32) for s in range(3)]
                cvR = [Res() for s in range(3)]
                uu = sb(st, "uu", [128, TB], F32)
                uuR = Res()
                ub = [sb(st, "ub%d" % i, [128, TB], BF16) for i in range(2)]
                ubR = [Res(), Res()]
                uxo = [sb(st, "uxo%d" % i, [128, TB], F32) for i in range(2)]
                uxoR = [Res(), Res()]
                x0o = [sb(st, "x0o%d" % i, [128, TB], F32) for i in range(2)]
                x0oR = [Res(), Res()]
                n_ = 0
                for si, L in enumerate(SEQS):
                    s0 = seq_off[si]
                    for tb in range(L // TB):
                        a = tb * TB
                        lo = 1 if a == 0 else 0
                        hi = 1 if a + TB == L else 0
                        for cc in range(4):
                            i = n_ % 2
                            n_ += 1
                            for s in range(3):
                                ch = s * 4 + cc
                                if lo:
                                    S.op("pool", lambda e: e.memset(hin[i][s][:, 0:1], 0.0), wr=[hinR[i][s]])
                                if hi:
                                    S.op("pool", lambda e: e.memset(hin[i][s][:, TB + 1:TB + 2], 0.0), wr=[hinR[i][s]])
                                S.dma("sp", hin[i][s][:, lo:TB + 2 - hi], HY[ch, :, s0 + a - 1 + lo:s0 + a + TB + 1 - hi], wr=[hinR[i][s]])
                                S.op("dve", lambda e: e.tensor_scalar(out=cv[s][:], in0=hin[i][s][:, 1:TB + 1], scalar1=cw[:, ch, 1:2],
                                                                      scalar2=cw[:, ch, 3:4], op0=ALU.mult, op1=ALU.add),
                                     rd=[hinR[i][s], cwR], wr=[cvR[s]])
                                S.op("dve", lambda e: e.scalar_tensor_tensor(out=cv[s][:], in0=hin[i][s][:, 0:TB], scalar=cw[:, ch, 0:1],
                                                                             in1=cv[s][:], op0=ALU.mult, op1=ALU.add),
                                     rd=[hinR[i][s], cwR, cvR[s]], wr=[cvR[s]])
                                S.op("dve", lambda e: e.scalar_tensor_tensor(out=cv[s][:], in0=hin[i][s][:, 2:TB + 2], scalar=cw[:, ch, 2:3],
                                                                             in1=cv[s][:], op0=ALU.mult, op1=ALU.add),
                                     rd=[hinR[i][s], cwR, cvR[s]], wr=[cvR[s]])
                            S.op("pool", lambda e: e.tensor_tensor(out=uu[:], in0=cv[1][:], in1=cv[2][:], op=ALU.mult),
                                 rd=[cvR[1], cvR[2]], wr=[uuR])
                            S.op("act", lambda e: e.activation(out=ub[i][:], in_=uu[:], func=AF.Copy), rd=[uuR], wr=[ubR[i]])
                            S.op("dve", lambda e: e.scalar_tensor_tensor(out=uxo[i][:], in0=uu[:], scalar=hd[:, cc:cc + 1], in1=cv[0][:],
                                                                         op0=ALU.mult, op1=ALU.mult), rd=[uuR, cvR[0], cwR], wr=[uxoR[i]])
                            S.op("act", lambda e: e.activation(out=x0o[i][:], in_=cv[0][:], func=AF.Copy, scale=rnorm[L][:, cc:cc + 1]),
                                 rd=[cvR[0], rnormR[L]], wr=[x0oR[i]])
                            S.dma("pool", U[cc * 128:(cc + 1) * 128, s0 + a:s0 + a + TB], ub[i][:], rd=[ubR[i]])
                            S.dma("pool", UX[cc, :, s0 + a:s0 + a + TB], uxo[i][:], rd=[uxoR[i]])
                            S.dma("pool", X0[cc, :, s0 + a:s0 + a + TB], x0o[i][:], rd=[x0oR[i]])
                S.barrier()
        if sub('p3b'):
            for L in LT:
                N1 = L // 64
                H = N1 // 2
                NCG = 32
                with ExitStack() as st:
                    tabs = fft_tabs(st, L, NCG, H)
                    itr = sb(st, "itwr", [N1, 256], F32)
                    iti = sb(st, "itwi", [N1, 256], F32)
                    g1 = sb(st, "g1", [N1, 2 * H], BF16)
                    itR = Res()
                    S.dma("sp", itr[:], CT["itwr_%d" % N1][:, :], wr=[itR])
                    S.dma("sp", iti[:], CT["itwi_%d" % N1][:, :], wr=[itR])
                    S.dma("sp", g1[:], CT["g1_%d" % N1][:, :], wr=[itR])
                    if N1 == 32:
                        itr4 = sb(st, "itr4", [128, 256], F32)
                        iti4 = sb(st, "iti4", [128, 256], F32)
                        g1p = sb(st, "g1p", [128, 4, 2 * H], BF16)
                        bre4 = sb(st, "bre4", [128, NCG // 4, 128], BF16)
                        bim4 = sb(st, "bim4", [128, NCG // 4, 128], BF16)
                        S.dma("sp", itr4[:], CT["itwr4_32"][:, :], wr=[itR])
                        S.dma("sp", iti4[:], CT["itwi4_32"][:, :], wr=[itR])
                        S.dma("sp", g1p[:], CT["g1p_32"][:, :, :], wr=[itR])
                    kf = [sb(st, "kf%d" % i, [128, NCG, 2, N1], BF16) for i in range(2)]
                    kfR = [Res(), Res()]
                    yre = sb(st, "yre", [128, NCG, N1], BF16)
                    yim = sb(st, "yim", [128, NCG, N1], BF16)
                    yR = Res()
                    mt = [sb(st, "mt%d" % i, [128, 4, 512], F32) for i in range(2)]
                    mtR = [Res(), Res()]
                    bre = sb(st, "bre", [N1, NCG, 128], BF16)
                    bim = sb(st, "bim", [N1, NCG, 128], BF16)
                    bR = Res()
                    yo = [sb(st, "yo%d" % i, [H, NCG, 128], F32) for i in range(2)]
                    yoR = [Res(), Res()]
                    n_ = 0
                    for si, Ls in enumerate(SEQS):
                        if Ls != L:
                            continue
                        s0 = seq_off[si]
                        for c0 in range(0, 512, NCG):
                            i = n_ % 2
                            n_ += 1
                            S.dma("sp", kf[i][:], KF[L][:, c0:c0 + NCG, :, :], wr=[kfR[i]])
                            fft_fwd(st, L, U[c0:c0 + NCG, s0:s0 + L], c0, NCG, H, kfs=(kf[i], kfR[i]),
                                    dst_sb=(yre, yim, yR, mt, mtR), tabs=tabs)
                            pp, ppR = tabs[5], tabs[6]
                            if N1 == 32:
                                for g in range(NCG // 8):
                                    p, pR = psum()
                                    pvw = p[:].rearrange("p (c k) -> p c k", c=2)

                                    def s1b(e):
                                        for sc in range(2):
                                            c_ = (g * 2 + sc) * 4
                                            e.matmul(pvw[:, sc, :], yre[:, c_:c_ + 4, :].rearrange("p c k -> p (c k)"), gg1, start=True, stop=False)
                                            ins = e.matmul(pvw[:, sc, :], yim[:, c_:c_ + 4, :].rearrange("p c k -> p (c k)"), gg2, start=False, stop=True)
                                        return ins
                                    S.op("pe", s1b, rd=[yR, cbR], wr=[pR])
                                    j = g % 2
                                    p1 = pp[j][:, 0, :].rearrange("p (c k) -> p c k", c=2)
                                    p2 = pp[j][:, 1, :].rearrange("p (c k) -> p c k", c=2)
                                    S.op("dve", lambda e: e.tensor_tensor(out=p1, in0=pvw, in1=itr4[:, None, :].to_broadcast([128, 2, 256]),
                                                                          op=ALU.mult), rd=[pR, itR], wr=[ppR[j]])
                                    S.op("dve", lambda e: e.tensor_tensor(out=p2, in0=pvw, in1=iti4[:, None, :].to_broadcast([128, 2, 256]),
                                                                          op=ALU.mult), rd=[pR, itR], wr=[ppR[j]])
                                    cs = slice(g * 2, g * 2 + 2)
                                    S.op("pool", lambda e: e.tensor_tensor(out=bre4[:, cs, :], in0=p1[:, :, 0:128], in1=p2[:, :, 128:256],
                                                                           op=ALU.subtract), rd=[ppR[j]], wr=[bR])
                                    S.op("pool", lambda e: e.tensor_tensor(out=bim4[:, cs, :], in0=p2[:, :, 0:128], in1=p1[:, :, 128:256],
                                                                           op=ALU.add), rd=[ppR[j]], wr=[bR])
                                for g in range(NCG // 4):
                                    cs = slice(g * 4, g * 4 + 4)
                                    p, pR = psum()

                                    def s2b(e):
                                        for c4 in range(4):
                                            e.matmul(p[0:H, c4 * 128:(c4 + 1) * 128], g1p[:, c4, 0:H], bre4[:, g, :], start=True, stop=False)
                                            ins = e.matmul(p[0:H, c4 * 128:(c4 + 1) * 128], g1p[:, c4, H:2 * H], bim4[:, g, :], start=False, stop=True)
                                        return ins
                                    S.op("pe", s2b, rd=[bR, itR], wr=[pR])
                                    dst = yo[i][:, cs, :].rearrange("p c k -> p (c k)")
                                    if g % 2 == 0:
                                        S.op("act", lambda e: e.activation(out=dst, in_=p[0:H, :], func=AF.Copy), rd=[pR], wr=[yoR[i]])
                                    else:
                                        S.op("dve", lambda e: e.tensor_copy(out=dst, in_=p[0:H, :]), rd=[pR], wr=[yoR[i]])
                            else:
                                for g in range(NCG // 2):
                                    p, pR = psum()
                                    pvw = p[0:N1, :].rearrange("p (c k) -> p c k", c=2)

                                    def s1i(e):
                                        for c in range(2):
                                            cc_ = g * 2 + c
                                            e.matmul(pvw[:, c, :], yre[:, cc_, :], gg1, start=True, stop=False)
                                            ins = e.matmul(pvw[:, c, :], yim[:, cc_, :], gg2, start=False, stop=True)
                                        return ins
                                    S.op("pe", s1i, rd=[yR, cbR], wr=[pR])
                                    j = g % 2
                                    p1 = pp[j][0:N1, 0, :].rearrange("p (c k) -> p c k", c=2)
                                    p2 = pp[j][0:N1, 1, :].rearrange("p (c k) -> p c k", c=2)
                                    S.op("dve", lambda e: e.tensor_tensor(out=p1, in0=pvw, in1=itr[:, None, :].to_broadcast([N1, 2, 256]),
                                                                          op=ALU.mult), rd=[pR, itR], wr=[ppR[j]])
                                    S.op("dve", lambda e: e.tensor_tensor(out=p2, in0=pvw, in1=iti[:, None, :].to_broadcast([N1, 2, 256]),
                                                                          op=ALU.mult), rd=[pR, itR], wr=[ppR[j]])
                                    cs = slice(g * 2, g * 2 + 2)
                                    S.op("pool", lambda e: e.tensor_tensor(out=bre[:, cs, :], in0=p1[:, :, 0:128], in1=p2[:, :, 128:256],
                                                                           op=ALU.subtract), rd=[ppR[j]], wr=[bR])
                                    S.op("pool", lambda e: e.tensor_tensor(out=bim[:, cs, :], in0=p2[:, :, 0:128], in1=p1[:, :, 128:256],
                                                                           op=ALU.add), rd=[ppR[j]], wr=[bR])
                                for g in range(NCG // 4):
                                    cs = slice(g * 4, g * 4 + 4)
                                    p, pR = psum()
                                    br_ = bre[:, cs, :].rearrange("p c k -> p (c k)")
                                    bi_ = bim[:, cs, :].rearrange("p c k -> p (c k)")
                                    S.op("pe", lambda e: mm_acc(e, p[0:H, :], [(g1[:, 0:H], br_), (g1[:, H:2 * H], bi_)]),
                                         rd=[bR, itR], wr=[pR])
                                    dst = yo[i][:, cs, :].rearrange("p c k -> p (c k)")
                                    if g % 2 == 0:
                                        S.op("act", lambda e: e.activation(out=dst, in_=p[0:H, :], func=AF.Copy), rd=[pR], wr=[yoR[i]])
                                    else:
                                        S.op("dve", lambda e: e.tensor_copy(out=dst, in_=p[0:H, :]), rd=[pR], wr=[yoR[i]])
                            S.dma("pool", YC[c0:c0 + NCG, s0:s0 + L].rearrange("c (n1 n2) -> n1 c n2", n2=128), yo[i][:], rd=[yoR[i]])
                    S.barrier()
        if sub('p3c'):
            with ExitStack() as st:
                yc = [sb(st, "ycs%d" % i, [128, TB], F32) for i in range(2)]
                ux = [sb(st, "uxs%d" % i, [128, TB], F32) for i in range(2)]
                x0 = [sb(st, "x0s%d" % i, [128, TB], F32) for i in range(2)]
                inR = [Res(), Res()]
                yb = [sb(st, "ybs%d" % i, [128, TB], BF16) for i in range(2)]
                ybR = [Res(), Res()]
                n_ = 0
                for t0 in range(0, NT, TB):
                    for cc in range(4):
                        i = n_ % 2
                        n_ += 1
                        S.dma("sp", yc[i][:], YC[cc * 128:(cc + 1) * 128, t0:t0 + TB], wr=[inR[i]])
                        S.dma("sp", ux[i][:], UX[cc, :, t0:t0 + TB], wr=[inR[i]])
                        S.dma("sp", x0[i][:], X0[cc, :, t0:t0 + TB], wr=[inR[i]])
                        S.op("dve", lambda e: e.tensor_tensor(out=yc[i][:], in0=yc[i][:], in1=x0[i][:], op=ALU.mult), rd=[inR[i]], wr=[inR[i]])
                        S.op("pool", lambda e: e.tensor_tensor(out=yb[i][:], in0=yc[i][:], in1=ux[i][:], op=ALU.add), rd=[inR[i]], wr=[ybR[i]])
                        S.dma("pool", MIX[4 + cc, :, t0:t0 + TB], yb[i][:], rd=[ybR[i]])
                S.barrier()

    def tok_major_proj(actT, actTR, nk, wt, wtR, j):
        res = []
        for h in range(2):
            p, pR = psum()
            S.op("pe", lambda e: mm_acc(e, p[:], [(actT[:, k, j * 128:(j + 1) * 128], wt[:, k, h * 512:(h + 1) * 512])
                                                  for k in range(nk)]), rd=[actTR, wtR], wr=[pR])
            res.append((p, pR))
        return [r[0] for r in res], [r[1] for r in res]

    def phase4a(l, SRC):
        with ExitStack() as st:
            wo, woR = load_w(st, "wout", W["w_out"][l], D, D)
            gam, gamR = load_gamma(st, "g4a", W["ln_mix_post"][l])
            xt = [sb(st, "xt%d" % i, [128, 4, D], F32) for i in range(2)]
            xtR = [Res() for _ in range(2)]
            mx = [sb(st, "mx%d" % i, [128, 8, 512], BF16) for i in range(2)]
            mxR = [Res() for _ in range(2)]
            ss = sb(st, "ss", [128, 8], F32)
            ssR = Res()
            junk = sb(st, "junk", [128, D], F32)
            junkR = Res()
            tmp = sb(st, "tmp", [128, D], F32)
            tmpR = Res()

            def ld(ti):
                b = ti % 2
                t0 = tiles[ti][0]
                S.dma("sp", xt[b][:], xview(SRC, t0), wr=[xtR[b]])
                S.dma("sp", mx[b][:], MIX[:, :, t0:t0 + 512].rearrange("c p t -> p c t"), wr=[mxR[b]])
            ld(0)
            for ti, (t0, si, p0) in enumerate(tiles):
                b = ti % 2
                if ti + 1 < len(tiles):
                    ld(ti + 1)
                for j in range(4):
                    ps2, ps2R = tok_major_proj(mx[b], mxR[b], 8, wo, woR, j)
                    postnorm_residual(ps2, ps2R, gam, gamR, xt[b][:, j, :], xtR[b], ss, ssR, junk, junkR, tmp, tmpR)
                S.dma("pool", xview(XR, t0), xt[b][:], rd=[xtR[b]])
            S.barrier()

    def phase4b(l):
        with ExitStack() as st:
            wq, wqR = load_w(st, "wq", W["wq_x"][l], D, D)
            wk, wkR = load_w(st, "wk", W["wk_x"][l], D, D)
            wv, wvR = load_w(st, "wv", W["wv_x"][l], D, D)
            wo, woR = load_w(st, "wo", W["wo_x"][l], D, D)
            gpre, gpreR = load_gamma(st, "gxpre", W["ln_x_pre"][l])
            gpost, gpostR = load_gamma(st, "gxpost", W["ln_x_post"][l])
            gmem, gmemR = load_gamma(st, "gmem", W["ln_mem"][l])
            xt = [sb(st, "xt%d" % i, [128, 4, D], F32) for i in range(2)]
            xtR = [Res() for _ in range(2)]
            xn = sb(st, "xn", [128, 4, D], BF16)
            xnR = [Res() for _ in range(4)]
            xnT = sb(st, "xnT", [128, 8, 512], BF16)
            xnTR = Res()
            qxT = sb(st, "qxT", [128, 8, 512], BF16)
            qxTR = [Res() for _ in range(8)]
            oT = sb(st, "oT", [128, 8, 512], BF16)
            oTR = Res()
            kxT = sb(st, "kxT", [128, 8, NMEM], BF16)
            kxTR = Res()
            vx = sb(st, "vx", [128, 2, D], BF16)
            vxR = Res()
            mt_ = sb(st, "memt", [128, 2, D], F32)
            mtR_ = Res()
            ss = sb(st, "ss", [128, 8], F32)
            ssR = Res()
            junk = sb(st, "junk", [128, D], F32)
            junkR = Res()
            tmp = sb(st, "tmp", [128, D], F32)
            tmpR = Res()
            esb = [sb(st, "esb%d" % i, [128, 512], BF16) for i in range(4)]
            esbR = [Res() for _ in range(4)]
            rz = sb(st, "rz", [128, 512], F32)
            rzR = Res()
            en = 0
            cur = -1
            S.dma("sp", xt[0][:], xview(XR, tiles[0][0]), wr=[xtR[0]])
            for ti, (t0, si, p0) in enumerate(tiles):
                b = ti % 2
                if ti + 1 < len(tiles):
                    S.dma("sp", xt[1 - b][:], xview(XR, tiles[ti + 1][0]), wr=[xtR[1 - b]])
                if si != cur:
                    cur = si
                    S.dma("sp", mt_[:], MEM[si * NMEM:(si + 1) * NMEM, :].rearrange("(j p) d -> p j d", p=128), wr=[mtR_])
                    norm_transpose(mt_, mtR_, 2, gmem, gmemR, xn, xnR, xnT, xnTR, ss, ssR, junk, junkR)
                    for fc in range(8):
                        p, pR = psum()
                        S.op("pe", lambda e: mm_acc(e, p[:, 0:NMEM], [(wk[:, k, fc * 128:(fc + 1) * 128], xnT[:, k, 0:NMEM])
                                                                      for k in range(8)]), rd=[wkR, xnTR], wr=[pR])
                        S.op("act", lambda e: e.activation(out=kxT[:, fc, :], in_=p[:, 0:NMEM], func=AF.Copy), rd=[pR], wr=[kxTR])
                    for mc in range(2):
                        for h in range(2):
                            p, pR = psum()
                            S.op("pe", lambda e: mm_acc(e, p[:], [(xnT[:, k, mc * 128:(mc + 1) * 128], wv[:, k, h * 512:(h + 1) * 512])
                                                                  for k in range(8)]), rd=[wvR, xnTR], wr=[pR])
                            S.op("dve", lambda e: e.tensor_copy(out=vx[:, mc, h * 512:(h + 1) * 512], in_=p[:]), rd=[pR], wr=[vxR])
                norm_transpose(xt[b], xtR[b], 4, gpre, gpreR, xn, xnR, xnT, xnTR, ss, ssR, junk, junkR)
                for fc in range(8):
                    p, pR = psum()
                    S.op("pe", lambda e: mm_acc(e, p[:], [(wq[:, k, fc * 128:(fc + 1) * 128], xnT[:, k, :]) for k in range(8)]),
                         rd=[wqR, xnTR], wr=[pR])
                    if fc % 2 == 0:
                        S.op("act", lambda e: e.activation(out=qxT[:, fc, :], in_=p[:], func=AF.Copy), rd=[pR], wr=[qxTR[fc]])
                    else:
                        S.op("dve", lambda e: e.tensor_copy(out=qxT[:, fc, :], in_=p[:]), rd=[pR], wr=[qxTR[fc]])
                for hx in range(4):
                    ee = []
                    for mc in range(2):
                        p, pR = psum()
                        S.op("pe", lambda e: mm_acc(e, p[:], [(kxT[:, 2 * hx + q, mc * 128:(mc + 1) * 128], qxT[:, 2 * hx + q, :])
                                                              for q in range(2)]), rd=[kxTR, qxTR[2 * hx], qxTR[2 * hx + 1]], wr=[pR])
                        ei = en % 4
                        en += 1
                        S.op("act", lambda e: e.activation(out=esb[ei][:], in_=p[:], func=AF.Exp, scale=1.0 / 16), rd=[pR], wr=[esbR[ei]])
                        ee.append(ei)
                    pz, pzR = psum()
                    S.op("pe", lambda e: mm_acc(e, pz[:], [(ones, esb[ee[0]][:]), (ones, esb[ee[1]][:])]),
                         rd=[cbR, esbR[ee[0]], esbR[ee[1]]], wr=[pzR])
                    S.op("act", lambda e: e.activation(out=rz[:], in_=pz[:], func=AF.Ln), rd=[pzR], wr=[rzR])
                    S.op("act", lambda e: e.activation(out=rz[:], in_=rz[:], func=AF.Exp, scale=-1.0), rd=[rzR], wr=[rzR])
                    for q in range(2):
                        fc = 2 * hx + q
                        p, pR = psum()
                        S.op("pe", lambda e: mm_acc(e, p[:], [(vx[:, mc, fc * 128:(fc + 1) * 128], esb[ee[mc]][:]) for mc in range(2)]),
                             rd=[vxR, esbR[ee[0]], esbR[ee[1]]], wr=[pR])
                        S.op("dve", lambda e: e.tensor_tensor(out=oT[:, fc, :], in0=p[:], in1=rz[:], op=ALU.mult), rd=[pR, rzR], wr=[oTR])
                for j in range(4):
                    ps2, ps2R = tok_major_proj(oT, oTR, 8, wo, woR, j)
                    postnorm_residual(ps2, ps2R, gpost, gpostR, xt[b][:, j, :], xtR[b], ss, ssR, junk, junkR, tmp, tmpR)
                S.dma("pool", xview(XR, t0), xt[b][:], rd=[xtR[b]])
            S.barrier()

    def phase5(l):
        with ExitStack() as st:
            wg, wgR = load_w(st, "wg", W["w_gate"][l], D, DFF)
            wu, wuR = load_w(st, "wu", W["w_up"][l], D, DFF)
            gam, gamR = load_gamma(st, "g5", W["ln_ffn_pre"][l])
            xt = [sb(st, "xt%d" % i, [128, 4, D], F32) for i in range(2)]
            xtR = [Res() for _ in range(2)]
            xn2 = [sb(st, "xn%d" % i, [128, 4, D], BF16) for i in range(2)]
            xnR2 = [[Res() for _ in range(4)] for i in range(2)]
            xnT2 = [sb(st, "xnT%d" % i, [128, 8, 512], BF16) for i in range(2)]
            xnTR2 = [Res(), Res()]
            ss = sb(st, "ss", [128, 8], F32)
            ssR = Res()
            junk = sb(st, "junk", [128, D], F32)
            junkR = Res()
            sg = [sb(st, "sg%d" % i, [128, 512], F32) for i in range(2)]
            sgR = [Res(), Res()]
            hb = [sb(st, "hb%d" % i, [128, 512], BF16) for i in range(3)]
            hbR = [Res() for _ in range(3)]
            n_ = 0
            S.dma("sp", xt[0][:], xview(XR, tiles[0][0]), wr=[xtR[0]])
            norm_transpose(xt[0], xtR[0], 4, gam, gamR, xn2[0], xnR2[0], xnT2[0], xnTR2[0], ss, ssR, junk, junkR)
            for ti, (t0, si, p0) in enumerate(tiles):
                b = ti % 2
                xnT, xnTR = xnT2[b], xnTR2[b]
                if ti + 1 < len(tiles):
                    S.dma("sp", xt[1 - b][:], xview(XR, tiles[ti + 1][0]), wr=[xtR[1 - b]])
                for f in range(NFF):
                    if f == NFF // 2 and ti + 1 < len(tiles):
                        norm_transpose(xt[1 - b], xtR[1 - b], 4, gam, gamR, xn2[1 - b], xnR2[1 - b], xnT2[1 - b], xnTR2[1 - b],
                                       ss, ssR, junk, junkR)
                    pg, pgR = psum()
                    pu, puR = psum()
                    S.op("pe", lambda e: mm_acc(e, pg[:], [(wg[:, k, f * 128:(f + 1) * 128], xnT[:, k, :]) for k in range(8)]),
                         rd=[wgR, xnTR], wr=[pgR])
                    S.op("pe", lambda e: mm_acc(e, pu[:], [(wu[:, k, f * 128:(f + 1) * 128], xnT[:, k, :]) for k in range(8)]),
                         rd=[wuR, xnTR], wr=[puR])
                    i = n_ % 2
                    o = n_ % 3
                    n_ += 1
                    S.op("act", lambda e: e.activation(out=sg[i][:], in_=pg[:], func=AF.Silu), rd=[pgR], wr=[sgR[i]])
                    S.op("dve", lambda e: e.tensor_tensor(out=hb[o][:], in0=pu[:], in1=sg[i][:], op=ALU.mult), rd=[puR, sgR[i]], wr=[hbR[o]])
                    S.dma("pool", HT[f, :, t0:t0 + 512], hb[o][:], rd=[hbR[o]])
            S.barrier()

    def phase6(l, DST):
        with ExitStack() as st:
            wd, wdR = load_w(st, "wd", W["w_down"][l], DFF, D)
            gam, gamR = load_gamma(st, "g6", W["ln_ffn_post"][l])
            xt = [sb(st, "xt%d" % i, [128, 4, D], F32) for i in range(2)]
            xtR = [Res() for _ in range(2)]
            hT = [sb(st, "hT%d" % i, [128, NFF, 512], BF16) for i in range(2)]
            hTR = [Res() for _ in range(2)]
            ss = sb(st, "ss", [128, 8], F32)
            ssR = Res()
            junk = sb(st, "junk", [128, D], F32)
            junkR = Res()
            tmp = sb(st, "tmp", [128, D], F32)
            tmpR = Res()

            def ld(ti):
                b = ti % 2
                t0 = tiles[ti][0]
                S.dma("sp", xt[b][:], xview(XR, t0), wr=[xtR[b]])
                S.dma("sp", hT[b][:], HT[:, :, t0:t0 + 512].rearrange("c p t -> p c t"), wr=[hTR[b]])
            ld(0)
            for ti, (t0, si, p0) in enumerate(tiles):
                b = ti % 2
                if ti + 1 < len(tiles):
                    ld(ti + 1)
                for j in range(4):
                    ps2, ps2R = tok_major_proj(hT[b], hTR[b], NFF, wd, wdR, j)
                    postnorm_residual(ps2, ps2R, gam, gamR, xt[b][:, j, :], xtR[b], ss, ssR, junk, junkR, tmp, tmpR)
                S.dma("pool", xview(DST, t0), xt[b][:], rd=[xtR[b]])
            S.barrier()

    S.barrier()
    for l in range(depth):
        src = X if l == 0 else XR
        for nm, fn in (("p1", lambda: phase1(l, src)), ("p2", lambda: phase2(l)), ("p3", lambda: phase3(l)),
                       ("p4a", lambda: phase4a(l, src)), ("p4b", lambda: phase4b(l)), ("p5", lambda: phase5(l)),
                       ("p6", lambda: phase6(l, Y if l == depth - 1 else XR))):
            if only is None or nm in only or (nm == 'p3' and any(k.startswith('p3') for k in only)):
                fn()
    S.barrier()
    es.close()
    return nc, consts


_CACHE = {}


def _core_inputs(core, x_prompt, x_sample, mem_prompt, mem_sample, weights, consts):
    xs = x_sample[core].reshape(-1, D)
    xp = x_prompt[4 * core:4 * core + 4].reshape(-1, D)
    m = {"x": np.ascontiguousarray(np.concatenate([xs, xp], 0)),
         "mem": np.ascontiguousarray(np.concatenate([mem_sample[core], mem_prompt[4 * core:4 * core + 4].reshape(-1, D)], 0))}
    for n, _ in WSPEC:
        m[n] = weights[n]
    for k, v in consts.items():
        m["c_" + k] = v
    return m


def kernel(**inputs):
    inputs = {k: np.asarray(v) for k, v in inputs.items()}
    SEQS = [8192, 2048, 2048, 2048, 2048]
    if "nc" not in _CACHE:
        _CACHE["nc"] = build(SEQS, DEPTH)
    nc, consts = _CACHE["nc"]
    weights = {n: np.ascontiguousarray(inputs[n], dtype=np.float32) for n, _ in WSPEC}
    in_maps = [_core_inputs(c, inputs["x_prompt"], inputs["x_sample"], inputs["mem_prompt"], inputs["mem_sample"],
                            weights, consts) for c in range(8)]
    res = run_bass_kernel_spmd(nc, in_maps, core_ids=list(range(8)))
    y_prompt = np.empty((32, 2048, D), np.float32)
    y_sample = np.empty((8, 8192, D), np.float32)
    for c in range(8):
        y = res.results[c]["y"]
        y_sample[c] = y[:8192]
        y_prompt[4 * c:4 * c + 4] = y[8192:].reshape(4, 2048, D)
    return (y_prompt, y_sample)
```
